# Optimizing a Trainium2 kernel written in Bass

```python
import math
import jax, jax.numpy as jnp
from jax import lax
import numpy as np

D_MODEL = 1024
BATCH = 8
SEQ = 2048
DEPTH = 2
DEC_BATCH = 128
DEC_SEQ = 1
PAST_LEN = 16384
PAGE_SIZE = 128

PLE_DIM = 256
D_FF = 4 * D_MODEL
CONV_W = 4
EPS = 1e-6
RG_WIDTH = D_MODEL // 2
RG_HEADS = 8
RG_HEAD_DIM = RG_WIDTH // RG_HEADS
RG_C = 8.0
SSD_WIDTH = D_MODEL // 2
SSD_HEAD_DIM = 64
SSD_HEADS = SSD_WIDTH // SSD_HEAD_DIM
SSD_STATE = 128
SSD_CHUNK = 128
SSD_CONV_DIM = SSD_WIDTH + 2 * SSD_STATE
HG_WIDTH = D_MODEL // 2
HG_HEADS = 8
HG_HEAD_DIM = HG_WIDTH // HG_HEADS
HG_CHUNK = 64
D_MIX = RG_WIDTH + SSD_WIDTH + HG_WIDTH
IN_SPLITS = (RG_WIDTH, RG_WIDTH, SSD_WIDTH, SSD_CONV_DIM, SSD_HEADS, HG_WIDTH, HG_WIDTH, HG_WIDTH, HG_WIDTH)
D_IN_PROJ = RG_WIDTH * 2 + SSD_WIDTH + SSD_CONV_DIM + SSD_HEADS + HG_WIDTH * 4

kernel_name = 'hymba_style_rglru_ssd_hgrn2_decoder'


def rmsnorm(x, g):
    xf = x.astype(jnp.float32)
    y = xf * lax.rsqrt(jnp.mean(xf * xf, axis=-1, keepdims=True) + EPS)
    return (y * g.astype(jnp.float32)).astype(x.dtype)


def causal_conv(x, buf, w, b):
    xe = jnp.concatenate([buf.astype(x.dtype), x], axis=1)
    L = x.shape[1]
    y = sum(xe[:, k:k + L] * w[k] for k in range(CONV_W)) + b
    return y, xe[:, -(CONV_W - 1):]


def to_chunks(t, q):
    bsz, L = t.shape[:2]
    lp = -(-L // q) * q
    t = jnp.pad(t, [(0, 0), (0, lp - L)] + [(0, 0)] * (t.ndim - 2))
    return jnp.moveaxis(t.reshape(bsz, lp // q, q, *t.shape[2:]), 1, 0)


def from_chunks(t, L):
    nc, bsz, q = t.shape[:3]
    return jnp.moveaxis(t, 0, 1).reshape(bsz, nc * q, *t.shape[3:])[:, :L]


def rg_lru(x, wa, ba, wx, bx, lam, h0, pos0):
    bsz, L, _ = x.shape
    xh = x.reshape(bsz, L, RG_HEADS, RG_HEAD_DIM)
    r = jax.nn.sigmoid(jnp.einsum('blhi,hij->blhj', xh, wa).reshape(bsz, L, RG_WIDTH) + ba)
    i = jax.nn.sigmoid(jnp.einsum('blhi,hij->blhj', xh, wx).reshape(bsz, L, RG_WIDTH) + bx)
    log_a = (-RG_C * r * jax.nn.softplus(-lam)).astype(jnp.float32)
    a = jnp.exp(log_a)
    mult = jnp.sqrt(-jnp.expm1(2.0 * log_a))
    pos = pos0 + jnp.arange(L)
    mult = jnp.where((pos == 0)[None, :, None], 1.0, mult)
    b = mult * (i * x).astype(jnp.float32)
    b = b.at[:, 0].add(a[:, 0] * h0.astype(jnp.float32))

    def combine(c1, c2):
        a1, b1 = c1
        a2, b2 = c2
        return a1 * a2, a2 * b1 + b2

    _, h = lax.associative_scan(combine, (a, b), axis=1)
    return h.astype(x.dtype), h[:, -1]


def ssd_chunked(x, dt, a_neg, bm, cm, s0):
    L = x.shape[1]
    q = min(SSD_CHUNK, L)
    causal = jnp.tril(jnp.ones((q, q), bool))
    xs = (to_chunks(x, q), to_chunks(dt, q), to_chunks(bm, q), to_chunks(cm, q))

    def step(s, inp):
        xc, dtc, bc, cc = inp
        cum = jnp.cumsum(dtc * a_neg, axis=1)
        seg = cum[:, :, None, :] - cum[:, None, :, :]
        lmat = jnp.exp(jnp.where(causal[None, :, :, None], seg, -jnp.inf))
        cb = jnp.einsum('btn,bsn->bts', cc, bc)
        y = jnp.einsum('bts,btsh,bsh,bshp->bthp', cb, lmat, dtc, xc)
        y = y + jnp.einsum('btn,bhpn,bth->bthp', cc, s, jnp.exp(cum))
        w_in = jnp.exp(cum[:, -1:, :] - cum) * dtc
        s = jnp.exp(cum[:, -1])[:, :, None, None] * s + jnp.einsum('bsn,bsh,bshp->bhpn', bc, w_in, xc)
        return s, y

    s, ys = lax.scan(step, s0.astype(jnp.float32), xs)
    return from_chunks(ys, L), s


def hgrn_chunked(qv, kv, logf, v, s0):
    L = qv.shape[1]
    q = min(HG_CHUNK, L)
    causal = jnp.tril(jnp.ones((q, q), bool))
    xs = (to_chunks(qv, q), to_chunks(kv, q), to_chunks(logf, q), to_chunks(v, q))

    def step(s, inp):
        qc, kc, gc, vc = inp
        b = jnp.cumsum(gc, axis=1)
        seg = b[:, :, None] - b[:, None]
        dec = jnp.exp(jnp.where(causal[None, :, :, None, None], seg, -jnp.inf))
        att = jnp.einsum('bthd,bshd,btshd->bhts', qc, kc, dec)
        o = jnp.einsum('bhts,bshe->bthe', att, vc)
        o = o + jnp.einsum('bthd,bhde->bthe', qc * jnp.exp(b), s)
        s = jnp.exp(b[:, -1])[..., None] * s + jnp.einsum('bshd,bshe->bhde', kc * jnp.exp(b[:, -1:] - b), vc)
        return s, o

    s, os_ = lax.scan(step, s0.astype(jnp.float32), xs)
    return from_chunks(os_, L), s


def layer_forward(h, p_i, rg_conv, rg_h, ssd_conv, ssd_s, hg_s, lw, lb, pos0):
    f32 = jnp.float32
    bsz, L, _ = h.shape
    u = rmsnorm(h, lw['norm_mix'])
    proj = u @ lw['w_in']
    cuts = [int(c) for c in np.cumsum(IN_SPLITS)[:-1]]
    a_x, a_g, b_z, b_xbc, b_dt, c_q, c_f, c_i, c_g = jnp.split(proj, cuts, axis=-1)

    xa, new_rg_conv = causal_conv(a_x, rg_conv, lw['conv_a_w'], lw['conv_a_b'])
    ha, new_rg_h = rg_lru(xa, lw['rg_wa'], lw['rg_ba'], lw['rg_wx'], lw['rg_bx'], lw['rg_lambda'], rg_h, pos0)
    y_a = ha * jax.nn.gelu(a_g)

    xbc, new_ssd_conv = causal_conv(b_xbc, ssd_conv, lw['conv_b_w'], lw['conv_b_b'])
    xbc = jax.nn.silu(xbc).astype(f32)
    xs, bm, cm = jnp.split(xbc, [SSD_WIDTH, SSD_WIDTH + SSD_STATE], axis=-1)
    xs = xs.reshape(bsz, L, SSD_HEADS, SSD_HEAD_DIM)
    dt = jax.nn.softplus(b_dt.astype(f32) + lw['ssd_dt_bias'].astype(f32))
    a_neg = -jnp.exp(lw['ssd_a_log'].astype(f32))
    yb, new_ssd_s = ssd_chunked(xs, dt, a_neg, bm, cm, ssd_s)
    yb = (yb + lw['ssd_d'].astype(f32)[:, None] * xs).reshape(bsz, L, SSD_WIDTH).astype(h.dtype)
    y_b = rmsnorm(yb * jax.nn.silu(b_z), lw['ssd_norm'])

    qv = jax.nn.silu(c_q.astype(f32)).reshape(bsz, L, HG_HEADS, HG_HEAD_DIM)
    lbh = lb.reshape(HG_HEADS, HG_HEAD_DIM)
    g = lbh + (1.0 - lbh) * jax.nn.sigmoid(c_f.astype(f32).reshape(bsz, L, HG_HEADS, HG_HEAD_DIM))
    kv = 1.0 - g
    logf = jnp.log(g)
    v = c_i.astype(f32).reshape(bsz, L, HG_HEADS, HG_HEAD_DIM)
    o, new_hg_s = hgrn_chunked(qv, kv, logf, v, hg_s)
    o = rmsnorm(o.astype(h.dtype), lw['hg_norm']) * jax.nn.silu(c_g.reshape(bsz, L, HG_HEADS, HG_HEAD_DIM))
    y_c = o.reshape(bsz, L, HG_WIDTH)

    h = h + jnp.concatenate([y_a, y_b, y_c], axis=-1) @ lw['w_out']
    z = jnp.square(jax.nn.relu(rmsnorm(h, lw['norm_ffn']) @ lw['w_up']))
    h = h + z @ lw['w_down']
    gate = jax.nn.sigmoid(rmsnorm(h, lw['norm_ple']) @ lw['w_ple_gate'])
    h = h + gate * (p_i @ lw['w_ple_proj'])
    return h, (new_rg_conv, new_rg_h, new_ssd_conv, new_ssd_s, new_hg_s)


def run_trunk(x, p, rg_conv, rg_h, ssd_conv, ssd_s, hg_s, W, norm_final, lbs, pos0):
    h = x
    outs = ([], [], [], [], [])
    for i in range(DEPTH):
        lw = {name: arr[i] for name, arr in W.items()}
        h, st = layer_forward(h, p[i], rg_conv[i], rg_h[i], ssd_conv[i], ssd_s[i], hg_s[i], lw, lbs[i], pos0)
        for lst, s in zip(outs, st):
            lst.append(s)
    y = rmsnorm(h, norm_final)
    return y, [jnp.stack(lst) for lst in outs]


def setup_inputs(seed: int = 0) -> dict:
    key = jax.random.key(seed)
    ks = iter(jax.random.split(key, 48))
    f32 = jnp.float32

    def nrm(shape, s):
        return jax.random.normal(next(ks), shape, f32) * s

    x_prompt = nrm((BATCH, SEQ, D_MODEL), 1.0)
    x_sample = nrm((DEC_BATCH, DEC_SEQ, D_MODEL), 1.0)
    state_rg_conv = nrm((DEPTH, DEC_BATCH, CONV_W - 1, RG_WIDTH), 1.0)
    state_rg_h = nrm((DEPTH, DEC_BATCH, RG_WIDTH), 0.5)
    state_ssd_conv = nrm((DEPTH, DEC_BATCH, CONV_W - 1, SSD_CONV_DIM), 1.0)
    state_ssd = nrm((DEPTH, DEC_BATCH, SSD_HEADS, SSD_HEAD_DIM, SSD_STATE), 0.1)
    state_hgrn = nrm((DEPTH, DEC_BATCH, HG_HEADS, HG_HEAD_DIM, HG_HEAD_DIM), 0.3)
    p_prompt = nrm((DEPTH, BATCH, SEQ, PLE_DIM), 1.0)
    p_sample = nrm((DEPTH, DEC_BATCH, DEC_SEQ, PLE_DIM), 1.0)

    norm_mix = 1.0 + nrm((DEPTH, D_MODEL), 0.05)
    w_in = nrm((DEPTH, D_MODEL, D_IN_PROJ), D_MODEL ** -0.5)
    conv_a_w = nrm((DEPTH, CONV_W, RG_WIDTH), CONV_W ** -0.5)
    conv_a_b = nrm((DEPTH, RG_WIDTH), 0.01)
    rg_wa = nrm((DEPTH, RG_HEADS, RG_HEAD_DIM, RG_HEAD_DIM), RG_HEAD_DIM ** -0.5)
    rg_ba = nrm((DEPTH, RG_WIDTH), 0.01)
    rg_wx = nrm((DEPTH, RG_HEADS, RG_HEAD_DIM, RG_HEAD_DIM), RG_HEAD_DIM ** -0.5)
    rg_bx = nrm((DEPTH, RG_WIDTH), 0.01)
    u = jax.random.uniform(next(ks), (DEPTH, RG_WIDTH), f32, 0.9, 0.999)
    a0 = u ** (1.0 / RG_C)
    rg_lambda = jnp.log(a0) - jnp.log1p(-a0)
    conv_b_w = nrm((DEPTH, CONV_W, SSD_CONV_DIM), CONV_W ** -0.5)
    conv_b_b = nrm((DEPTH, SSD_CONV_DIM), 0.01)
    dt0 = jnp.exp(jax.random.uniform(next(ks), (DEPTH, SSD_HEADS), f32, math.log(1e-3), math.log(1e-1)))
    ssd_dt_bias = dt0 + jnp.log(-jnp.expm1(-dt0))
    ssd_a_log = jnp.log(jax.random.uniform(next(ks), (DEPTH, SSD_HEADS), f32, 1.0, 16.0))
    ssd_d = 1.0 + nrm((DEPTH, SSD_HEADS), 0.1)
    ssd_norm = 1.0 + nrm((DEPTH, SSD_WIDTH), 0.05)
    hg_lower_bounds = nrm((DEPTH, HG_WIDTH), 0.5)
    hg_norm = 1.0 + nrm((DEPTH, HG_HEAD_DIM), 0.05)
    w_out = nrm((DEPTH, D_MIX, D_MODEL), D_MIX ** -0.5)
    norm_ffn = 1.0 + nrm((DEPTH, D_MODEL), 0.05)
    w_up = nrm((DEPTH, D_MODEL, D_FF), D_MODEL ** -0.5)
    w_down = nrm((DEPTH, D_FF, D_MODEL), D_FF ** -0.5)
    norm_ple = 1.0 + nrm((DEPTH, D_MODEL), 0.05)
    w_ple_gate = nrm((DEPTH, D_MODEL, D_MODEL), D_MODEL ** -0.5)
    w_ple_proj = nrm((DEPTH, PLE_DIM, D_MODEL), PLE_DIM ** -0.5)
    norm_final = 1.0 + nrm((D_MODEL,), 0.05)
    return {
        'x_prompt': x_prompt, 'x_sample': x_sample,
        'state_rg_conv': state_rg_conv, 'state_rg_h': state_rg_h,
        'state_ssd_conv': state_ssd_conv, 'state_ssd': state_ssd, 'state_hgrn': state_hgrn,
        'p_prompt': p_prompt, 'p_sample': p_sample,
        'norm_mix': norm_mix, 'w_in': w_in,
        'conv_a_w': conv_a_w, 'conv_a_b': conv_a_b,
        'rg_wa': rg_wa, 'rg_ba': rg_ba, 'rg_wx': rg_wx, 'rg_bx': rg_bx, 'rg_lambda': rg_lambda,
        'conv_b_w': conv_b_w, 'conv_b_b': conv_b_b,
        'ssd_dt_bias': ssd_dt_bias, 'ssd_a_log': ssd_a_log, 'ssd_d': ssd_d, 'ssd_norm': ssd_norm,
        'hg_lower_bounds': hg_lower_bounds, 'hg_norm': hg_norm,
        'w_out': w_out, 'norm_ffn': norm_ffn, 'w_up': w_up, 'w_down': w_down,
        'norm_ple': norm_ple, 'w_ple_gate': w_ple_gate, 'w_ple_proj': w_ple_proj,
        'norm_final': norm_final,
    }


def reference(x_prompt, x_sample, state_rg_conv, state_rg_h, state_ssd_conv, state_ssd, state_hgrn,
              p_prompt, p_sample, norm_mix, w_in, conv_a_w, conv_a_b, rg_wa, rg_ba, rg_wx, rg_bx,
              rg_lambda, conv_b_w, conv_b_b, ssd_dt_bias, ssd_a_log, ssd_d, ssd_norm,
              hg_lower_bounds, hg_norm, w_out, norm_ffn, w_up, w_down, norm_ple, w_ple_gate,
              w_ple_proj, norm_final):
    W = dict(norm_mix=norm_mix, w_in=w_in, conv_a_w=conv_a_w, conv_a_b=conv_a_b,
             rg_wa=rg_wa, rg_ba=rg_ba, rg_wx=rg_wx, rg_bx=rg_bx, rg_lambda=rg_lambda,
             conv_b_w=conv_b_w, conv_b_b=conv_b_b, ssd_dt_bias=ssd_dt_bias, ssd_a_log=ssd_a_log,
             ssd_d=ssd_d, ssd_norm=ssd_norm, hg_norm=hg_norm, w_out=w_out, norm_ffn=norm_ffn,
             w_up=w_up, w_down=w_down, norm_ple=norm_ple, w_ple_gate=w_ple_gate, w_ple_proj=w_ple_proj)
    sm = jax.nn.softmax(hg_lower_bounds.astype(jnp.float32), axis=0)
    cs = jnp.cumsum(sm, axis=0)
    lbs = cs - cs[0:1]

    f32 = jnp.float32
    z_rc = jnp.zeros((DEPTH, BATCH, CONV_W - 1, RG_WIDTH), x_prompt.dtype)
    z_rh = jnp.zeros((DEPTH, BATCH, RG_WIDTH), f32)
    z_sc = jnp.zeros((DEPTH, BATCH, CONV_W - 1, SSD_CONV_DIM), x_prompt.dtype)
    z_ss = jnp.zeros((DEPTH, BATCH, SSD_HEADS, SSD_HEAD_DIM, SSD_STATE), f32)
    z_hs = jnp.zeros((DEPTH, BATCH, HG_HEADS, HG_HEAD_DIM, HG_HEAD_DIM), f32)

    y_prompt, st_p = run_trunk(x_prompt, p_prompt, z_rc, z_rh, z_sc, z_ss, z_hs, W, norm_final, lbs, 0)
    y_sample, st_s = run_trunk(x_sample, p_sample, state_rg_conv, state_rg_h, state_ssd_conv,
                               state_ssd, state_hgrn, W, norm_final, lbs, PAST_LEN)
    rc_p, rh_p, sc_p, ss_p, hs_p = st_p
    rc_s, rh_s, sc_s, ss_s, hs_s = st_s
    return (y_prompt, y_sample, rc_p, rh_p, sc_p, ss_p, hs_p, rc_s, rh_s, sc_s, ss_s, hs_s)
```

```python
import contextlib
import numpy as np
import concourse.bass as bass
import concourse.mybir as mybir
from concourse.bass_utils import run_bass_kernel_spmd

F32 = mybir.dt.float32
BF16 = mybir.dt.bfloat16
ALU = mybir.AluOpType
AF = mybir.ActivationFunctionType
AX = mybir.AxisListType

NCORES = 8
D = 1024
SEQ = 2048
DEPTH = 2
NSAMP = 16
TPG = 1024
NSG = 8
NG = 2
TC = TPG + NSG
SGS = [(0, 512), (512, 512), (1024, NSG)]
EPS = 1e-6
DIN = 4360
DFF = 4096
BIGNEG = -30000.0

LC = 104
C_NMIX, C_NFFN, C_NPLE, C_CAW, C_CAB, C_RBA, C_RBX, C_LAM, C_CBW, C_CBB, C_DTB, C_DEXP, C_ALEXP = \
    0, 8, 16, 24, 40, 44, 48, 52, 56, 80, 86, 88, 92
C_SNORMC, C_HNORMC = 96, 100
C_NFIN, C_HLB0, C_HLB1 = 208, 216, 220
NCOLS = 224
LR = 592
R_SNORM, R_HNORM, R_SD, R_ALOG = 0, 512, 576, 584
NROWS = LR * 2
K_ID, K_ULE, K_ONES, K_NEGS, K_MASK2, K_I2, K_BLK, K_RMASK, K_E8 = 0, 128, 256, 384, 512, 640, 704, 832, 1856
NCONST = 2368

ENG = ['pe', 'act', 'dve', 'pool', 'sp']
BIG = 1 << 30


class Tile:
    def __init__(self, h, name, shape, base=0, ps=None, dsl=None, sloff=0.0, slscale=1.0, width=None):
        self.h = h
        self.name = name
        self.shape = list(shape)
        self.ps = ps if ps is not None else int(np.prod(shape[1:]))
        self.base = base
        self.dsl = dsl if dsl is not None else (0, BIG)
        self.sloff = sloff
        self.slscale = slscale
        self.width = width if width is not None else int(np.prod(shape[1:]))

    def v(self, off=0, dims=None, p0=0, n=None, sl=None):
        if n is None:
            n = self.shape[0] - p0
        if dims is None:
            dims = [[1, self.width - off]]
        ap = bass.AP(self.h, p0 * self.ps + self.base + off, [[self.ps, n]] + [list(d) for d in dims])
        return View(ap, self, sl if sl is not None else self.dsl)

    def csl(self, c0, c1):
        return (self.sloff + c0 * self.slscale, self.sloff + c1 * self.slscale)


class View:
    def __init__(self, ap, tile, sl):
        self.ap = ap
        self.tile = tile
        self.sl = sl


class Sched:
    def __init__(self, nc):
        self.nc = nc
        self.streams = {e: [] for e in ENG}
        self.count = {e: 0 for e in ENG}
        self.waited = {e: {} for e in ENG}
        self.recs = {}
        self.ndma = 0
        self.KD = 24
        self.dma_tokens = []
        self.out_tokens = []
        self.alias = {}
        self.nsw = 0
        self.NSWS = 8

    def _deps(self, eng, reads, writes):
        need = {}

        def add(tok, raw):
            sem, val, src = tok
            if src == eng and sem == eng:
                if eng == 'pe' or eng == 'sp':
                    return
                if not raw:
                    return
            if need.get(sem, 0) < val:
                need[sem] = val

        for v in reads:
            r = self.recs.setdefault(v.tile.name, {'w': [], 'r': []})
            for (lo, hi, tok) in r['w']:
                if lo < v.sl[1] and v.sl[0] < hi:
                    add(tok, True)
            if v.tile.name.startswith('ps'):
                for (lo, hi, tok) in r['r']:
                    if lo < v.sl[1] and v.sl[0] < hi and tok[2] != eng:
                        add(tok, False)
        for v in writes:
            r = self.recs.setdefault(v.tile.name, {'w': [], 'r': []})
            for (lo, hi, tok) in r['w']:
                if lo < v.sl[1] and v.sl[0] < hi:
                    add(tok, False)
            for (lo, hi, tok) in r['r']:
                if lo < v.sl[1] and v.sl[0] < hi:
                    add(tok, False)
        return need

    def _record(self, tok, reads, writes):
        for v in reads:
            r = self.recs[v.tile.name]
            r['r'] = [x for x in r['r'] if not (x[2][0] == tok[0] and v.sl[0] <= x[0] and x[1] <= v.sl[1])]
            r['r'].append((v.sl[0], v.sl[1], tok))
        for v in writes:
            r = self.recs[v.tile.name]
            r['w'] = [x for x in r['w'] if not (v.sl[0] <= x[0] and x[1] <= v.sl[1])]
            r['r'] = [x for x in r['r'] if not (v.sl[0] <= x[0] and x[1] <= v.sl[1])]
            r['w'].append((v.sl[0], v.sl[1], tok))

    def _emit(self, eng, need, fn, inc):
        waits = []
        for sem, val in need.items():
            if self.waited[eng].get(sem, 0) < val:
                self.waited[eng][sem] = val
                waits.append((sem, val))
        self.streams[eng].append((waits, fn, inc, None))

    def I(self, eng, method, *args, **kw):
        reads, writes = [], []
        for k, a in kw.items():
            if isinstance(a, View):
                (writes if k in ('out', 'accum_out', 'ap') else reads).append(a)
        if kw.pop('_rmw', False):
            pass
        need = self._deps(eng, reads, writes)
        self.count[eng] += 1
        tok = (eng, self.count[eng], eng)
        kw2 = {k: (a.ap if isinstance(a, View) else a) for k, a in kw.items()}

        def fn(e, method=method, args=args, kw2=kw2):
            return getattr(e, method)(*args, **kw2)

        self._emit(eng, need, fn, (eng, 1))
        self._record(tok, reads, writes)
        return tok

    def dma(self, eng, out, in_, is_output=False, **kw):
        reads = [in_] if isinstance(in_, View) else []
        writes = [out] if isinstance(out, View) else []
        need = self._deps(eng, reads, writes)
        n = self.ndma
        self.ndma += 1
        sem = 'dma%d' % (n % self.KD)
        val = 16 * (n // self.KD + 1)
        if n >= self.KD:
            ptok = self.dma_tokens[n - self.KD]
            if need.get(ptok[0], 0) < ptok[1]:
                need[ptok[0]] = ptok[1]
        tok = (sem, val, eng)
        self.dma_tokens.append(tok)
        o = out.ap if isinstance(out, View) else out
        i = in_.ap if isinstance(in_, View) else in_

        def fn(e, o=o, i=i, kw=kw):
            return e.dma_start(out=o, in_=i, **kw)

        self._emit(eng, need, fn, (sem, 16))
        self._record(tok, reads, writes)
        if is_output:
            self.out_tokens.append(tok)
        return tok

    def dma_sw(self, eng, out, in_, slot):
        reads = [in_] if isinstance(in_, View) else []
        writes = [out] if isinstance(out, View) else []
        need = self._deps(eng, reads, writes)
        key = 'sw%d' % self.nsw
        self.nsw += 1
        real = 'wsem%d' % slot
        self.alias[key] = real
        tok = (key, 16, eng)
        o = out.ap if isinstance(out, View) else out
        i = in_.ap if isinstance(in_, View) else in_

        def fn(e, o=o, i=i):
            return e.dma_start(out=o, in_=i)

        waits = []
        for sem, val in need.items():
            if self.waited[eng].get(sem, 0) < val:
                self.waited[eng][sem] = val
                waits.append((sem, val))
        self.streams[eng].append((waits, fn, (real, 16), real))
        self._record(tok, reads, writes)
        return tok

    def finish(self):
        need = {}
        for (sem, val, _) in self.out_tokens:
            need[sem] = max(need.get(sem, 0), val)
        waits = [(s, v) for s, v in need.items() if self.waited['sp'].get(s, 0) < v]
        self.streams['sp'].append((waits, None, None, None))

    def emit(self, stack):
        nc = self.nc
        sems = {}
        for e in ENG:
            sems[e] = stack.enter_context(nc.semaphore("s_" + e))
        for i in range(self.KD):
            sems['dma%d' % i] = stack.enter_context(nc.semaphore("s_dma%d" % i))
        for i in range(self.NSWS):
            sems['wsem%d' % i] = stack.enter_context(nc.semaphore("s_wsem%d" % i))
        alias = self.alias
        block = stack.enter_context(nc.Block())
        amap = {'pe': 'tensor', 'act': 'scalar', 'dve': 'vector', 'pool': 'gpsimd', 'sp': 'sync'}
        for e in ENG:
            stream = self.streams[e]

            def body(engine, stream=stream):
                for (waits, fn, inc, pre) in stream:
                    for (s, v) in waits:
                        engine.wait_ge(sems[alias.get(s, s)], v)
                    if pre is not None:
                        engine.sem_clear(sems[pre])
                    if fn is not None:
                        ins = fn(engine)
                        ins.then_inc(sems[inc[0]], inc[1])

            getattr(block, amap[e])(body)


def interleave(gens, width=2):
    it = iter(gens)
    active = []
    first = next(it, None)
    if first is not None:
        active.append(first)
    while active:
        for gnr in list(active):
            try:
                next(gnr)
            except StopIteration:
                active.remove(gnr)
        if len(active) < width:
            nxt = next(it, None)
            if nxt is not None:
                active.append(nxt)


def build_program():
    nc = bass.Bass("TRN2", target_bir_lowering=False)
    stack = contextlib.ExitStack()
    S = Sched(nc)

    def din(name, shape):
        return nc.dram_tensor(name, list(shape), F32, kind="ExternalInput")

    def dout(name, shape):
        return nc.dram_tensor(name, list(shape), F32, kind="ExternalOutput")

    xp = din("xp", [SEQ, D]); xs = din("xs", [NSAMP, D])
    pp = din("pp", [DEPTH, SEQ, 256]); psm = din("psm", [DEPTH, NSAMP, 256])
    st_rc = din("st_rc", [DEPTH, NSAMP, 3, 512]); st_rh = din("st_rh", [DEPTH, NSAMP, 512])
    st_sc = din("st_sc", [DEPTH, NSAMP, 3, 768]); st_ss = din("st_ss", [DEPTH, NSAMP, 512, 128])
    st_hs = din("st_hs", [DEPTH, NSAMP, 512, 64])
    w_in = din("w_in", [DEPTH, D, DIN]); w_out = din("w_out", [DEPTH, 1536, D])
    w_up = din("w_up", [DEPTH, D, DFF]); w_down = din("w_down", [DEPTH, DFF, D])
    w_pg = din("w_pg", [DEPTH, D, D]); w_pp = din("w_pp", [DEPTH, 256, D])
    rgw = din("rgw", [16, 128, 128])
    cols_d = din("cols", [128, NCOLS]); rows_d = din("rows", [128, NROWS]); cst_d = din("cst", [128, NCONST])

    y_p = dout("y_p", [SEQ, D]); y_s = dout("y_s", [NSAMP, D])
    o_rc_p = dout("o_rc_p", [DEPTH, 3, 512]); o_rh_p = dout("o_rh_p", [DEPTH, 512])
    o_sc_p = dout("o_sc_p", [DEPTH, 3, 768]); o_ss_p = dout("o_ss_p", [DEPTH, 512, 128])
    o_hs_p = dout("o_hs_p", [DEPTH, 512, 64])
    o_rc_s = dout("o_rc_s", [DEPTH, NSAMP, 3, 512]); o_rh_s = dout("o_rh_s", [DEPTH, NSAMP, 512])
    o_sc_s = dout("o_sc_s", [DEPTH, NSAMP, 3, 768]); o_ss_s = dout("o_ss_s", [DEPTH, NSAMP, 512, 128])
    o_hs_s = dout("o_hs_s", [DEPTH, NSAMP, 512, 64])

    def sb(name, shape, dt=F32):
        h = stack.enter_context(nc.sbuf_tensor(name, list(shape), dt))
        return Tile(h, name, shape)

    NWB = 3
    cst = sb("cstT", [128, NCONST]); cols = sb("colsT", [128, NCOLS]); rows = sb("rowsT", [128, NROWS])
    cbf = sb("cbf", [128, 640], BF16)
    der = sb("der", [128, 96])
    rgwb = sb("rgwb", [128, 16, 128], BF16)
    hT = sb("hT", [128, 8, TC]); uT = sb("uT", [128, 8, TC], BF16); ymT = sb("ymT", [128, 12, TC], BF16)
    wbs = [sb("wb%d" % i, [128, 8, 512], BF16) for i in range(NWB)]
    NBG = 10
    bigmem = sb("bigmem", [128, NBG * TC])
    bigs = [Tile(bigmem.h, "bigmem", [128, TC], base=i * TC, ps=NBG * TC, dsl=(i, i + 1), width=TC)
            for i in range(NBG)]
    xin = [bigs[9], sb("xin1", [128, 1024])]
    axs = sb("axs", [128, 4, NSG, 4]); cmp_ = sb("cmp", [128, 6, 24])
    rgc = sb("rgc", [128, DEPTH, 4, 3]); rgh = sb("rgh", [128, DEPTH, 4]); h0s = sb("h0s", [128, 4, NSG])
    pT = sb("pT", [128, 2, TC], BF16)
    zT = Tile(bigmem.h.bitcast(BF16), "bigmem", [128, 16, TC], ps=2 * NBG * TC, sloff=0.0, slscale=0.5,
              width=16 * TC)
    pss = []
    for i in range(4):
        h = stack.enter_context(nc.psum_tensor("ps%d" % i, [128, 1024], F32))
        pss.append(Tile(h, "ps%d" % i, [128, 1024]))
    psb = [Tile(t.h.bitcast(BF16), t.name, [128, 2048]) for t in pss]
    pctr = [0]
    ptouch = [0] * 8
    pclock = [0]

    please = [0] * 8

    def _touch(bank):
        t, hf = bank[0], bank[1]
        if len(bank) > 2 and please[t * 2 + hf] != bank[2]:
            raise RuntimeError("stale PSUM bank use: bank %d lease %d != %d" % (t * 2 + hf, bank[2], please[t * 2 + hf]))
        pclock[0] += 1
        ptouch[t * 2 + hf] = pclock[0]

    pheld = set()

    def phold(bank):
        pheld.add(bank[0] * 2 + bank[1])

    def prel(bank):
        pheld.discard(bank[0] * 2 + bank[1])

    def pbank():
        i = min((b for b in range(8) if b not in pheld), key=lambda b: ptouch[b])
        please[i] += 1
        bank = (i // 2, i % 2, please[i])
        _touch(bank)
        return bank

    def pbank2():
        t = min((k for k in range(4) if 2 * k not in pheld and 2 * k + 1 not in pheld),
                key=lambda k: max(ptouch[2 * k], ptouch[2 * k + 1]))
        please[2 * t] += 1; please[2 * t + 1] += 1
        b0 = (t, 0, please[2 * t]); b1 = (t, 1, please[2 * t + 1])
        _touch(b0); _touch(b1)
        return b0, b1

    def PS(bank, off=0, dims=None, p0=0, n=None):
        t, hf = bank[0], bank[1]
        _touch(bank)
        return pss[t].v(hf * 512 + off, dims if dims is not None else [[1, 512 - off]], p0=p0, n=n, sl=(hf, hf + 1))

    def PSB(bank, off=0, dims=None, p0=0, n=None):
        t, hf = bank[0], bank[1]
        _touch(bank)
        return psb[t].v(hf * 1024 + off, dims if dims is not None else [[1, 1024 - off]], p0=p0, n=n, sl=(hf, hf + 1))

    bctr = [0]

    def big(lo=0, hi=NBG):
        t = bigs[lo + bctr[0] % (hi - lo)]
        bctr[0] += 1
        return t

    def col(c, n=128):
        return cols.v(c, [[1, 1]], n=n)

    def dcol(c, n=128):
        return der.v(c, [[1, 1]], n=n)

    def CH(t, c, a=0, b=TC):
        W = t.shape[2]
        return t.v(c * W + a, [[1, b - a]], sl=t.csl(c, c + 1))

    ident = cst.v(K_ID, [[1, 128]])
    identb = cbf.v(0, [[1, 128]])
    onesb = cbf.v(256, [[1, 128]])
    ones_f = cst.v(K_ONES, [[1, 128]])
    ule_f = cst.v(K_ULE, [[1, 128]])

    S.dma('sp', cst.v(), cst_d.ap())
    S.dma('sp', cols.v(), cols_d.ap())
    S.dma('sp', rows.v(), rows_d.ap())
    S.dma('pool', rgwb.v(), rgw.ap().rearrange("a k m -> k a m"))
    S.I('act', 'activation', out=cbf.v(0, [[1, 384]]), in_=cst.v(0, [[1, 384]]), func=AF.Copy)
    S.I('act', 'activation', out=cbf.v(384, [[1, 128]]), in_=cst.v(K_BLK, [[1, 128]]), func=AF.Copy)
    S.I('act', 'activation', out=cbf.v(512, [[1, 128]]), in_=cst.v(K_NEGS, [[1, 128]]), func=AF.Copy)
    anegb = sb("anegb", [128, DEPTH, 8]); anegx = sb("anegx", [128, DEPTH, 4])
    TMW = 8500
    tm = sb("tm", [128, TMW])
    tm_bf = tm.h.bitcast(BF16)

    def tmt(off, n, bf=False):
        assert off + n <= TMW
        if bf:
            return Tile(tm_bf, "tm", [128, 2 * n], base=2 * off, ps=2 * TMW, dsl=(off, off + n), width=2 * n)
        return Tile(tm.h, "tm", [128, n], base=off, ps=TMW, dsl=(off, off + n), width=n)

    xabf = tmt(0, 516, bf=True)
    sqb = [tmt(520, 516, bf=True), tmt(1040, 516, bf=True)]
    rstd = tmt(1560, TC)
    ssc = sb("ssc", [128, DEPTH, 6, 3]); xbs = sb("xbs", [128, 6, NSG, 4])
    Sst = sb("Sst", [128, DEPTH, 512])
    Shg = sb("Shg", [128, DEPTH, 256]); hsm = sb("hsm", [128, 16, NSG]); Dd = sb("Dd", [128, 4, 16])
    S.I('dve', 'memset', ap=Shg.v(), constant=0.0)
    S.I('dve', 'memset', ap=ssc.v(), constant=0.0)
    S.I('dve', 'memset', ap=Sst.v(), constant=0.0)
    bigbf = bigmem.h.bitcast(BF16)

    def bigbf_tile(i0, nchunks):
        return Tile(bigbf, "bigmem", [128, nchunks, TC], base=2 * i0 * TC, ps=2 * NBG * TC, sloff=float(i0),
                    slscale=0.5, width=nchunks * TC, dsl=(i0, i0 + nchunks / 2.0))

    for l in range(DEPTH):
        lam = cols.v(l * LC + C_LAM, [[1, 4]])
        t1 = der.v(48, [[1, 4]])
        S.I('act', 'activation', out=t1, in_=lam, func=AF.Exp, scale=-1.0)
        S.I('act', 'activation', out=t1, in_=t1, func=AF.Ln, bias=1.0)
        S.I('dve', 'tensor_scalar', out=der.v(l * 16, [[1, 4]]), in0=t1, scalar1=-4.0, scalar2=None, op0=ALU.mult)
        S.I('dve', 'tensor_scalar', out=der.v(l * 16 + 4, [[1, 4]]), in0=t1, scalar1=-8.0, scalar2=None, op0=ALU.mult)
        S.I('dve', 'tensor_scalar', out=der.v(l * 16 + 8, [[1, 4]]), in0=cols.v(l * LC + C_RBA, [[1, 4]]), scalar1=0.5,
            scalar2=None, op0=ALU.mult)
        S.I('dve', 'tensor_scalar', out=der.v(l * 16 + 12, [[1, 4]]), in0=cols.v(l * LC + C_RBX, [[1, 4]]), scalar1=0.5,
            scalar2=None, op0=ALU.mult)
        S.I('act', 'activation', out=anegb.v(l * 8, [[1, 8]]), in_=rows.v(l * LR + R_ALOG, [[1, 8]]), func=AF.Exp)
        S.I('dve', 'tensor_scalar', out=anegb.v(l * 8, [[1, 8]]), in0=anegb.v(l * 8, [[1, 8]]), scalar1=-1.0,
            scalar2=None, op0=ALU.mult)
        S.I('act', 'activation', out=anegx.v(l * 4, [[1, 4]]), in_=cols.v(l * LC + C_ALEXP, [[1, 4]]), func=AF.Exp)
        S.I('dve', 'tensor_scalar', out=anegx.v(l * 4, [[1, 4]]), in0=anegx.v(l * 4, [[1, 4]]), scalar1=-1.0,
            scalar2=None, op0=ALU.mult)
    S.I('dve', 'memset', ap=der.v(32, [[1, 4]]), constant=0.0)
    S.I('dve', 'memset', ap=der.v(36, [[1, 4]]), constant=1.0)
    S.I('dve', 'tensor_tensor', out=der.v(52, [[1, 4]]), in0=cols.v(C_HLB1, [[1, 4]]), in1=cols.v(C_HLB0, [[1, 4]]),
        op=ALU.subtract)
    S.I('act', 'activation', out=der.v(40, [[1, 4]]), in_=der.v(52, [[1, 4]]), func=AF.Sigmoid)
    S.I('dve', 'tensor_scalar', out=der.v(44, [[1, 4]]), in0=der.v(40, [[1, 4]]), scalar1=-1.0, scalar2=1.0,
        op0=ALU.mult, op1=ALU.add)
    for l in range(DEPTH):
        S.I('dve', 'tensor_scalar', out=der.v(64 + l * 8, [[1, 4]]), in0=der.v(36 + l * 8, [[1, 4]]), scalar1=0.5,
            scalar2=None, op0=ALU.mult)
        S.I('dve', 'tensor_tensor', out=der.v(64 + l * 8 + 4, [[1, 4]]), in0=der.v(64 + l * 8, [[1, 4]]),
            in1=der.v(32 + l * 8, [[1, 4]]), op=ALU.add)
    S.I('dve', 'memset', ap=rgc.v(), constant=0.0)
    S.I('dve', 'memset', ap=rgh.v(), constant=0.0)

    wq = []

    def wpiece(dt_, l, r0, kc, c0, ncols):
        src = dt_[l, r0:r0 + kc * 128, c0:c0 + ncols].rearrange("(k p) n -> p k n", p=128)
        wq.append((src, kc, ncols))
        return len(wq) - 1

    wissued = [0]

    def wget(i):
        while wissued[0] < len(wq) and wissued[0] <= i + NWB - 2:
            j = wissued[0]
            src, kc, ncols = wq[j]
            t = wbs[j % NWB]
            S.dma('pool', t.v(0, [[512, kc], [1, ncols]]), src)
            wissued[0] += 1
        return wbs[i % NWB]

    plan = []
    for g in range(NG):
        for l in range(DEPTH):
            for c0, n in [(0, 512), (512, 512), (1024, 512), (1536, 512), (2048, 256), (2304, 8),
                          (2824, 512), (2312, 512), (3336, 512), (3848, 512)]:
                plan.append(wpiece(w_in, l, 0, 8, c0, n))
            for nh in range(2):
                plan.append(wpiece(w_out, l, 0, 8, nh * 512, 512))
                plan.append(wpiece(w_out, l, 1024, 4, nh * 512, 512))
            for G in range(2):
                for q in range(4):
                    plan.append(wpiece(w_up, l, 0, 8, G * 2048 + q * 512, 512))
                for nh in range(2):
                    for rb in range(2):
                        plan.append(wpiece(w_down, l, G * 2048 + rb * 1024, 8, nh * 512, 512))
            for nh in range(2):
                plan.append(wpiece(w_pg, l, 0, 8, nh * 512, 512))
            for nh in range(2):
                plan.append(wpiece(w_pp, l, 0, 2, nh * 512, 512))
    wptr = [0]

    def wnext():
        i = wptr[0]
        wptr[0] += 1
        return wget(i)

    def WV(t, kc, m0, M):
        return t.v(kc * 512 + m0, [[1, M]])

    def transpose_in(src_rows_ap, nrows, dst_tile, dst_cols0, nchunks, xt, xoff=0, dma=True, flip=0):
        if dma:
            S.dma('sp', xt.v(xoff, [[1, nchunks * 128]], n=nrows), src_rows_ap)
        for c0 in range(0, nchunks, 4):
            ncc = min(4, nchunks - c0)
            bk = pbank()
            for c in range(ncc):
                S.I('pe', 'transpose', out=PS(bk, c * 128, [[1, nrows]]),
                    in_=xt.v(xoff + (c0 + c) * 128, [[1, 128]], n=nrows), identity=cst.v(K_ID, [[1, nrows]], n=nrows))
            W = dst_tile.shape[2]
            eng = 'act' if (c0 // 4 + flip) % 2 == 0 else 'dve'
            outv = dst_tile.v(c0 * W + dst_cols0, [[W, ncc], [1, nrows]], sl=(c0, c0 + ncc))
            inv = PS(bk, 0, [[128, ncc], [1, nrows]])
            if eng == 'act':
                S.I('act', 'activation', out=outv, in_=inv, func=AF.Copy)
            else:
                S.I('dve', 'tensor_copy', out=outv, in_=inv)

    def norm(gcol0, out_tile, nchunks=8, src=None, in_place=False, dim=1024.0):
        src = src or hT
        bks = [pbank() for _ in SGS]
        for c in range(nchunks):
            sq = sqb[c % 2]
            S.I('act', 'activation', out=sq.v(), in_=CH(src, c), func=AF.Square)
            for si, (a, n) in enumerate(SGS):
                S.I('pe', 'matmul', out=PS(bks[si], 0, [[1, n]]), lhsT=onesb, rhs=sq.v(a, [[1, n]]),
                    start=(c == 0), stop=(c == nchunks - 1))
        for si, (a, n) in enumerate(SGS):
            S.I('act', 'activation', out=rstd.v(a, [[1, n]]), in_=PS(bks[si], 0, [[1, n]]), func=AF.Ln,
                scale=1.0 / dim, bias=EPS)
        S.I('act', 'activation', out=rstd.v(), in_=rstd.v(), func=AF.Exp, scale=-0.5)
        for c in range(nchunks):
            S.I('dve', 'scalar_tensor_tensor', out=CH(out_tile, c), in0=CH(src, c), scalar=col(gcol0 + c),
                in1=rstd.v(), op0=ALU.mult, op1=ALU.mult)

    def dense(wt, nk, m0, M, rhs_tile, consume, kc0=0, first=True, last=True, banks=None, sgs=None):
        sgs = sgs or SGS
        bks = banks or [pbank() for _ in sgs]
        for kc in range(nk):
            for si, (a, n) in enumerate(sgs):
                S.I('pe', 'matmul', out=PS(bks[si], 0, [[1, n]], n=M), lhsT=WV(wt, kc, m0, M),
                    rhs=CH(rhs_tile, kc0 + kc, a, a + n), start=(first and kc == 0), stop=(last and kc == nk - 1))
        if last and consume is not None:
            for si, (a, n) in enumerate(sgs):
                consume(si, a, n, bks[si])
        return bks

    for g in range(NG):
        t_base = g * TPG
        s_base = g * NSG
        for i in range(8):
            S.dma('sp', bigs[i].v(0, [[1, 1024]]), xp[t_base + i * 128: t_base + (i + 1) * 128, :])
        for i in range(8):
            transpose_in(None, 128, hT, i * 128, 8, bigs[i], dma=False)
        transpose_in(xs[s_base:s_base + NSG, :], NSG, hT, TPG, 8, bigs[8])

        for l in range(DEPTH):
            cb = l * LC
            pbufs = [xin[1], bigs[9]]
            for hb in range(2):
                S.dma('sp', pbufs[hb].v(0, [[256, 4], [1, 256]]),
                      pp[l, t_base + hb * 512: t_base + (hb + 1) * 512, :].rearrange("(i p) d -> p i d", p=128))
            for i in range(8):
                transpose_in(None, 128, pT, i * 128, 2, pbufs[i // 4], xoff=(i % 4) * 256, dma=False, flip=i % 2)
            transpose_in(psm[l, s_base:s_base + NSG, :], NSG, pT, TPG, 2, bigs[8])

            norm(cb + C_NMIX, uT)

            wax = wnext()
            S.I('dve', 'tensor_copy', out=bigmem.v(0, [[TC, 4], [1, 3]], sl=(0, 4)), in_=rgc.v(l * 12, [[3, 4], [1, 3]]))
            xt = xin[1]
            S.dma('sp', xt.v(0, [[1, 512]], n=3 * NSG),
                  st_rc[l, s_base:s_base + NSG].rearrange("b k c -> (b k) c"))
            bk = pbank()
            for c in range(4):
                S.I('pe', 'transpose', out=PS(bk, c * 32, [[1, 24]]), in_=xt.v(c * 128, [[1, 128]], n=24),
                    identity=cst.v(K_ID, [[1, 24]], n=24))
            S.I('dve', 'tensor_copy', out=axs.v(0, [[NSG * 4, 4], [4, NSG], [1, 3]]),
                in_=PS(bk, 0, [[32, 4], [3, NSG], [1, 3]]))
            S.dma('sp', xt.v(0, [[1, 512]], n=NSG), st_rh[l, s_base:s_base + NSG, :])
            bk = pbank()
            for c in range(4):
                S.I('pe', 'transpose', out=PS(bk, c * 8, [[1, NSG]]), in_=xt.v(c * 128, [[1, 128]], n=NSG),
                    identity=cst.v(K_ID, [[1, NSG]], n=NSG))
            S.I('dve', 'tensor_copy', out=h0s.v(), in_=PS(bk, 0, [[1, 4 * NSG]]))

            def ax_dense(c):
                def cons_ax(si, a, n, bank, c=c):
                    if si < 2:
                        S.I('act', 'activation', out=bigs[c].v(3 + a, [[1, n]]),
                            in_=PS(bank, 0, [[1, n]]), func=AF.Copy)
                    else:
                        S.I('act', 'activation', out=axs.v(c * NSG * 4 + 3, [[4, NSG]]), in_=PS(bank, 0, [[1, n]]),
                            func=AF.Copy)
                dense(wax, 8, c * 128, 128, uT, cons_ax)
            ax_dense(0)
            wag = wnext()
            def HV(t, a, n):
                lo = t.dsl[0] + (0.0 if a < 512 else 0.5)
                hi = t.dsl[0] + (0.5 if a + n <= 512 else 1.0)
                return t.v(a, [[1, n]], sl=(lo, hi))

            def rg_unit(c, hf):
                xa, r_, i_, a_, m_, hh = bigs[4:10]
                a0, nn = (0, 512) if hf == 0 else (512, 512 + NSG)
                npr = 512
                sgs_h = [SGS[0]] if hf == 0 else [SGS[1], SGS[2]]
                S.I('dve', 'tensor_scalar', out=HV(xa, a0, npr), in0=bigs[c].v(a0, [[1, npr]]),
                    scalar1=col(cb + C_CAW + c * 4), scalar2=col(cb + C_CAB + c), op0=ALU.mult, op1=ALU.add)
                for k in range(1, 4):
                    S.I('dve', 'scalar_tensor_tensor', out=HV(xa, a0, npr),
                        in0=bigs[c].v(a0 + k, [[1, npr]]), scalar=col(cb + C_CAW + c * 4 + k),
                        in1=HV(xa, a0, npr), op0=ALU.mult, op1=ALU.add)
                if hf == 1:
                    S.I('dve', 'tensor_scalar', out=HV(xa, TPG, NSG), in0=axs.v(c * NSG * 4, [[4, NSG]]),
                        scalar1=col(cb + C_CAW + c * 4), scalar2=col(cb + C_CAB + c), op0=ALU.mult, op1=ALU.add)
                    for k in range(1, 4):
                        S.I('dve', 'scalar_tensor_tensor', out=HV(xa, TPG, NSG),
                            in0=axs.v(c * NSG * 4 + k, [[4, NSG]]), scalar=col(cb + C_CAW + c * 4 + k),
                            in1=HV(xa, TPG, NSG), op0=ALU.mult, op1=ALU.add)
                xab = xabf.v(a0, [[1, nn]], sl=(xabf.dsl[0] + hf * 258, xabf.dsl[0] + (hf + 1) * 258))
                S.I('act', 'activation', out=xab, in_=HV(xa, a0, nn), func=AF.Copy)
                yield
                for which, dst, bcol in ((0, r_, C_RBA), (1, i_, C_RBX)):
                    for (a, n) in sgs_h:
                        bk = pbank()
                        S.I('pe', 'matmul', out=PS(bk, 0, [[1, n]]),
                            lhsT=rgwb.v(((l * 2 + which) * 4 + c) * 128, [[1, 128]]),
                            rhs=xabf.v(a, [[1, n]], sl=(xabf.dsl[0] + hf * 258, xabf.dsl[0] + (hf + 1) * 258)),
                            start=True, stop=True)
                        S.I('act', 'activation', out=HV(dst, a, n), in_=PS(bk, 0, [[1, n]]), func=AF.Tanh,
                            scale=0.5, bias=dcol(l * 16 + (8 if which == 0 else 12) + c))
                if hf == 0 and c < 3:
                    ax_dense(c + 1)
                S.I('act', 'activation', out=HV(a_, a0, nn), in_=HV(r_, a0, nn), func=AF.Exp,
                    scale=dcol(l * 16 + c), bias=dcol(l * 16 + c))
                S.I('act', 'activation', out=HV(m_, a0, nn), in_=HV(r_, a0, nn), func=AF.Exp,
                    scale=dcol(l * 16 + 4 + c), bias=dcol(l * 16 + 4 + c))
                S.I('act', 'activation', out=HV(m_, a0, nn), in_=HV(m_, a0, nn), func=AF.Ln, scale=-1.0, bias=1.0)
                S.I('act', 'activation', out=HV(m_, a0, nn), in_=HV(m_, a0, nn), func=AF.Exp, scale=0.5,
                    bias=-0.6931471805599453)
                yield
                if g == 0 and hf == 0:
                    S.I('dve', 'memset', ap=HV(m_, 0, 1), constant=0.5)
                S.I('dve', 'scalar_tensor_tensor', out=HV(i_, a0, nn), in0=HV(i_, a0, nn), scalar=1.0,
                    in1=HV(xa, a0, nn), op0=ALU.add, op1=ALU.mult)
                S.I('dve', 'tensor_tensor', out=HV(i_, a0, nn), in0=HV(i_, a0, nn), in1=HV(m_, a0, nn), op=ALU.mult)
                init = rgh.v(l * 4 + c, [[1, 1]]) if hf == 0 else HV(hh, 511, 1)
                S.I('dve', 'tensor_tensor_scan', out=HV(hh, a0, npr), data0=HV(a_, a0, npr),
                    data1=HV(i_, a0, npr), initial=init, op0=ALU.mult, op1=ALU.add)
                if hf == 1:
                    S.I('dve', 'tensor_copy', out=rgh.v(l * 4 + c, [[1, 1]]), in_=HV(hh, TPG - 1, 1))
                    S.I('dve', 'tensor_tensor', out=HV(hh, TPG, NSG), in0=HV(a_, TPG, NSG),
                        in1=h0s.v(c * NSG, [[1, NSG]]), op=ALU.mult)
                    S.I('dve', 'tensor_tensor', out=HV(hh, TPG, NSG), in0=HV(hh, TPG, NSG),
                        in1=HV(i_, TPG, NSG), op=ALU.add)
                    S.I('dve', 'tensor_copy', out=h0s.v(c * NSG, [[1, NSG]]), in_=HV(hh, TPG, NSG))

                yield
                def cons_ag(si, a, n, bank, c=c, hh=hh, xa=xa, r_=r_):
                    u1 = HV(xa, a, n)
                    u3 = HV(r_, a, n)
                    S.I('act', 'activation', out=u1, in_=PS(bank, 0, [[1, n]]), func=AF.Square)
                    S.I('dve', 'tensor_scalar', out=u1, in0=u1, scalar1=0.044715, scalar2=1.0, op0=ALU.mult,
                        op1=ALU.add)
                    S.I('dve', 'tensor_tensor', out=u3, in0=u1, in1=PS(bank, 0, [[1, n]]), op=ALU.mult)
                    S.I('act', 'activation', out=u3, in_=u3, func=AF.Tanh, scale=0.7978845608028654)
                    S.I('dve', 'scalar_tensor_tensor', out=u3, in0=u3, scalar=1.0, in1=PS(bank, 0, [[1, n]]),
                        op0=ALU.add, op1=ALU.mult)
                    S.I('dve', 'scalar_tensor_tensor', out=CH(ymT, c, a, a + n), in0=u3, scalar=0.5,
                        in1=HV(hh, a, n), op0=ALU.mult, op1=ALU.mult)
                dense(wag, 8, c * 128, 128, uT, cons_ag, sgs=sgs_h)

            interleave((rg_unit(c, hf) for c in range(4) for hf in range(2)), width=2)
            S.I('dve', 'tensor_copy', out=rgc.v(l * 12, [[3, 4], [1, 3]]), in_=bigmem.v(TPG, [[TC, 4], [1, 3]], sl=(0, 4)))
            bk = pbank()
            S.I('dve', 'tensor_copy', out=cmp_.v(0, [[24, 4], [3, NSG], [1, 3]]),
                in_=axs.v(1, [[NSG * 4, 4], [4, NSG], [1, 3]]))
            for c in range(4):
                S.I('pe', 'transpose', out=PS(bk, c * 128, [[1, 128]], n=24),
                    in_=cmp_.v(c * 24, [[1, 24]]), identity=ident)
            xo = big(8, 10)
            S.I('act', 'activation', out=xo.v(0, [[1, 512]], n=24), in_=PS(bk, 0, [[1, 512]], n=24), func=AF.Copy)
            S.dma('sp', o_rc_s[l, s_base:s_base + NSG].rearrange("b k c -> (b k) c"), xo.v(0, [[1, 512]], n=24),
                  is_output=True)
            bk = pbank()
            for c in range(4):
                S.I('pe', 'transpose', out=PS(bk, c * 128, [[1, 128]], n=NSG), in_=h0s.v(c * NSG, [[1, NSG]]),
                    identity=ident)
            xo = big(8, 10)
            S.I('act', 'activation', out=xo.v(0, [[1, 512]], n=NSG), in_=PS(bk, 0, [[1, 512]], n=NSG), func=AF.Copy)
            S.dma('sp', o_rh_s[l, s_base:s_base + NSG, :], xo.v(0, [[1, 512]], n=NSG), is_output=True)
            if g == NG - 1:
                bk = pbank()
                for c in range(4):
                    S.I('pe', 'transpose', out=PS(bk, c * 128, [[1, 128]], n=3), in_=rgc.v(l * 12 + c * 3, [[1, 3]]),
                        identity=ident)
                xo = big(8, 10)
                S.I('act', 'activation', out=xo.v(0, [[1, 512]], n=3), in_=PS(bk, 0, [[1, 512]], n=3), func=AF.Copy)
                S.dma('sp', o_rc_p[l], xo.v(0, [[1, 512]], n=3), is_output=True)
                bk = pbank()
                for c in range(4):
                    S.I('pe', 'transpose', out=PS(bk, c * 128, [[1, 128]], n=1), in_=rgh.v(l * 4 + c, [[1, 1]]),
                        identity=ident)
                xo = big(8, 10)
                S.I('act', 'activation', out=xo.v(0, [[1, 512]], n=1), in_=PS(bk, 0, [[1, 512]], n=1), func=AF.Copy)
                S.dma('sp', o_rh_p[l:l + 1, :], xo.v(0, [[1, 512]], n=1), is_output=True)


            zs = bigbf_tile(7, 4)
            xbf = bigbf_tile(9, 2)
            dtT = tmt(0, TC)
            raw2 = tmt(TC, TC)
            wz = wnext()
            for c in range(4):
                def cons_zs(si, a, n, bank, c=c):
                    S.I('act', 'activation', out=CH(zs, c, a, a + n), in_=PS(bank, 0, [[1, n]]), func=AF.Silu)
                dense(wz, 8, c * 128, 128, uT, cons_zs)
            xt = xin[1]
            S.dma('sp', xt.v(0, [[1, 768]], n=3 * NSG),
                  st_sc[l, s_base:s_base + NSG].rearrange("b k c -> (b k) c"))
            bk = pbank()
            for c in range(6):
                S.I('pe', 'transpose', out=PS(bk, c * 32, [[1, 24]]), in_=xt.v(c * 128, [[1, 128]], n=24),
                    identity=cst.v(K_ID, [[1, 24]], n=24))
            S.I('dve', 'tensor_copy', out=xbs.v(0, [[NSG * 4, 6], [4, NSG], [1, 3]]),
                in_=PS(bk, 0, [[32, 6], [3, NSG], [1, 3]]))
            wx1 = wnext(); wx2 = wnext()
            for c in range(6):
                raw = bigs[6] if c % 2 == 0 else raw2
                wt_, mc = (wx1, c) if c < 4 else (wx2, c - 4)
                S.I('dve', 'tensor_copy', out=raw.v(0, [[1, 3]]), in_=ssc.v((l * 6 + c) * 3, [[1, 3]]))

                def cons_xb(si, a, n, bank, c=c, raw=raw):
                    if si < 2:
                        S.I('act', 'activation', out=raw.v(3 + a, [[1, n]]), in_=PS(bank, 0, [[1, n]]), func=AF.Copy)
                    else:
                        S.I('act', 'activation', out=xbs.v(c * NSG * 4 + 3, [[4, NSG]]), in_=PS(bank, 0, [[1, n]]),
                            func=AF.Copy)
                dense(wt_, 8, mc * 128, 128, uT, cons_xb)
                S.I('dve', 'tensor_copy', out=ssc.v((l * 6 + c) * 3, [[1, 3]]), in_=raw.v(TPG, [[1, 3]]))
                xo_ = bigs[c]
                wc0 = cb + C_CBW + c * 4
                S.I('dve', 'tensor_scalar', out=xo_.v(0, [[1, TPG]]), in0=raw.v(0, [[1, TPG]]),
                    scalar1=col(wc0), scalar2=col(cb + C_CBB + c), op0=ALU.mult, op1=ALU.add)
                for k in range(1, 4):
                    S.I('dve', 'scalar_tensor_tensor', out=xo_.v(0, [[1, TPG]]), in0=raw.v(k, [[1, TPG]]),
                        scalar=col(wc0 + k), in1=xo_.v(0, [[1, TPG]]), op0=ALU.mult, op1=ALU.add)
                S.I('dve', 'tensor_scalar', out=xo_.v(TPG, [[1, NSG]]), in0=xbs.v(c * NSG * 4, [[4, NSG]]),
                    scalar1=col(wc0), scalar2=col(cb + C_CBB + c), op0=ALU.mult, op1=ALU.add)
                for k in range(1, 4):
                    S.I('dve', 'scalar_tensor_tensor', out=xo_.v(TPG, [[1, NSG]]),
                        in0=xbs.v(c * NSG * 4 + k, [[4, NSG]]), scalar=col(wc0 + k), in1=xo_.v(TPG, [[1, NSG]]),
                        op0=ALU.mult, op1=ALU.add)
                S.I('act', 'activation', out=xo_.v(), in_=xo_.v(), func=AF.Silu)
                if c >= 4:
                    S.I('act', 'activation', out=CH(xbf, c - 4), in_=xo_.v(), func=AF.Copy)
            S.I('dve', 'tensor_copy', out=cmp_.v(0, [[24, 6], [3, NSG], [1, 3]]),
                in_=xbs.v(1, [[NSG * 4, 6], [4, NSG], [1, 3]]))
            bka, bkb = pbank2()
            for c in range(6):
                bk_ = bka if c < 4 else bkb
                S.I('pe', 'transpose', out=PS(bk_, (c % 4) * 128, [[1, 128]], n=24), in_=cmp_.v(c * 24, [[1, 24]]),
                    identity=ident)
            xo = big(6, 7)
            S.I('act', 'activation', out=xo.v(0, [[1, 512]], n=24), in_=PS(bka, 0, [[1, 512]], n=24), func=AF.Copy)
            S.I('act', 'activation', out=xo.v(512, [[1, 256]], n=24), in_=PS(bkb, 0, [[1, 256]], n=24), func=AF.Copy)
            S.dma('sp', o_sc_s[l, s_base:s_base + NSG].rearrange("b k c -> (b k) c"), xo.v(0, [[1, 768]], n=24),
                  is_output=True)
            if g == NG - 1:
                bka, bkb = pbank2()
                for c in range(6):
                    bk_ = bka if c < 4 else bkb
                    S.I('pe', 'transpose', out=PS(bk_, (c % 4) * 128, [[1, 128]], n=3),
                        in_=ssc.v((l * 6 + c) * 3, [[1, 3]]), identity=ident)
                xo = raw2
                S.I('act', 'activation', out=xo.v(0, [[1, 512]], n=3), in_=PS(bka, 0, [[1, 512]], n=3), func=AF.Copy)
                S.I('act', 'activation', out=xo.v(512, [[1, 256]], n=3), in_=PS(bkb, 0, [[1, 256]], n=3),
                    func=AF.Copy)
                S.dma('sp', o_sc_p[l], xo.v(0, [[1, 768]], n=3), is_output=True)
            wdt = wnext()

            def cons_dt(si, a, n, bank):
                S.I('act', 'activation', out=dtT.v(a, [[1, n]], n=8), in_=PS(bank, 0, [[1, n]], n=8), func=AF.Exp,
                    bias=col(cb + C_DTB, n=8))
            dense(wdt, 8, 0, 8, uT, cons_dt)
            S.I('act', 'activation', out=dtT.v(0, [[1, TC]], n=8), in_=dtT.v(0, [[1, TC]], n=8), func=AF.Ln, bias=1.0)

            o0 = 2 * TC
            Ss = tmt(o0, 4096); t1 = tmt(o0 + 4096, 1024); dg = tmt(o0 + 5120, 1024)
            sm2 = tmt(o0 + 6144, 256)
            dte = sm2.v(0, [[1, 32]]); dec = sm2.v(32, [[1, 32]]); xdts = sm2.v(64, [[1, 32]])
            ys = sm2.v(96, [[1, 32]]); ygs = sm2.v(128, [[1, 32]]); rs8 = sm2.v(160, [[1, 8]])
            sq8 = sm2.v(192, [[1, 32]])
            for c in range(4):
                S.dma('sp', Ss.v(c * 1024, [[128, NSG], [1, 128]]),
                      st_ss[l, s_base:s_base + NSG, c * 128:(c + 1) * 128, :].rearrange("b p n -> p b n"))
            bk = pbank()
            for c in range(4):
                S.I('pe', 'matmul', out=PS(bk, c * 8, [[1, NSG]]), lhsT=cst.v(K_E8 + c * 128, [[1, 128]], n=8),
                    rhs=dtT.v(TPG, [[1, NSG]], n=8), start=True, stop=True)
            S.I('dve', 'tensor_copy', out=dte, in_=PS(bk, 0, [[1, 32]]))
            for c in range(4):
                S.I('act', 'activation', out=sm2.v(32 + c * 8, [[1, 8]]), in_=sm2.v(c * 8, [[1, 8]]), func=AF.Exp,
                    scale=anegx.v(l * 4 + c, [[1, 1]]))
                S.I('dve', 'tensor_tensor', out=sm2.v(64 + c * 8, [[1, 8]]), in0=sm2.v(c * 8, [[1, 8]]),
                    in1=bigs[c].v(TPG, [[1, NSG]]), op=ALU.mult)
            bcs = []
            for which in (4, 5):
                S.I('dve', 'tensor_tensor', out=dg.v(0, [[128, NSG], [1, 128]]), in0=cst.v(K_ID, [[0, NSG], [1, 128]]),
                    in1=bigs[which].v(TPG, [[1, NSG], [0, 128]]), op=ALU.mult)
                b2 = pbank2()
                for hf in range(2):
                    S.I('pe', 'matmul', out=PS(b2[hf]), lhsT=ones_f, rhs=dg.v(hf * 512, [[1, 512]]), start=True,
                        stop=True)
                bcs.append(b2)
            for c in range(4):
                for hf in range(2):
                    S.I('dve', 'tensor_tensor', out=t1.v(hf * 512, [[128, 4], [1, 128]]),
                        in0=PS(bcs[0][hf], 0, [[128, 4], [1, 128]]),
                        in1=sm2.v(64 + c * 8 + hf * 4, [[1, 4], [0, 128]]), op=ALU.mult)
                Sc = Ss.v(c * 1024, [[128, NSG], [1, 128]])
                S.I('dve', 'tensor_tensor', out=Sc, in0=Sc, in1=sm2.v(32 + c * 8, [[1, NSG], [0, 128]]), op=ALU.mult)
                S.I('dve', 'tensor_tensor', out=Sc, in0=Sc, in1=t1.v(0, [[128, NSG], [1, 128]]), op=ALU.add)
                S.dma('sp', o_ss_s[l, s_base:s_base + NSG, c * 128:(c + 1) * 128, :].rearrange("b p n -> p b n"),
                      Ss.v(c * 1024, [[128, NSG], [1, 128]]), is_output=True)
                for hf in range(2):
                    S.I('dve', 'tensor_tensor', out=t1.v(hf * 512, [[128, 4], [1, 128]]),
                        in0=PS(bcs[1][hf], 0, [[128, 4], [1, 128]]),
                        in1=Ss.v(c * 1024 + hf * 512, [[128, 4], [1, 128]]), op=ALU.mult)
                S.I('dve', 'tensor_reduce', out=sm2.v(96 + c * 8, [[1, NSG]]), in_=t1.v(0, [[128, NSG], [1, 128]]),
                    axis=AX.X, op=ALU.add)
                S.I('dve', 'scalar_tensor_tensor', out=sm2.v(96 + c * 8, [[1, NSG]]), in0=bigs[c].v(TPG, [[1, NSG]]),
                    scalar=col(cb + C_DEXP + c), in1=sm2.v(96 + c * 8, [[1, NSG]]), op0=ALU.mult, op1=ALU.add)
                S.I('dve', 'tensor_tensor', out=sm2.v(128 + c * 8, [[1, NSG]]), in0=sm2.v(96 + c * 8, [[1, NSG]]),
                    in1=CH(zs, c, TPG, TC), op=ALU.mult)
            S.I('dve', 'tensor_tensor', out=sq8, in0=ygs, in1=ygs, op=ALU.mult)
            bk = pbank()
            for c in range(4):
                S.I('pe', 'matmul', out=PS(bk, 0, [[1, NSG]]), lhsT=ones_f, rhs=sm2.v(192 + c * 8, [[1, NSG]]),
                    start=(c == 0), stop=(c == 3))
            S.I('act', 'activation', out=rs8, in_=PS(bk, 0, [[1, NSG]]), func=AF.Ln, scale=1.0 / 512, bias=EPS)
            S.I('act', 'activation', out=rs8, in_=rs8, func=AF.Exp, scale=-0.5)
            for c in range(4):
                S.I('dve', 'scalar_tensor_tensor', out=CH(ymT, 4 + c, TPG, TC), in0=sm2.v(128 + c * 8, [[1, NSG]]),
                    scalar=col(cb + C_SNORMC + c), in1=rs8, op0=ALU.mult, op1=ALU.mult)

            o1 = TC

            def sset(p):
                o = o1 + p * 3328
                return dict(R1h=tmt(o, 512, bf=True), R1l=tmt(o + 512, 512, bf=True), LM=tmt(o + 1024, 512, bf=True), smt=tmt(o + 1536, 128),
                            cbs=tmt(o + 1664, 64, bf=True), Btm=tmt(o + 1728, 64, bf=True),
                            xdt=tmt(o + 1792, 256, bf=True), xw=tmt(o + 2048, 256, bf=True),
                            xDb=tmt(o + 2304, 256, bf=True), yy=tmt(o + 2560, 512), ynb=tmt(o + 3072, 256, bf=True),
                            dth=tmt(o + 1536 + 96, 4, bf=True), dtl=tmt(o + 1536 + 104, 4, bf=True))
            ssets = [sset(0), sset(1)]
            junk = tmt(o1 + 6656, 256, bf=True)
            Sbf = [tmt(o1 + 6912, 256, bf=True), tmt(o1 + 7168, 256, bf=True)]
            Scur = Sst.v(l * 512, [[1, 512]])
            S.I('act', 'activation', out=Sbf[0].v(), in_=Scur, func=AF.Copy)
            ugt_f = cst.v(K_NEGS, [[1, 128]])

            def ssd_tile(j):
                t0 = j * 128
                Q = ssets[j % 2]
                R1h, R1l, LM, smt, cbs, Btm, xdt, xw, xDb, yy, ynb = (Q['R1h'], Q['R1l'], Q['LM'], Q['smt'], Q['cbs'], Q['Btm'], Q['xdt'],
                                                               Q['xw'], Q['xDb'], Q['yy'], Q['ynb'])
                dt_tm = smt.v(0, [[1, 8]]); dta = smt.v(8, [[1, 8]]); cum_sb = smt.v(16, [[1, 16]])
                ecum = smt.v(32, [[1, 8]]); etot = smt.v(40, [[1, 8]]); wdec = smt.v(48, [[1, 8]])
                w2 = smt.v(56, [[1, 8]]); ddv = smt.v(64, [[1, 8]]); ssq = smt.v(72, [[1, 1]])
                dth = Q['dth']; dtl = Q['dtl']
                bkX = pbank()
                phold(bkX)
                for c in range(4):
                    S.I('pe', 'transpose', out=PS(bkX, c * 128, [[1, 128]]), in_=bigs[c].v(t0, [[1, 128]]),
                        identity=ident)
                bkB = pbank()
                S.I('pe', 'transpose', out=PS(bkB, 0, [[1, 128]]), in_=bigs[4].v(t0, [[1, 128]]), identity=ident)
                S.I('pe', 'transpose', out=PS(bkB, 128, [[1, 8]]), in_=dtT.v(t0, [[1, 128]], n=8),
                    identity=cst.v(K_ID, [[1, 8]], n=8))
                S.I('dve', 'tensor_copy', out=dt_tm, in_=PS(bkB, 128, [[1, 8]]))
                S.I('dve', 'tensor_tensor', out=dta, in0=dt_tm, in1=anegb.v(l * 8, [[1, 8]]), op=ALU.mult)
                S.I('dve', 'tensor_copy', out=Btm.v(), in_=PS(bkB, 0, [[1, 128]]))
                S.I('dve', 'tensor_copy', out=dth.v(), in_=dta)
                S.I('dve', 'tensor_tensor', out=dtl.v(), in0=dta, in1=dth.v(), op=ALU.subtract)
                S.I('pool', 'tensor_tensor', out=R1h.v(0, [[128, 8], [1, 128]]), in0=cbf.v(128, [[0, 8], [1, 128]]),
                    in1=dth.v(0, [[1, 8], [0, 128]]), op=ALU.mult)
                S.I('dve', 'tensor_tensor', out=R1l.v(0, [[128, 8], [1, 128]]), in0=cbf.v(128, [[0, 8], [1, 128]]),
                    in1=dtl.v(0, [[1, 8], [0, 128]]), op=ALU.mult)
                yield
                b2 = pbank2()
                for hf in range(2):
                    S.I('pe', 'matmul', out=PS(b2[hf]), lhsT=cbf.v(512, [[1, 128]]), rhs=R1h.v(hf * 512, [[1, 512]]),
                        start=True, stop=False)
                    S.I('pe', 'matmul', out=PS(b2[hf]), lhsT=cbf.v(512, [[1, 128]]), rhs=R1l.v(hf * 512, [[1, 512]]),
                        start=False, stop=True)
                bkC = pbank()
                S.I('pe', 'matmul', out=PS(bkC, 0, [[1, 8]]), lhsT=ule_f, rhs=dta, start=True, stop=True)
                S.I('pe', 'matmul', out=PS(bkC, 8, [[1, 8]]), lhsT=ones_f, rhs=dta, start=True, stop=True)
                S.I('pe', 'matmul', out=PS(bkC, 128, [[1, 128]]), lhsT=CH(xbf, 0, t0, t0 + 128),
                    rhs=CH(xbf, 1, t0, t0 + 128), start=True, stop=True)
                for hf in range(2):
                    S.I('act', 'activation', out=LM.v(hf * 512, [[1, 512]]), in_=PS(b2[hf]), func=AF.Exp)
                S.I('dve', 'tensor_copy', out=cum_sb, in_=PS(bkC, 0, [[1, 16]]))
                S.I('dve', 'tensor_tensor', out=cbs.v(), in0=PS(bkC, 128, [[1, 128]]), in1=ule_f, op=ALU.mult)
                S.I('dve', 'tensor_tensor', out=ddv, in0=smt.v(24, [[1, 8]]), in1=smt.v(16, [[1, 8]]), op=ALU.subtract)
                S.I('act', 'activation', out=ecum, in_=smt.v(16, [[1, 8]]), func=AF.Exp)
                S.I('act', 'activation', out=etot, in_=smt.v(24, [[1, 8]]), func=AF.Exp)
                S.I('act', 'activation', out=wdec, in_=ddv, func=AF.Exp)
                S.I('dve', 'tensor_tensor', out=w2, in0=dt_tm, in1=wdec, op=ALU.mult)
                S.I('dve', 'tensor_tensor', out=LM.v(0, [[128, 8], [1, 128]]), in0=LM.v(0, [[128, 8], [1, 128]]),
                    in1=cbs.v(0, [[0, 8], [1, 128]]), op=ALU.mult)
                X3 = PS(bkX, 0, [[64, 8], [1, 64]])
                S.I('dve', 'tensor_tensor', out=xdt.v(0, [[64, 8], [1, 64]]), in0=X3, in1=smt.v(0, [[1, 8], [0, 64]]),
                    op=ALU.mult)
                S.I('dve', 'tensor_tensor', out=xw.v(0, [[64, 8], [1, 64]]), in0=X3, in1=smt.v(56, [[1, 8], [0, 64]]),
                    op=ALU.mult)
                S.I('dve', 'tensor_tensor', out=xDb.v(0, [[64, 8], [1, 64]]), in0=X3,
                    in1=rows.v(l * LR + R_SD, [[1, 8], [0, 64]]), op=ALU.mult)
                prel(bkX)
                yield
                bkY = pbank()
                S.I('pe', 'matmul', out=PS(bkY), lhsT=identb, rhs=xDb.v(), start=True, stop=False,
                    skip_group_check=True)
                for h in range(8):
                    S.I('pe', 'matmul', out=PS(bkY, h * 64, [[1, 64]]), lhsT=LM.v(h * 128, [[1, 128]]),
                        rhs=xdt.v(h * 64, [[1, 64]]), start=False, stop=(h == 7), skip_group_check=True)
                bkD = pbank()
                S.I('pe', 'matmul', out=PS(bkD), lhsT=Btm.v(), rhs=xw.v(), start=True, stop=True)
                bkYi = pbank()
                S.I('pe', 'matmul', out=PS(bkYi), lhsT=CH(xbf, 1, t0, t0 + 128), rhs=Sbf[j % 2].v(), start=True,
                    stop=True)
                S.I('dve', 'tensor_tensor', out=Sst.v(l * 512, [[64, 8], [1, 64]]), in0=Sst.v(l * 512, [[64, 8], [1, 64]]),
                    in1=smt.v(40, [[1, 8], [0, 64]]), op=ALU.mult)
                S.I('dve', 'tensor_tensor', out=Scur, in0=Scur, in1=PS(bkD), op=ALU.add)
                S.I('act', 'activation', out=Sbf[(j + 1) % 2].v(), in_=Scur, func=AF.Copy)
                S.I('dve', 'tensor_tensor', out=yy.v(0, [[64, 8], [1, 64]]), in0=PS(bkYi, 0, [[64, 8], [1, 64]]),
                    in1=smt.v(32, [[1, 8], [0, 64]]), op=ALU.mult)
                S.I('dve', 'tensor_tensor', out=yy.v(), in0=yy.v(), in1=PS(bkY), op=ALU.add)
                yield
                bkZ = pbank()
                for c in range(4):
                    S.I('pe', 'transpose', out=PSB(bkZ, c * 128, [[1, 128]]), in_=CH(zs, c, t0, t0 + 128),
                        identity=identb)
                S.I('dve', 'tensor_tensor', out=yy.v(), in0=yy.v(), in1=PSB(bkZ, 0, [[1, 512]]), op=ALU.mult)
                S.I('act', 'activation', out=junk.v(), in_=yy.v(), func=AF.Square, accum_out=ssq)
                S.I('act', 'activation', out=ssq, in_=ssq, func=AF.Ln, scale=1.0 / 512, bias=EPS)
                S.I('act', 'activation', out=ssq, in_=ssq, func=AF.Exp, scale=-0.5)
                S.I('dve', 'scalar_tensor_tensor', out=ynb.v(), in0=yy.v(), scalar=ssq,
                    in1=rows.v(l * LR + R_SNORM, [[1, 512]]), op0=ALU.mult, op1=ALU.mult)
                bkT = pbank()
                for c in range(4):
                    S.I('pe', 'transpose', out=PSB(bkT, c * 128, [[1, 128]]), in_=ynb.v(c * 128, [[1, 128]]),
                        identity=identb)
                S.I('act', 'activation', out=ymT.v(4 * TC + t0, [[TC, 4], [1, 128]], sl=(4, 8)),
                    in_=PSB(bkT, 0, [[128, 4], [1, 128]]), func=AF.Copy)

            interleave((ssd_tile(j) for j in range(8)), width=2)
            xD = tmt(o1 + 2560, 512)
            if g == NG - 1:
                bk = pbank()
                for c in range(4):
                    S.I('pe', 'transpose', out=PS(bk, c * 128, [[1, 128]]), in_=Sst.v(l * 512 + c * 128, [[1, 128]]),
                        identity=ident)
                S.I('act', 'activation', out=xD.v(), in_=PS(bk), func=AF.Copy)
                S.dma('sp', o_ss_p[l].rearrange("(c p) n -> p c n", p=128), xD.v(0, [[128, 4], [1, 128]]),
                      is_output=True)


            AqT = bigbf_tile(0, 4); BkT = bigbf_tile(2, 4); KdT = bigbf_tile(4, 4)
            vTb = bigbf_tile(6, 4); gsT = bigbf_tile(8, 4)
            tA = tmt(0, TC); tB = tmt(TC, TC); tC_ = tmt(2 * TC, TC)
            tE = [tmt((3 + c) * TC, TC) for c in range(4)]
            blk_f = cst.v(K_BLK, [[1, 128]])
            def HV2(t, a, n):
                w_ = (t.dsl[1] - t.dsl[0]) / 2.0
                lo = t.dsl[0] + (0.0 if a < 512 else w_)
                hi = t.dsl[0] + (w_ if a + n <= 512 else 2 * w_)
                return t.v(a, [[1, n]], sl=(lo, hi))

            wf_ = wnext()

            def hgf_unit(c, hf):
                a0, nn = (0, 512) if hf == 0 else (512, 512 + NSG)
                sgs_h = [SGS[0]] if hf == 0 else [SGS[1], SGS[2]]

                def cons_f(si, a, n, bank):
                    S.I('act', 'activation', out=HV2(tA, a, n), in_=PS(bank, 0, [[1, n]]), func=AF.Tanh, scale=0.5)
                dense(wf_, 8, c * 128, 128, uT, cons_f, sgs=sgs_h)
                yield
                S.I('dve', 'tensor_scalar', out=HV2(tA, a0, nn), in0=HV2(tA, a0, nn), scalar1=dcol(64 + l * 8 + c),
                    scalar2=dcol(64 + l * 8 + 4 + c), op0=ALU.mult, op1=ALU.add)
                if hf == 1:
                    S.I('dve', 'tensor_copy', out=hsm.v((4 + c) * NSG, [[1, NSG]]), in_=HV2(tA, TPG, NSG))
                S.I('act', 'activation', out=HV2(tB, a0, nn), in_=HV2(tA, a0, nn), func=AF.Ln)
                S.I('dve', 'tensor_scalar', out=HV2(tA, a0, nn), in0=HV2(tA, a0, nn), scalar1=-1.0, scalar2=1.0,
                    op0=ALU.mult, op1=ALU.add)
                if hf == 1:
                    S.I('dve', 'tensor_copy', out=hsm.v((8 + c) * NSG, [[1, NSG]]), in_=HV2(tA, TPG, NSG))
                yield
                S.I('dve', 'tensor_tensor_scan', out=HV2(tC_, a0, 512), data0=cst.v(K_RMASK + a0, [[1, 512]]),
                    data1=HV2(tB, a0, 512), initial=0.0, op0=ALU.mult, op1=ALU.add)
                S.I('act', 'activation', out=HV2(tE[c], a0, 512), in_=HV2(tC_, a0, 512), func=AF.Exp)
                S.I('act', 'activation', out=HV2(tB, a0, 512), in_=HV2(tC_, a0, 512), func=AF.Exp, scale=-1.0)
                yield
                S.I('dve', 'tensor_tensor', out=HV2(tC_, a0, 512), in0=HV2(tA, a0, 512), in1=HV2(tB, a0, 512),
                    op=ALU.mult)
                S.I('act', 'activation', out=CH(BkT, c, a0, a0 + 512), in_=HV2(tC_, a0, 512), func=AF.Copy)
                slh = (tC_.dsl[0] + hf * (TC / 2.0), tC_.dsl[0] + (hf + 1) * (TC / 2.0))
                sle = (tE[c].dsl[0] + hf * (TC / 2.0), tE[c].dsl[0] + (hf + 1) * (TC / 2.0))
                S.I('dve', 'tensor_tensor', out=KdT.v(c * TC + a0, [[64, 8], [1, 64]], sl=KdT.csl(c, c + 1)),
                    in0=tC_.v(a0, [[64, 8], [1, 64]], sl=slh), in1=tE[c].v(a0 + 63, [[64, 8], [0, 64]], sl=sle),
                    op=ALU.mult)
                S.I('dve', 'tensor_copy', out=Dd.v(c * 16 + hf * 8, [[1, 8]]), in_=tE[c].v(a0 + 63, [[64, 8]], sl=sle))

            interleave((hgf_unit(c, hf) for c in range(4) for hf in range(2)), width=2)
            wq_ = wnext()

            def hgq_unit(c, hf):
                a0, nn = (0, 512) if hf == 0 else (512, 512 + NSG)
                sgs_h = [SGS[0]] if hf == 0 else [SGS[1], SGS[2]]

                def cons_q(si, a, n, bank):
                    S.I('act', 'activation', out=HV2(tA, a, n), in_=PS(bank, 0, [[1, n]]), func=AF.Silu)
                dense(wq_, 8, c * 128, 128, uT, cons_q, sgs=sgs_h)
                yield
                if hf == 1:
                    S.I('dve', 'tensor_copy', out=hsm.v(c * NSG, [[1, NSG]]), in_=HV2(tA, TPG, NSG))
                S.I('dve', 'tensor_tensor', out=CH(AqT, c, a0, a0 + 512), in0=HV2(tA, a0, 512),
                    in1=HV2(tE[c], a0, 512), op=ALU.mult)

            interleave((hgq_unit(c, hf) for c in range(4) for hf in range(2)), width=2)
            wi_ = wnext()
            for c in range(4):
                def cons_v(si, a, n, bank, c=c):
                    if si < 2:
                        S.I('act', 'activation', out=CH(vTb, c, a, a + n), in_=PS(bank, 0, [[1, n]]), func=AF.Copy)
                    else:
                        S.I('act', 'activation', out=hsm.v((12 + c) * NSG, [[1, NSG]]), in_=PS(bank, 0, [[1, n]]),
                            func=AF.Copy)
                dense(wi_, 8, c * 128, 128, uT, cons_v)
            wg2 = wnext()
            for c in range(4):
                def cons_g2(si, a, n, bank, c=c):
                    S.I('act', 'activation', out=CH(gsT, c, a, a + n), in_=PS(bank, 0, [[1, n]]), func=AF.Silu)
                dense(wg2, 8, c * 128, 128, uT, cons_g2)

            Ssh = tmt(0, 2048); dgv = tmt(2048, 512); h1 = tmt(2560, 512); h2 = tmt(3072, 512)
            hs2 = tmt(3584, 128)
            for c in range(4):
                S.dma('sp', Ssh.v(c * 512, [[64, NSG], [1, 64]]),
                      st_hs[l, s_base:s_base + NSG, c * 128:(c + 1) * 128, :].rearrange("b p e -> p b e"))
            for c in range(4):
                S.I('dve', 'tensor_tensor', out=dgv.v(0, [[64, NSG], [1, 64]]), in0=cst.v(K_I2, [[0, NSG], [1, 64]]),
                    in1=hsm.v((12 + c) * NSG, [[1, NSG], [0, 64]]), op=ALU.mult)
                bkv = pbank()
                S.I('pe', 'matmul', out=PS(bkv), lhsT=blk_f, rhs=dgv.v(), start=True, stop=True)
                S.I('dve', 'tensor_tensor', out=h1.v(0, [[64, NSG], [1, 64]]), in0=PS(bkv, 0, [[64, NSG], [1, 64]]),
                    in1=hsm.v((8 + c) * NSG, [[1, NSG], [0, 64]]), op=ALU.mult)
                Sc = Ssh.v(c * 512, [[64, NSG], [1, 64]])
                S.I('dve', 'tensor_tensor', out=Sc, in0=Sc, in1=hsm.v((4 + c) * NSG, [[1, NSG], [0, 64]]), op=ALU.mult)
                S.I('dve', 'tensor_tensor', out=Sc, in0=Sc, in1=h1.v(0, [[64, NSG], [1, 64]]), op=ALU.add)
                S.dma('sp', o_hs_s[l, s_base:s_base + NSG, c * 128:(c + 1) * 128, :].rearrange("b p e -> p b e"),
                      Ssh.v(c * 512, [[64, NSG], [1, 64]]), is_output=True)
                S.I('dve', 'tensor_tensor', out=h2.v(0, [[64, NSG], [1, 64]]), in0=Sc,
                    in1=hsm.v(c * NSG, [[1, NSG], [0, 64]]), op=ALU.mult)
                bko = pbank()
                S.I('pe', 'matmul', out=PS(bko), lhsT=blk_f, rhs=h2.v(), start=True, stop=True)
                S.I('dve', 'tensor_tensor', out=h1.v(0, [[64, NSG], [1, 64]]), in0=PS(bko, 0, [[64, NSG], [1, 64]]),
                    in1=cst.v(K_I2, [[0, NSG], [1, 64]]), op=ALU.mult)
                osv = hs2.v(c * NSG, [[1, NSG]])
                S.I('dve', 'tensor_reduce', out=osv, in_=h1.v(0, [[64, NSG], [1, 64]]), axis=AX.X, op=ALU.add)
                sqv = hs2.v(32 + c * NSG, [[1, NSG]])
                S.I('dve', 'tensor_tensor', out=sqv, in0=osv, in1=osv, op=ALU.mult)
                bkn = pbank()
                S.I('pe', 'matmul', out=PS(bkn, 0, [[1, NSG]]), lhsT=blk_f, rhs=sqv, start=True, stop=True)
                rsv = hs2.v(64 + c * NSG, [[1, NSG]])
                S.I('act', 'activation', out=rsv, in_=PS(bkn, 0, [[1, NSG]]), func=AF.Ln, scale=1.0 / 64, bias=EPS)
                S.I('act', 'activation', out=rsv, in_=rsv, func=AF.Exp, scale=-0.5)
                S.I('dve', 'scalar_tensor_tensor', out=osv, in0=osv, scalar=col(cb + C_HNORMC), in1=rsv, op0=ALU.mult,
                    op1=ALU.mult)
                S.I('dve', 'tensor_tensor', out=CH(ymT, 8 + c, TPG, TC), in0=osv, in1=CH(gsT, c, TPG, TC), op=ALU.mult)

            HB = 8500 - 2 * 2624

            def hset(p):
                o = HB + p * 2624
                return dict(Kdtm=tmt(o, 256, bf=True), vtm=tmt(o + 256, 256, bf=True), attm=tmt(o + 512, 512, bf=True),
                            sqo=tmt(o + 1024, 512), onf=tmt(o + 1536, 512), on2=tmt(o + 2048, 256, bf=True),
                            Sbh=[tmt(o + 2304, 128, bf=True), tmt(o + 2432, 128, bf=True)], hq=tmt(o + 2560, 64))
            hsets = [hset(0), hset(1)]
            Sl = Shg.v(l * 256, [[1, 256]])

            def hg_tile(j):
                t0 = j * 128
                H = hsets[j % 2]
                Kdtm, vtm_, attm_, sqo, onf, on2, Sbh_, hq = (H['Kdtm'], H['vtm'], H['attm'], H['sqo'], H['onf'],
                                                             H['on2'], H['Sbh'], H['hq'])
                ss8 = hq.v(0, [[1, 8]])
                bkK = pbank()
                for c in range(4):
                    S.I('pe', 'transpose', out=PSB(bkK, c * 128, [[1, 128]]), in_=CH(KdT, c, t0, t0 + 128),
                        identity=identb)
                bkV = pbank()
                for c in range(4):
                    S.I('pe', 'transpose', out=PSB(bkV, c * 128, [[1, 128]]), in_=CH(vTb, c, t0, t0 + 128),
                        identity=identb)
                S.I('act', 'activation', out=Kdtm.v(), in_=PSB(bkK, 0, [[1, 512]]), func=AF.Copy)
                S.I('dve', 'tensor_copy', out=vtm_.v(), in_=PSB(bkV, 0, [[1, 512]]))
                yield
                b2 = pbank2()
                for hh in range(2):
                    for c in range(4):
                        S.I('pe', 'matmul', out=PS(b2[hh], c * 128, [[1, 128]]),
                            lhsT=BkT.v(c * TC + t0, [[1, 128]], p0=hh * 64, n=64, sl=BkT.csl(c, c + 1)),
                            rhs=AqT.v(c * TC + t0, [[1, 128]], p0=hh * 64, n=64, sl=AqT.csl(c, c + 1)),
                            start=(c == 0), stop=(c == 3), skip_group_check=True)
                for hh in range(2):
                    S.I('dve', 'tensor_tensor', out=attm_.v(hh * 128, [[256, 4], [1, 128]]),
                        in0=PS(b2[hh], 0, [[128, 4], [1, 128]]), in1=cst.v(K_MASK2, [[0, 4], [1, 128]]), op=ALU.mult)
                bS = pbank2()
                for jj in range(2):
                    for h in range(8):
                        c, hh = h // 2, h % 2
                        S.I('pe', 'matmul', out=PS(bS[jj], c * 64, [[1, 64]], p0=hh * 64, n=64),
                            lhsT=Kdtm.v(h * 64, [[1, 64]], p0=jj * 64, n=64),
                            rhs=vtm_.v(h * 64, [[1, 64]], p0=jj * 64, n=64), start=(h < 2),
                            stop=(h >= 6), skip_group_check=True)
                for jj in range(2):
                    S.I('act', 'activation', out=Sbh_[jj].v(), in_=Sl, func=AF.Copy)
                    for c in range(4):
                        S.I('dve', 'scalar_tensor_tensor', out=Shg.v(l * 256 + c * 64, [[1, 64]]),
                            in0=Shg.v(l * 256 + c * 64, [[1, 64]]), scalar=Dd.v(c * 16 + 2 * j + jj, [[1, 1]]),
                            in1=PS(bS[jj], c * 64, [[1, 64]]), op0=ALU.mult, op1=ALU.add)
                yield
                bO = pbank2()
                for hh in range(2):
                    for c in range(4):
                        h = 2 * c + hh
                        S.I('pe', 'matmul', out=PS(bO[hh], c * 64, [[1, 64]]), lhsT=attm_.v(h * 128, [[1, 128]]),
                            rhs=vtm_.v(h * 64, [[1, 64]]), start=(c == 0), stop=False, skip_group_check=True)
                for hh in range(2):
                    for jj in range(2):
                        for c in range(4):
                            S.I('pe', 'matmul', out=PS(bO[hh], c * 64, [[1, 64]], p0=jj * 64, n=64),
                                lhsT=AqT.v(c * TC + t0 + jj * 64, [[1, 64]], p0=hh * 64, n=64, sl=AqT.csl(c, c + 1)),
                                rhs=Sbh_[jj].v(c * 64, [[1, 64]], p0=hh * 64, n=64), start=False,
                                stop=(jj == 1 and c == 3), skip_group_check=True)
                for hh in range(2):
                    S.I('act', 'activation', out=sqo.v(hh * 256, [[1, 256]]), in_=PS(bO[hh], 0, [[1, 256]]),
                        func=AF.Square)
                S.I('dve', 'tensor_reduce', out=ss8, in_=sqo.v(0, [[64, 8], [1, 64]]), axis=AX.X, op=ALU.add)
                S.I('act', 'activation', out=ss8, in_=ss8, func=AF.Ln, scale=1.0 / 64, bias=EPS)
                S.I('act', 'activation', out=ss8, in_=ss8, func=AF.Exp, scale=-0.5)
                yield
                for hh in range(2):
                    S.I('dve', 'tensor_tensor', out=onf.v(hh * 256, [[64, 4], [1, 64]]),
                        in0=PS(bO[hh], 0, [[64, 4], [1, 64]]), in1=hq.v(hh * 4, [[1, 4], [0, 64]]), op=ALU.mult)
                    S.I('dve', 'tensor_tensor', out=on2.v(hh * 64, [[128, 4], [1, 64]]),
                        in0=onf.v(hh * 256, [[64, 4], [1, 64]]), in1=rows.v(l * LR + R_HNORM, [[0, 4], [1, 64]]),
                        op=ALU.mult)
                bkT = pbank()
                for c in range(4):
                    S.I('pe', 'transpose', out=PSB(bkT, c * 128, [[1, 128]]), in_=on2.v(c * 128, [[1, 128]]),
                        identity=identb)
                S.I('dve', 'tensor_tensor', out=ymT.v(8 * TC + t0, [[TC, 4], [1, 128]], sl=(8, 12)),
                    in0=PSB(bkT, 0, [[128, 4], [1, 128]]), in1=gsT.v(t0, [[TC, 4], [1, 128]]), op=ALU.mult)

            interleave((hg_tile(j) for j in range(8)), width=2)
            if g == NG - 1:
                S.dma('sp', o_hs_p[l].rearrange("(c p) e -> p c e", p=128), Shg.v(l * 256, [[64, 4], [1, 64]]),
                      is_output=True)

            for nh in range(2):
                wa = wnext(); wb_ = wnext()
                for m in range(4):
                    mo = nh * 4 + m

                    def cons_res(si, a, n, bank, mo=mo):
                        S.I('dve', 'tensor_tensor', out=CH(hT, mo, a, a + n), in0=CH(hT, mo, a, a + n),
                            in1=PS(bank, 0, [[1, n]]), op=ALU.add)
                    bks = dense(wa, 8, m * 128, 128, ymT, None, kc0=0, first=True, last=False)
                    dense(wb_, 4, m * 128, 128, ymT, cons_res, kc0=8, first=False, last=True, banks=bks)

            norm(cb + C_NFFN, uT)
            for G in range(2):
                for q in range(4):
                    wu = wnext()
                    for m in range(4):
                        zc = q * 4 + m

                        def cons_z(si, a, n, bank, zc=zc):
                            t = big(8, 10)
                            S.I('act', 'activation', out=t.v(a, [[1, n]]), in_=PS(bank, 0, [[1, n]]), func=AF.Relu)
                            S.I('pool', 'tensor_tensor', out=CH(zT, zc, a, a + n), in0=t.v(a, [[1, n]]),
                                in1=t.v(a, [[1, n]]), op=ALU.mult)
                        dense(wu, 8, m * 128, 128, uT, cons_z)
                for nh in range(2):
                    wa = wnext(); wb_ = wnext()
                    for m in range(4):
                        mo = nh * 4 + m

                        def cons_res(si, a, n, bank, mo=mo):
                            S.I('dve', 'tensor_tensor', out=CH(hT, mo, a, a + n), in0=CH(hT, mo, a, a + n),
                                in1=PS(bank, 0, [[1, n]]), op=ALU.add)
                        bks = dense(wa, 8, m * 128, 128, zT, None, kc0=0, first=True, last=False)
                        dense(wb_, 8, m * 128, 128, zT, cons_res, kc0=8, first=False, last=True, banks=bks)

            norm(cb + C_NPLE, uT)
            gts = []
            for nh in range(2):
                wg_ = wnext()
                for m in range(4):
                    gt = bigs[nh * 4 + m]
                    gts.append(gt)

                    def cons_g(si, a, n, bank, gt=gt):
                        S.I('act', 'activation', out=gt.v(a, [[1, n]]), in_=PS(bank, 0, [[1, n]]), func=AF.Sigmoid)
                    dense(wg_, 8, m * 128, 128, uT, cons_g)
            for nh in range(2):
                wp_ = wnext()
                for m in range(4):
                    mo = nh * 4 + m
                    gt = gts[mo]

                    def cons_p(si, a, n, bank, mo=mo, gt=gt):
                        S.I('dve', 'tensor_tensor', out=gt.v(a, [[1, n]]), in0=gt.v(a, [[1, n]]),
                            in1=PS(bank, 0, [[1, n]]), op=ALU.mult)
                        S.I('dve', 'tensor_tensor', out=CH(hT, mo, a, a + n), in0=CH(hT, mo, a, a + n),
                            in1=gt.v(a, [[1, n]]), op=ALU.add)
                    dense(wp_, 2, m * 128, 128, pT, cons_p)

        norm(C_NFIN, hT, in_place=True)
        for i in range(8):
            yt = bigs[i] if i != 8 else xin[1]
            for hf in range(2):
                bk = pbank()
                for c in range(4):
                    S.I('pe', 'transpose', out=PS(bk, c * 128, [[1, 128]]),
                        in_=CH(hT, hf * 4 + c, i * 128, (i + 1) * 128), identity=ident)
                if hf == 0:
                    S.I('act', 'activation', out=yt.v(0, [[1, 512]]), in_=PS(bk), func=AF.Copy)
                else:
                    S.I('dve', 'tensor_copy', out=yt.v(512, [[1, 512]]), in_=PS(bk))
            S.dma('sp', y_p[t_base + i * 128: t_base + (i + 1) * 128, :], yt.v(0, [[1, 1024]]), is_output=True)
        yt = xin[0]
        for hf in range(2):
            bk = pbank()
            for c in range(4):
                S.I('pe', 'transpose', out=PS(bk, c * 128, [[1, 128]], n=NSG),
                    in_=CH(hT, hf * 4 + c, TPG, TC), identity=ident)
            S.I('act', 'activation', out=yt.v(hf * 512, [[1, 512]], n=NSG), in_=PS(bk, 0, [[1, 512]], n=NSG),
                func=AF.Copy)
        S.dma('sp', y_s[s_base:s_base + NSG, :], yt.v(0, [[1, 1024]], n=NSG), is_output=True)

    S.finish()
    S.emit(stack)
    stack.close()
    return nc


def make_consts():
    c = np.zeros((128, NCONST), np.float32)
    idx = np.arange(128)
    c[:, K_ID:K_ID + 128] = np.eye(128, dtype=np.float32)
    c[:, K_ULE:K_ULE + 128] = (idx[:, None] <= idx[None, :]).astype(np.float32)
    c[:, K_ONES:K_ONES + 128] = 1.0
    c[:, K_NEGS:K_NEGS + 128] = (idx[:, None] > idx[None, :]).astype(np.float32)
    c[:, K_MASK2:K_MASK2 + 128] = ((idx[:, None] // 64 == idx[None, :] // 64) & (idx[:, None] <= idx[None, :])).astype(np.float32)
    c[:, K_I2:K_I2 + 64] = (idx[:, None] % 64 == np.arange(64)[None, :]).astype(np.float32)
    c[:, K_BLK:K_BLK + 128] = (idx[:, None] // 64 == idx[None, :] // 64).astype(np.float32)
    c[:, K_RMASK:K_RMASK + 1024] = (np.arange(1024) % 64 != 0).astype(np.float32)[None, :]
    c[:8, K_E8:K_E8 + 512] = (np.arange(8)[:, None] == (np.arange(512) // 64)[None, :]).astype(np.float32)
    return c


_NC_CACHE = {}


def kernel(**inp):
    f = lambda a: np.ascontiguousarray(np.asarray(a, dtype=np.float32))
    x_prompt = f(inp['x_prompt']); x_sample = f(inp['x_sample'])
    cols = np.zeros((128, NCOLS), np.float32)
    rows = np.zeros((128, NROWS), np.float32)

    def colv(v):
        v = f(v)
        return v.reshape(-1, 128).T

    for l in range(DEPTH):
        b = l * LC
        cols[:, b + C_NMIX:b + C_NMIX + 8] = colv(inp['norm_mix'][l])
        cols[:, b + C_NFFN:b + C_NFFN + 8] = colv(inp['norm_ffn'][l])
        cols[:, b + C_NPLE:b + C_NPLE + 8] = colv(inp['norm_ple'][l])
        caw = f(inp['conv_a_w'][l])
        for c in range(4):
            for k in range(4):
                cols[:, b + C_CAW + c * 4 + k] = caw[k, c * 128:(c + 1) * 128]
        cols[:, b + C_CAB:b + C_CAB + 4] = colv(inp['conv_a_b'][l])
        cols[:, b + C_RBA:b + C_RBA + 4] = colv(inp['rg_ba'][l])
        cols[:, b + C_RBX:b + C_RBX + 4] = colv(inp['rg_bx'][l])
        cols[:, b + C_LAM:b + C_LAM + 4] = colv(inp['rg_lambda'][l])
        cbw = f(inp['conv_b_w'][l])
        for c in range(6):
            for k in range(4):
                cols[:, b + C_CBW + c * 4 + k] = cbw[k, c * 128:(c + 1) * 128]
        cols[:, b + C_CBB:b + C_CBB + 6] = colv(inp['conv_b_b'][l])
        cols[:8, b + C_DTB] = f(inp['ssd_dt_bias'][l])
        cols[:, b + C_DEXP:b + C_DEXP + 4] = colv(np.repeat(f(inp['ssd_d'][l]), 64))
        cols[:, b + C_ALEXP:b + C_ALEXP + 4] = colv(np.repeat(f(inp['ssd_a_log'][l]), 64))
        cols[:, b + C_SNORMC:b + C_SNORMC + 4] = colv(inp['ssd_norm'][l])
        cols[:, b + C_HNORMC] = np.tile(f(inp['hg_norm'][l]), 2)
        r = l * LR
        rows[:, r + R_SNORM:r + R_SNORM + 512] = f(inp['ssd_norm'][l])[None, :]
        rows[:, r + R_HNORM:r + R_HNORM + 64] = f(inp['hg_norm'][l])[None, :]
        rows[:, r + R_SD:r + R_SD + 8] = f(inp['ssd_d'][l])[None, :]
        rows[:, r + R_ALOG:r + R_ALOG + 8] = f(inp['ssd_a_log'][l])[None, :]
    cols[:, C_NFIN:C_NFIN + 8] = colv(inp['norm_final'])
    cols[:, C_HLB0:C_HLB0 + 4] = colv(inp['hg_lower_bounds'][0])
    cols[:, C_HLB1:C_HLB1 + 4] = colv(inp['hg_lower_bounds'][1])
    rgw = np.zeros((DEPTH, 2, 4, 128, 128), np.float32)
    for l in range(DEPTH):
        for wi, nm in enumerate(('rg_wa', 'rg_wx')):
            w = f(inp[nm][l])
            for h in range(8):
                c, hh = h // 2, h % 2
                rgw[l, wi, c, hh * 64:(hh + 1) * 64, hh * 64:(hh + 1) * 64] = w[h]
    rgw = rgw.reshape(16, 128, 128)
    cst = make_consts()

    if 'nc' not in _NC_CACHE:
        _NC_CACHE['nc'] = build_program()
    nc = _NC_CACHE['nc']

    shared = dict(w_in=f(inp['w_in']), w_out=f(inp['w_out']), w_up=f(inp['w_up']), w_down=f(inp['w_down']),
                  w_pg=f(inp['w_ple_gate']), w_pp=f(inp['w_ple_proj']), rgw=rgw, cols=cols, rows=rows, cst=cst)
    in_maps = []
    for c in range(NCORES):
        sl = slice(c * NSAMP, (c + 1) * NSAMP)
        m = dict(shared)
        m['xp'] = f(x_prompt[c]); m['xs'] = f(x_sample[sl, 0])
        m['pp'] = f(inp['p_prompt'][:, c]); m['psm'] = f(inp['p_sample'][:, sl, 0])
        m['st_rc'] = f(inp['state_rg_conv'][:, sl]); m['st_rh'] = f(inp['state_rg_h'][:, sl])
        m['st_sc'] = f(inp['state_ssd_conv'][:, sl])
        m['st_ss'] = f(inp['state_ssd'][:, sl]).reshape(DEPTH, NSAMP, 512, 128)
        m['st_hs'] = f(inp['state_hgrn'][:, sl]).reshape(DEPTH, NSAMP, 512, 64)
        in_maps.append(m)
    res = run_bass_kernel_spmd(nc, in_maps, core_ids=list(range(NCORES)))
    R = res.results

    def cat(name, axis):
        return np.concatenate([np.asarray(r[name]) for r in R], axis=axis)

    def stk(name):
        return np.stack([np.asarray(r[name]) for r in R], axis=1)

    y_prompt = np.stack([np.asarray(r['y_p']) for r in R], 0)
    y_sample = cat('y_s', 0).reshape(NCORES * NSAMP, 1, D)
    rc_p = stk('o_rc_p'); rh_p = stk('o_rh_p'); sc_p = stk('o_sc_p')
    ss_p = stk('o_ss_p').reshape(DEPTH, NCORES, 8, 64, 128)
    hs_p = stk('o_hs_p').reshape(DEPTH, NCORES, 8, 64, 64)
    rc_s = cat('o_rc_s', 1); rh_s = cat('o_rh_s', 1); sc_s = cat('o_sc_s', 1)
    ss_s = cat('o_ss_s', 1).reshape(DEPTH, NCORES * NSAMP, 8, 64, 128)
    hs_s = cat('o_hs_s', 1).reshape(DEPTH, NCORES * NSAMP, 8, 64, 64)
    outs = (y_prompt, y_sample, rc_p, rh_p, sc_p, ss_p, hs_p, rc_s, rh_s, sc_s, ss_s, hs_s)
    return tuple(np.ascontiguousarray(o, dtype=np.float32) for o in outs)
```

```python
import contextlib
import numpy as np
import concourse.bass as bass
import concourse.mybir as mybir
from concourse.bass_utils import run_bass_kernel_spmd

F32 = mybir.dt.float32
BF16 = mybir.dt.bfloat16
ALU = mybir.AluOpType
AF = mybir.ActivationFunctionType
AX = mybir.AxisListType

NCORES = 8
D = 1024
SEQ = 2048
DEPTH = 2
NSAMP = 16
TPG = 1024
NSG = 8
NG = 2
TC = TPG + NSG
SGS = [(0, 512), (512, 512), (1024, NSG)]
EPS = 1e-6
DIN = 4360
DFF = 4096
BIGNEG = -30000.0

LC = 104
C_NMIX, C_NFFN, C_NPLE, C_CAW, C_CAB, C_RBA, C_RBX, C_LAM, C_CBW, C_CBB, C_DTB, C_DEXP, C_ALEXP = \
    0, 8, 16, 24, 40, 44, 48, 52, 56, 80, 86, 88, 92
C_SNORMC, C_HNORMC = 96, 100
C_NFIN, C_HLB0, C_HLB1 = 208, 216, 220
NCOLS = 224
LR = 592
R_SNORM, R_HNORM, R_SD, R_ALOG = 0, 512, 576, 584
NROWS = LR * 2
K_ID, K_ULE, K_ONES, K_NEGS, K_MASK2, K_I2, K_BLK, K_RMASK, K_E8 = 0, 128, 256, 384, 512, 640, 704, 832, 1856
NCONST = 2368

ENG = ['pe', 'act', 'dve', 'pool', 'sp']
BIG = 1 << 30


class Tile:
    def __init__(self, h, name, shape, base=0, ps=None, dsl=None, sloff=0.0, slscale=1.0, width=None):
        self.h = h
        self.name = name
        self.shape = list(shape)
        self.ps = ps if ps is not None else int(np.prod(shape[1:]))
        self.base = base
        self.dsl = dsl if dsl is not None else (0, BIG)
        self.sloff = sloff
        self.slscale = slscale
        self.width = width if width is not None else int(np.prod(shape[1:]))

    def v(self, off=0, dims=None, p0=0, n=None, sl=None):
        if n is None:
            n = self.shape[0] - p0
        if dims is None:
            dims = [[1, self.width - off]]
        ap = bass.AP(self.h, p0 * self.ps + self.base + off, [[self.ps, n]] + [list(d) for d in dims])
        return View(ap, self, sl if sl is not None else self.dsl)

    def csl(self, c0, c1):
        return (self.sloff + c0 * self.slscale, self.sloff + c1 * self.slscale)


class View:
    def __init__(self, ap, tile, sl):
        self.ap = ap
        self.tile = tile
        self.sl = sl


class Sched:
    def __init__(self, nc):
        self.nc = nc
        self.streams = {e: [] for e in ENG}
        self.count = {e: 0 for e in ENG}
        self.waited = {e: {} for e in ENG}
        self.recs = {}
        self.ndma = 0
        self.KD = 24
        self.dma_tokens = []
        self.out_tokens = []
        self.alias = {}
        self.nsw = 0
        self.NSWS = 8

    def _deps(self, eng, reads, writes):
        need = {}

        def add(tok, raw):
            sem, val, src = tok
            if src == eng and sem == eng:
                if eng == 'pe' or eng == 'sp':
                    return
                if not raw:
                    return
            if need.get(sem, 0) < val:
                need[sem] = val

        for v in reads:
            r = self.recs.setdefault(v.tile.name, {'w': [], 'r': []})
            for (lo, hi, tok) in r['w']:
                if lo < v.sl[1] and v.sl[0] < hi:
                    add(tok, True)
            if v.tile.name.startswith('ps'):
                for (lo, hi, tok) in r['r']:
                    if lo < v.sl[1] and v.sl[0] < hi and tok[2] != eng:
                        add(tok, False)
        for v in writes:
            r = self.recs.setdefault(v.tile.name, {'w': [], 'r': []})
            for (lo, hi, tok) in r['w']:
                if lo < v.sl[1] and v.sl[0] < hi:
                    add(tok, False)
            for (lo, hi, tok) in r['r']:
                if lo < v.sl[1] and v.sl[0] < hi:
                    add(tok, False)
        return need

    def _record(self, tok, reads, writes):
        for v in reads:
            r = self.recs[v.tile.name]
            r['r'] = [x for x in r['r'] if not (x[2][0] == tok[0] and v.sl[0] <= x[0] and x[1] <= v.sl[1])]
            r['r'].append((v.sl[0], v.sl[1], tok))
        for v in writes:
            r = self.recs[v.tile.name]
            r['w'] = [x for x in r['w'] if not (v.sl[0] <= x[0] and x[1] <= v.sl[1])]
            r['r'] = [x for x in r['r'] if not (v.sl[0] <= x[0] and x[1] <= v.sl[1])]
            r['w'].append((v.sl[0], v.sl[1], tok))

    def _emit(self, eng, need, fn, inc):
        waits = []
        for sem, val in need.items():
            if self.waited[eng].get(sem, 0) < val:
                self.waited[eng][sem] = val
                waits.append((sem, val))
        self.streams[eng].append((waits, fn, inc, None))

    def I(self, eng, method, *args, **kw):
        reads, writes = [], []
        for k, a in kw.items():
            if isinstance(a, View):
                (writes if k in ('out', 'accum_out', 'ap') else reads).append(a)
        if kw.pop('_rmw', False):
            pass
        need = self._deps(eng, reads, writes)
        self.count[eng] += 1
        tok = (eng, self.count[eng], eng)
        kw2 = {k: (a.ap if isinstance(a, View) else a) for k, a in kw.items()}

        def fn(e, method=method, args=args, kw2=kw2):
            return getattr(e, method)(*args, **kw2)

        self._emit(eng, need, fn, (eng, 1))
        self._record(tok, reads, writes)
        return tok

    def dma(self, eng, out, in_, is_output=False, **kw):
        reads = [in_] if isinstance(in_, View) else []
        writes = [out] if isinstance(out, View) else []
        need = self._deps(eng, reads, writes)
        n = self.ndma
        self.ndma += 1
        sem = 'dma%d' % (n % self.KD)
        val = 16 * (n // self.KD + 1)
        if n >= self.KD:
            ptok = self.dma_tokens[n - self.KD]
            if need.get(ptok[0], 0) < ptok[1]:
                need[ptok[0]] = ptok[1]
        tok = (sem, val, eng)
        self.dma_tokens.append(tok)
        o = out.ap if isinstance(out, View) else out
        i = in_.ap if isinstance(in_, View) else in_

        def fn(e, o=o, i=i, kw=kw):
            return e.dma_start(out=o, in_=i, **kw)

        self._emit(eng, need, fn, (sem, 16))
        self._record(tok, reads, writes)
        if is_output:
            self.out_tokens.append(tok)
        return tok

    def dma_sw(self, eng, out, in_, slot):
        reads = [in_] if isinstance(in_, View) else []
        writes = [out] if isinstance(out, View) else []
        need = self._deps(eng, reads, writes)
        key = 'sw%d' % self.nsw
        self.nsw += 1
        real = 'wsem%d' % slot
        self.alias[key] = real
        tok = (key, 16, eng)
        o = out.ap if isinstance(out, View) else out
        i = in_.ap if isinstance(in_, View) else in_

        def fn(e, o=o, i=i):
            return e.dma_start(out=o, in_=i)

        waits = []
        for sem, val in need.items():
            if self.waited[eng].get(sem, 0) < val:
                self.waited[eng][sem] = val
                waits.append((sem, val))
        self.streams[eng].append((waits, fn, (real, 16), real))
        self._record(tok, reads, writes)
        return tok

    def finish(self):
        need = {}
        for (sem, val, _) in self.out_tokens:
            need[sem] = max(need.get(sem, 0), val)
        waits = [(s, v) for s, v in need.items() if self.waited['sp'].get(s, 0) < v]
        self.streams['sp'].append((waits, None, None, None))

    def emit(self, stack):
        nc = self.nc
        sems = {}
        for e in ENG:
            sems[e] = stack.enter_context(nc.semaphore("s_" + e))
        for i in range(self.KD):
            sems['dma%d' % i] = stack.enter_context(nc.semaphore("s_dma%d" % i))
        for i in range(self.NSWS):
            sems['wsem%d' % i] = stack.enter_context(nc.semaphore("s_wsem%d" % i))
        alias = self.alias
        block = stack.enter_context(nc.Block())
        amap = {'pe': 'tensor', 'act': 'scalar', 'dve': 'vector', 'pool': 'gpsimd', 'sp': 'sync'}
        for e in ENG:
            stream = self.streams[e]

            def body(engine, stream=stream):
                for (waits, fn, inc, pre) in stream:
                    for (s, v) in waits:
                        engine.wait_ge(sems[alias.get(s, s)], v)
                    if pre is not None:
                        engine.sem_clear(sems[pre])
                    if fn is not None:
                        ins = fn(engine)
                        ins.then_inc(sems[inc[0]], inc[1])

            getattr(block, amap[e])(body)


def interleave(gens, width=2):
    it = iter(gens)
    active = []
    first = next(it, None)
    if first is not None:
        active.append(first)
    while active:
        for gnr in list(active):
            try:
                next(gnr)
            except StopIteration:
                active.remove(gnr)
        if len(active) < width:
            nxt = next(it, None)
            if nxt is not None:
                active.append(nxt)


def build_program():
    nc = bass.Bass("TRN2", target_bir_lowering=False)
    stack = contextlib.ExitStack()
    S = Sched(nc)

    def din(name, shape):
        return nc.dram_tensor(name, list(shape), F32, kind="ExternalInput")

    def dout(name, shape):
        return nc.dram_tensor(name, list(shape), F32, kind="ExternalOutput")

    xp = din("xp", [SEQ, D]); xs = din("xs", [NSAMP, D])
    pp = din("pp", [DEPTH, SEQ, 256]); psm = din("psm", [DEPTH, NSAMP, 256])
    st_rc = din("st_rc", [DEPTH, NSAMP, 3, 512]); st_rh = din("st_rh", [DEPTH, NSAMP, 512])
    st_sc = din("st_sc", [DEPTH, NSAMP, 3, 768]); st_ss = din("st_ss", [DEPTH, NSAMP, 512, 128])
    st_hs = din("st_hs", [DEPTH, NSAMP, 512, 64])
    w_in = din("w_in", [DEPTH, D, DIN]); w_out = din("w_out", [DEPTH, 1536, D])
    w_up = din("w_up", [DEPTH, D, DFF]); w_down = din("w_down", [DEPTH, DFF, D])
    w_pg = din("w_pg", [DEPTH, D, D]); w_pp = din("w_pp", [DEPTH, 256, D])
    rgw = din("rgw", [16, 128, 128])
    cols_d = din("cols", [128, NCOLS]); rows_d = din("rows", [128, NROWS]); cst_d = din("cst", [128, NCONST])

    y_p = dout("y_p", [SEQ, D]); y_s = dout("y_s", [NSAMP, D])
    o_rc_p = dout("o_rc_p", [DEPTH, 3, 512]); o_rh_p = dout("o_rh_p", [DEPTH, 512])
    o_sc_p = dout("o_sc_p", [DEPTH, 3, 768]); o_ss_p = dout("o_ss_p", [DEPTH, 512, 128])
    o_hs_p = dout("o_hs_p", [DEPTH, 512, 64])
    o_rc_s = dout("o_rc_s", [DEPTH, NSAMP, 3, 512]); o_rh_s = dout("o_rh_s", [DEPTH, NSAMP, 512])
    o_sc_s = dout("o_sc_s", [DEPTH, NSAMP, 3, 768]); o_ss_s = dout("o_ss_s", [DEPTH, NSAMP, 512, 128])
    o_hs_s = dout("o_hs_s", [DEPTH, NSAMP, 512, 64])

    def sb(name, shape, dt=F32):
        h = stack.enter_context(nc.sbuf_tensor(name, list(shape), dt))
        return Tile(h, name, shape)

    NWB = 3
    cst = sb("cstT", [128, NCONST]); cols = sb("colsT", [128, NCOLS]); rows = sb("rowsT", [128, NROWS])
    cbf = sb("cbf", [128, 640], BF16)
    der = sb("der", [128, 96])
    rgwb = sb("rgwb", [128, 16, 128], BF16)
    hT = sb("hT", [128, 8, TC]); uT = sb("uT", [128, 8, TC], BF16); ymT = sb("ymT", [128, 12, TC], BF16)
    wbs = [sb("wb%d" % i, [128, 8, 512], BF16) for i in range(NWB)]
    NBG = 10
    bigmem = sb("bigmem", [128, NBG * TC])
    bigs = [Tile(bigmem.h, "bigmem", [128, TC], base=i * TC, ps=NBG * TC, dsl=(i, i + 1), width=TC)
            for i in range(NBG)]
    xin = [bigs[9], sb("xin1", [128, 1024])]
    axs = sb("axs", [128, 4, NSG, 4]); cmp_ = sb("cmp", [128, 6, 24])
    rgc = sb("rgc", [128, DEPTH, 4, 3]); rgh = sb("rgh", [128, DEPTH, 4]); h0s = sb("h0s", [128, 4, NSG])
    pT = sb("pT", [128, 2, TC], BF16)
    zT = Tile(bigmem.h.bitcast(BF16), "bigmem", [128, 16, TC], ps=2 * NBG * TC, sloff=0.0, slscale=0.5,
              width=16 * TC)
    pss = []
    for i in range(4):
        h = stack.enter_context(nc.psum_tensor("ps%d" % i, [128, 1024], F32))
        pss.append(Tile(h, "ps%d" % i, [128, 1024]))
    psb = [Tile(t.h.bitcast(BF16), t.name, [128, 2048]) for t in pss]
    pctr = [0]
    ptouch = [0] * 8
    pclock = [0]

    please = [0] * 8

    def _touch(bank):
        t, hf = bank[0], bank[1]
        if len(bank) > 2 and please[t * 2 + hf] != bank[2]:
            raise RuntimeError("stale PSUM bank use: bank %d lease %d != %d" % (t * 2 + hf, bank[2], please[t * 2 + hf]))
        pclock[0] += 1
        ptouch[t * 2 + hf] = pclock[0]

    pheld = set()

    def phold(bank):
        pheld.add(bank[0] * 2 + bank[1])

    def prel(bank):
        pheld.discard(bank[0] * 2 + bank[1])

    def pbank():
        i = min((b for b in range(8) if b not in pheld), key=lambda b: ptouch[b])
        please[i] += 1
        bank = (i // 2, i % 2, please[i])
        _touch(bank)
        return bank

    def pbank2():
        t = min((k for k in range(4) if 2 * k not in pheld and 2 * k + 1 not in pheld),
                key=lambda k: max(ptouch[2 * k], ptouch[2 * k + 1]))
        please[2 * t] += 1; please[2 * t + 1] += 1
        b0 = (t, 0, please[2 * t]); b1 = (t, 1, please[2 * t + 1])
        _touch(b0); _touch(b1)
        return b0, b1

    def PS(bank, off=0, dims=None, p0=0, n=None):
        t, hf = bank[0], bank[1]
        _touch(bank)
        return pss[t].v(hf * 512 + off, dims if dims is not None else [[1, 512 - off]], p0=p0, n=n, sl=(hf, hf + 1))

    def PSB(bank, off=0, dims=None, p0=0, n=None):
        t, hf = bank[0], bank[1]
        _touch(bank)
        return psb[t].v(hf * 1024 + off, dims if dims is not None else [[1, 1024 - off]], p0=p0, n=n, sl=(hf, hf + 1))

    bctr = [0]

    def big(lo=0, hi=NBG):
        t = bigs[lo + bctr[0] % (hi - lo)]
        bctr[0] += 1
        return t

    def col(c, n=128):
        return cols.v(c, [[1, 1]], n=n)

    def dcol(c, n=128):
        return der.v(c, [[1, 1]], n=n)

    def CH(t, c, a=0, b=TC):
        W = t.shape[2]
        return t.v(c * W + a, [[1, b - a]], sl=t.csl(c, c + 1))

    ident = cst.v(K_ID, [[1, 128]])
    identb = cbf.v(0, [[1, 128]])
    onesb = cbf.v(256, [[1, 128]])
    ones_f = cst.v(K_ONES, [[1, 128]])
    ule_f = cst.v(K_ULE, [[1, 128]])

    S.dma('sp', cst.v(), cst_d.ap())
    S.dma('sp', cols.v(), cols_d.ap())
    S.dma('sp', rows.v(), rows_d.ap())
    S.dma('pool', rgwb.v(), rgw.ap().rearrange("a k m -> k a m"))
    S.I('act', 'activation', out=cbf.v(0, [[1, 384]]), in_=cst.v(0, [[1, 384]]), func=AF.Copy)
    S.I('act', 'activation', out=cbf.v(384, [[1, 128]]), in_=cst.v(K_BLK, [[1, 128]]), func=AF.Copy)
    S.I('act', 'activation', out=cbf.v(512, [[1, 128]]), in_=cst.v(K_NEGS, [[1, 128]]), func=AF.Copy)
    anegb = sb("anegb", [128, DEPTH, 8]); anegx = sb("anegx", [128, DEPTH, 4])
    TMW = 8500
    tm = sb("tm", [128, TMW])
    tm_bf = tm.h.bitcast(BF16)

    def tmt(off, n, bf=False):
        assert off + n <= TMW
        if bf:
            return Tile(tm_bf, "tm", [128, 2 * n], base=2 * off, ps=2 * TMW, dsl=(off, off + n), width=2 * n)
        return Tile(tm.h, "tm", [128, n], base=off, ps=TMW, dsl=(off, off + n), width=n)

    xabf = tmt(0, 516, bf=True)
    sqb = [tmt(520, 516, bf=True), tmt(1040, 516, bf=True)]
    rstd = tmt(1560, TC)
    ssc = sb("ssc", [128, DEPTH, 6, 3]); xbs = sb("xbs", [128, 6, NSG, 4])
    Sst = sb("Sst", [128, DEPTH, 512])
    Shg = sb("Shg", [128, DEPTH, 256]); hsm = sb("hsm", [128, 16, NSG]); Dd = sb("Dd", [128, 4, 16])
    S.I('dve', 'memset', ap=Shg.v(), constant=0.0)
    S.I('dve', 'memset', ap=ssc.v(), constant=0.0)
    S.I('dve', 'memset', ap=Sst.v(), constant=0.0)
    bigbf = bigmem.h.bitcast(BF16)

    def bigbf_tile(i0, nchunks):
        return Tile(bigbf, "bigmem", [128, nchunks, TC], base=2 * i0 * TC, ps=2 * NBG * TC, sloff=float(i0),
                    slscale=0.5, width=nchunks * TC, dsl=(i0, i0 + nchunks / 2.0))

    for l in range(DEPTH):
        lam = cols.v(l * LC + C_LAM, [[1, 4]])
        t1 = der.v(48, [[1, 4]])
        S.I('act', 'activation', out=t1, in_=lam, func=AF.Exp, scale=-1.0)
        S.I('act', 'activation', out=t1, in_=t1, func=AF.Ln, bias=1.0)
        S.I('dve', 'tensor_scalar', out=der.v(l * 16, [[1, 4]]), in0=t1, scalar1=-4.0, scalar2=None, op0=ALU.mult)
        S.I('dve', 'tensor_scalar', out=der.v(l * 16 + 4, [[1, 4]]), in0=t1, scalar1=-8.0, scalar2=None, op0=ALU.mult)
        S.I('dve', 'tensor_scalar', out=der.v(l * 16 + 8, [[1, 4]]), in0=cols.v(l * LC + C_RBA, [[1, 4]]), scalar1=0.5,
            scalar2=None, op0=ALU.mult)
        S.I('dve', 'tensor_scalar', out=der.v(l * 16 + 12, [[1, 4]]), in0=cols.v(l * LC + C_RBX, [[1, 4]]), scalar1=0.5,
            scalar2=None, op0=ALU.mult)
        S.I('act', 'activation', out=anegb.v(l * 8, [[1, 8]]), in_=rows.v(l * LR + R_ALOG, [[1, 8]]), func=AF.Exp)
        S.I('dve', 'tensor_scalar', out=anegb.v(l * 8, [[1, 8]]), in0=anegb.v(l * 8, [[1, 8]]), scalar1=-1.0,
            scalar2=None, op0=ALU.mult)
        S.I('act', 'activation', out=anegx.v(l * 4, [[1, 4]]), in_=cols.v(l * LC + C_ALEXP, [[1, 4]]), func=AF.Exp)
        S.I('dve', 'tensor_scalar', out=anegx.v(l * 4, [[1, 4]]), in0=anegx.v(l * 4, [[1, 4]]), scalar1=-1.0,
            scalar2=None, op0=ALU.mult)
    S.I('dve', 'memset', ap=der.v(32, [[1, 4]]), constant=0.0)
    S.I('dve', 'memset', ap=der.v(36, [[1, 4]]), constant=1.0)
    S.I('dve', 'tensor_tensor', out=der.v(52, [[1, 4]]), in0=cols.v(C_HLB1, [[1, 4]]), in1=cols.v(C_HLB0, [[1, 4]]),
        op=ALU.subtract)
    S.I('act', 'activation', out=der.v(40, [[1, 4]]), in_=der.v(52, [[1, 4]]), func=AF.Sigmoid)
    S.I('dve', 'tensor_scalar', out=der.v(44, [[1, 4]]), in0=der.v(40, [[1, 4]]), scalar1=-1.0, scalar2=1.0,
        op0=ALU.mult, op1=ALU.add)
    for l in range(DEPTH):
        S.I('dve', 'tensor_scalar', out=der.v(64 + l * 8, [[1, 4]]), in0=der.v(36 + l * 8, [[1, 4]]), scalar1=0.5,
            scalar2=None, op0=ALU.mult)
        S.I('dve', 'tensor_tensor', out=der.v(64 + l * 8 + 4, [[1, 4]]), in0=der.v(64 + l * 8, [[1, 4]]),
            in1=der.v(32 + l * 8, [[1, 4]]), op=ALU.add)
    S.I('dve', 'memset', ap=rgc.v(), constant=0.0)
    S.I('dve', 'memset', ap=rgh.v(), constant=0.0)

    wq = []

    def wpiece(dt_, l, r0, kc, c0, ncols):
        src = dt_[l, r0:r0 + kc * 128, c0:c0 + ncols].rearrange("(k p) n -> p k n", p=128)
        wq.append((src, kc, ncols))
        return len(wq) - 1

    wissued = [0]

    def wget(i):
        while wissued[0] < len(wq) and wissued[0] <= i + NWB - 2:
            j = wissued[0]
            src, kc, ncols = wq[j]
            t = wbs[j % NWB]
            S.dma('pool', t.v(0, [[512, kc], [1, ncols]]), src)
            wissued[0] += 1
        return wbs[i % NWB]

    plan = []
    for g in range(NG):
        for l in range(DEPTH):
            for c0, n in [(0, 512), (512, 512), (1024, 512), (1536, 512), (2048, 256), (2304, 8),
                          (2824, 512), (2312, 512), (3336, 512), (3848, 512)]:
                plan.append(wpiece(w_in, l, 0, 8, c0, n))
            for nh in range(2):
                plan.append(wpiece(w_out, l, 0, 8, nh * 512, 512))
                plan.append(wpiece(w_out, l, 1024, 4, nh * 512, 512))
            for G in range(2):
                for q in range(4):
                    plan.append(wpiece(w_up, l, 0, 8, G * 2048 + q * 512, 512))
                for nh in range(2):
                    for rb in range(2):
                        plan.append(wpiece(w_down, l, G * 2048 + rb * 1024, 8, nh * 512, 512))
            for nh in range(2):
                plan.append(wpiece(w_pg, l, 0, 8, nh * 512, 512))
            for nh in range(2):
                plan.append(wpiece(w_pp, l, 0, 2, nh * 512, 512))
    wptr = [0]

    def wnext():
        i = wptr[0]
        wptr[0] += 1
        return wget(i)

    def WV(t, kc, m0, M):
        return t.v(kc * 512 + m0, [[1, M]])

    def transpose_in(src_rows_ap, nrows, dst_tile, dst_cols0, nchunks, xt, xoff=0, dma=True, flip=0):
        if dma:
            S.dma('sp', xt.v(xoff, [[1, nchunks * 128]], n=nrows), src_rows_ap)
        for c0 in range(0, nchunks, 4):
            ncc = min(4, nchunks - c0)
            bk = pbank()
            for c in range(ncc):
                S.I('pe', 'transpose', out=PS(bk, c * 128, [[1, nrows]]),
                    in_=xt.v(xoff + (c0 + c) * 128, [[1, 128]], n=nrows), identity=cst.v(K_ID, [[1, nrows]], n=nrows))
            W = dst_tile.shape[2]
            eng = 'act' if (c0 // 4 + flip) % 2 == 0 else 'dve'
            outv = dst_tile.v(c0 * W + dst_cols0, [[W, ncc], [1, nrows]], sl=(c0, c0 + ncc))
            inv = PS(bk, 0, [[128, ncc], [1, nrows]])
            if eng == 'act':
                S.I('act', 'activation', out=outv, in_=inv, func=AF.Copy)
            else:
                S.I('dve', 'tensor_copy', out=outv, in_=inv)

    def norm(gcol0, out_tile, nchunks=8, src=None, in_place=False, dim=1024.0, tmpn=None):
        src = src or hT
        bks = [pbank() for _ in SGS]
        for c in range(nchunks):
            sq = sqb[c % 2]
            S.I('act', 'activation', out=sq.v(), in_=CH(src, c), func=AF.Square)
            for si, (a, n) in enumerate(SGS):
                S.I('pe', 'matmul', out=PS(bks[si], 0, [[1, n]]), lhsT=onesb, rhs=sq.v(a, [[1, n]]),
                    start=(c == 0), stop=(c == nchunks - 1))
        for si, (a, n) in enumerate(SGS):
            S.I('act', 'activation', out=rstd.v(a, [[1, n]]), in_=PS(bks[si], 0, [[1, n]]), func=AF.Ln,
                scale=1.0 / dim, bias=EPS)
        S.I('act', 'activation', out=rstd.v(), in_=rstd.v(), func=AF.Exp, scale=-0.5)
        for c in range(nchunks):
            if c % 3 == 2 and nchunks == 8 and tmpn is not None:
                S.I('act', 'activation', out=tmpn.v(), in_=CH(src, c), func=AF.Copy, scale=col(gcol0 + c))
                S.I('pool', 'tensor_tensor', out=CH(out_tile, c), in0=tmpn.v(), in1=rstd.v(), op=ALU.mult)
            else:
                S.I('dve', 'scalar_tensor_tensor', out=CH(out_tile, c), in0=CH(src, c), scalar=col(gcol0 + c),
                    in1=rstd.v(), op0=ALU.mult, op1=ALU.mult)

    def dense(wt, nk, m0, M, rhs_tile, consume, kc0=0, first=True, last=True, banks=None, sgs=None):
        sgs = sgs or SGS
        bks = banks or [pbank() for _ in sgs]
        for kc in range(nk):
            for si, (a, n) in enumerate(sgs):
                S.I('pe', 'matmul', out=PS(bks[si], 0, [[1, n]], n=M), lhsT=WV(wt, kc, m0, M),
                    rhs=CH(rhs_tile, kc0 + kc, a, a + n), start=(first and kc == 0), stop=(last and kc == nk - 1))
        if last and consume is not None:
            for si, (a, n) in enumerate(sgs):
                consume(si, a, n, bks[si])
        return bks

    for g in range(NG):
        t_base = g * TPG
        s_base = g * NSG
        for i in range(8):
            S.dma('sp', bigs[i].v(0, [[1, 1024]]), xp[t_base + i * 128: t_base + (i + 1) * 128, :])
        for i in range(8):
            transpose_in(None, 128, hT, i * 128, 8, bigs[i], dma=False)
        transpose_in(xs[s_base:s_base + NSG, :], NSG, hT, TPG, 8, bigs[8])

        for l in range(DEPTH):
            cb = l * LC
            pbufs = [xin[1], bigs[9]]
            for hb in range(2):
                S.dma('sp', pbufs[hb].v(0, [[256, 4], [1, 256]]),
                      pp[l, t_base + hb * 512: t_base + (hb + 1) * 512, :].rearrange("(i p) d -> p i d", p=128))
            for i in range(8):
                transpose_in(None, 128, pT, i * 128, 2, pbufs[i // 4], xoff=(i % 4) * 256, dma=False, flip=i % 2)
            transpose_in(psm[l, s_base:s_base + NSG, :], NSG, pT, TPG, 2, bigs[8])

            norm(cb + C_NMIX, uT)

            wax = wnext()
            S.I('dve', 'tensor_copy', out=bigmem.v(0, [[TC, 4], [1, 3]], sl=(0, 4)), in_=rgc.v(l * 12, [[3, 4], [1, 3]]))
            xt = xin[1]
            S.dma('sp', xt.v(0, [[1, 512]], n=3 * NSG),
                  st_rc[l, s_base:s_base + NSG].rearrange("b k c -> (b k) c"))
            bk = pbank()
            for c in range(4):
                S.I('pe', 'transpose', out=PS(bk, c * 32, [[1, 24]]), in_=xt.v(c * 128, [[1, 128]], n=24),
                    identity=cst.v(K_ID, [[1, 24]], n=24))
            S.I('dve', 'tensor_copy', out=axs.v(0, [[NSG * 4, 4], [4, NSG], [1, 3]]),
                in_=PS(bk, 0, [[32, 4], [3, NSG], [1, 3]]))
            S.dma('sp', xt.v(0, [[1, 512]], n=NSG), st_rh[l, s_base:s_base + NSG, :])
            bk = pbank()
            for c in range(4):
                S.I('pe', 'transpose', out=PS(bk, c * 8, [[1, NSG]]), in_=xt.v(c * 128, [[1, 128]], n=NSG),
                    identity=cst.v(K_ID, [[1, NSG]], n=NSG))
            S.I('dve', 'tensor_copy', out=h0s.v(), in_=PS(bk, 0, [[1, 4 * NSG]]))

            for c in range(4):
                def cons_ax(si, a, n, bank, c=c):
                    if si < 2:
                        S.I('act', 'activation', out=bigs[c].v(3 + a, [[1, n]]),
                            in_=PS(bank, 0, [[1, n]]), func=AF.Copy)
                    else:
                        S.I('act', 'activation', out=axs.v(c * NSG * 4 + 3, [[4, NSG]]), in_=PS(bank, 0, [[1, n]]),
                            func=AF.Copy)
                dense(wax, 8, c * 128, 128, uT, cons_ax)
            S.I('dve', 'tensor_copy', out=rgc.v(l * 12, [[3, 4], [1, 3]]), in_=bigmem.v(TPG, [[TC, 4], [1, 3]], sl=(0, 4)))
            wag = wnext()
            def HV(t, a, n):
                lo = t.dsl[0] + (0.0 if a < 512 else 0.5)
                hi = t.dsl[0] + (0.5 if a + n <= 512 else 1.0)
                return t.v(a, [[1, n]], sl=(lo, hi))

            def rg_unit(c, hf):
                xa, r_, i_, a_, m_, hh = bigs[4:10]
                a0, nn = (0, 512) if hf == 0 else (512, 512 + NSG)
                npr = 512
                sgs_h = [SGS[0]] if hf == 0 else [SGS[1], SGS[2]]
                S.I('dve', 'tensor_scalar', out=HV(xa, a0, npr), in0=bigs[c].v(a0, [[1, npr]]),
                    scalar1=col(cb + C_CAW + c * 4), scalar2=col(cb + C_CAB + c), op0=ALU.mult, op1=ALU.add)
                for k in range(1, 4):
                    S.I('dve', 'scalar_tensor_tensor', out=HV(xa, a0, npr),
                        in0=bigs[c].v(a0 + k, [[1, npr]]), scalar=col(cb + C_CAW + c * 4 + k),
                        in1=HV(xa, a0, npr), op0=ALU.mult, op1=ALU.add)
                if hf == 1:
                    S.I('dve', 'tensor_scalar', out=HV(xa, TPG, NSG), in0=axs.v(c * NSG * 4, [[4, NSG]]),
                        scalar1=col(cb + C_CAW + c * 4), scalar2=col(cb + C_CAB + c), op0=ALU.mult, op1=ALU.add)
                    for k in range(1, 4):
                        S.I('dve', 'scalar_tensor_tensor', out=HV(xa, TPG, NSG),
                            in0=axs.v(c * NSG * 4 + k, [[4, NSG]]), scalar=col(cb + C_CAW + c * 4 + k),
                            in1=HV(xa, TPG, NSG), op0=ALU.mult, op1=ALU.add)
                xab = xabf.v(a0, [[1, nn]], sl=(xabf.dsl[0] + hf * 258, xabf.dsl[0] + (hf + 1) * 258))
                S.I('act', 'activation', out=xab, in_=HV(xa, a0, nn), func=AF.Copy)
                yield
                for which, dst, bcol in ((0, r_, C_RBA), (1, i_, C_RBX)):
                    for (a, n) in sgs_h:
                        bk = pbank()
                        S.I('pe', 'matmul', out=PS(bk, 0, [[1, n]]),
                            lhsT=rgwb.v(((l * 2 + which) * 4 + c) * 128, [[1, 128]]),
                            rhs=xabf.v(a, [[1, n]], sl=(xabf.dsl[0] + hf * 258, xabf.dsl[0] + (hf + 1) * 258)),
                            start=True, stop=True)
                        S.I('act', 'activation', out=HV(dst, a, n), in_=PS(bk, 0, [[1, n]]), func=AF.Tanh,
                            scale=0.5, bias=dcol(l * 16 + (8 if which == 0 else 12) + c))
                S.I('act', 'activation', out=HV(a_, a0, nn), in_=HV(r_, a0, nn), func=AF.Exp,
                    scale=dcol(l * 16 + c), bias=dcol(l * 16 + c))
                S.I('act', 'activation', out=HV(m_, a0, nn), in_=HV(r_, a0, nn), func=AF.Exp,
                    scale=dcol(l * 16 + 4 + c), bias=dcol(l * 16 + 4 + c))
                S.I('act', 'activation', out=HV(m_, a0, nn), in_=HV(m_, a0, nn), func=AF.Ln, scale=-1.0, bias=1.0)
                S.I('act', 'activation', out=HV(m_, a0, nn), in_=HV(m_, a0, nn), func=AF.Exp, scale=0.5,
                    bias=-0.6931471805599453)
                yield
                if g == 0 and hf == 0:
                    S.I('dve', 'memset', ap=HV(m_, 0, 1), constant=0.5)
                S.I('dve', 'scalar_tensor_tensor', out=HV(i_, a0, nn), in0=HV(i_, a0, nn), scalar=1.0,
                    in1=HV(xa, a0, nn), op0=ALU.add, op1=ALU.mult)
                S.I('dve', 'tensor_tensor', out=HV(i_, a0, nn), in0=HV(i_, a0, nn), in1=HV(m_, a0, nn), op=ALU.mult)
                init = rgh.v(l * 4 + c, [[1, 1]]) if hf == 0 else HV(hh, 511, 1)
                S.I('dve', 'tensor_tensor_scan', out=HV(hh, a0, npr), data0=HV(a_, a0, npr),
                    data1=HV(i_, a0, npr), initial=init, op0=ALU.mult, op1=ALU.add)
                if hf == 1:
                    S.I('dve', 'tensor_copy', out=rgh.v(l * 4 + c, [[1, 1]]), in_=HV(hh, TPG - 1, 1))
                    S.I('dve', 'tensor_tensor', out=HV(hh, TPG, NSG), in0=HV(a_, TPG, NSG),
                        in1=h0s.v(c * NSG, [[1, NSG]]), op=ALU.mult)
                    S.I('dve', 'tensor_tensor', out=HV(hh, TPG, NSG), in0=HV(hh, TPG, NSG),
                        in1=HV(i_, TPG, NSG), op=ALU.add)
                    S.I('dve', 'tensor_copy', out=h0s.v(c * NSG, [[1, NSG]]), in_=HV(hh, TPG, NSG))

                yield
                def cons_ag(si, a, n, bank, c=c, hh=hh, xa=xa, r_=r_):
                    u1 = HV(xa, a, n)
                    u3 = HV(r_, a, n)
                    S.I('act', 'activation', out=u1, in_=PS(bank, 0, [[1, n]]), func=AF.Square)
                    S.I('dve', 'tensor_scalar', out=u1, in0=u1, scalar1=0.044715, scalar2=1.0, op0=ALU.mult,
                        op1=ALU.add)
                    S.I('dve', 'tensor_tensor', out=u3, in0=u1, in1=PS(bank, 0, [[1, n]]), op=ALU.mult)
                    S.I('act', 'activation', out=u3, in_=u3, func=AF.Tanh, scale=0.7978845608028654)
                    S.I('dve', 'scalar_tensor_tensor', out=u3, in0=u3, scalar=1.0, in1=PS(bank, 0, [[1, n]]),
                        op0=ALU.add, op1=ALU.mult)
                    S.I('dve', 'scalar_tensor_tensor', out=CH(ymT, c, a, a + n), in0=u3, scalar=0.5,
                        in1=HV(hh, a, n), op0=ALU.mult, op1=ALU.mult)
                dense(wag, 8, c * 128, 128, uT, cons_ag, sgs=sgs_h)

            interleave((rg_unit(c, hf) for c in range(4) for hf in range(2)), width=2)
            bk = pbank()
            S.I('dve', 'tensor_copy', out=cmp_.v(0, [[24, 4], [3, NSG], [1, 3]]),
                in_=axs.v(1, [[NSG * 4, 4], [4, NSG], [1, 3]]))
            for c in range(4):
                S.I('pe', 'transpose', out=PS(bk, c * 128, [[1, 128]], n=24),
                    in_=cmp_.v(c * 24, [[1, 24]]), identity=ident)
            xo = big(8, 10)
            S.I('act', 'activation', out=xo.v(0, [[1, 512]], n=24), in_=PS(bk, 0, [[1, 512]], n=24), func=AF.Copy)
            S.dma('sp', o_rc_s[l, s_base:s_base + NSG].rearrange("b k c -> (b k) c"), xo.v(0, [[1, 512]], n=24),
                  is_output=True)
            bk = pbank()
            for c in range(4):
                S.I('pe', 'transpose', out=PS(bk, c * 128, [[1, 128]], n=NSG), in_=h0s.v(c * NSG, [[1, NSG]]),
                    identity=ident)
            xo = big(8, 10)
            S.I('act', 'activation', out=xo.v(0, [[1, 512]], n=NSG), in_=PS(bk, 0, [[1, 512]], n=NSG), func=AF.Copy)
            S.dma('sp', o_rh_s[l, s_base:s_base + NSG, :], xo.v(0, [[1, 512]], n=NSG), is_output=True)
            if g == NG - 1:
                bk = pbank()
                for c in range(4):
                    S.I('pe', 'transpose', out=PS(bk, c * 128, [[1, 128]], n=3), in_=rgc.v(l * 12 + c * 3, [[1, 3]]),
                        identity=ident)
                xo = big(8, 10)
                S.I('act', 'activation', out=xo.v(0, [[1, 512]], n=3), in_=PS(bk, 0, [[1, 512]], n=3), func=AF.Copy)
                S.dma('sp', o_rc_p[l], xo.v(0, [[1, 512]], n=3), is_output=True)
                bk = pbank()
                for c in range(4):
                    S.I('pe', 'transpose', out=PS(bk, c * 128, [[1, 128]], n=1), in_=rgh.v(l * 4 + c, [[1, 1]]),
                        identity=ident)
                xo = big(8, 10)
                S.I('act', 'activation', out=xo.v(0, [[1, 512]], n=1), in_=PS(bk, 0, [[1, 512]], n=1), func=AF.Copy)
                S.dma('sp', o_rh_p[l:l + 1, :], xo.v(0, [[1, 512]], n=1), is_output=True)


            zs = bigbf_tile(7, 4)
            xbf = bigbf_tile(9, 2)
            dtT = tmt(0, TC)
            raw2 = tmt(TC, TC)
            wz = wnext()
            for c in range(4):
                def cons_zs(si, a, n, bank, c=c):
                    S.I('act', 'activation', out=CH(zs, c, a, a + n), in_=PS(bank, 0, [[1, n]]), func=AF.Silu)
                dense(wz, 8, c * 128, 128, uT, cons_zs)
            xt = xin[1]
            S.dma('sp', xt.v(0, [[1, 768]], n=3 * NSG),
                  st_sc[l, s_base:s_base + NSG].rearrange("b k c -> (b k) c"))
            bk = pbank()
            for c in range(6):
                S.I('pe', 'transpose', out=PS(bk, c * 32, [[1, 24]]), in_=xt.v(c * 128, [[1, 128]], n=24),
                    identity=cst.v(K_ID, [[1, 24]], n=24))
            S.I('dve', 'tensor_copy', out=xbs.v(0, [[NSG * 4, 6], [4, NSG], [1, 3]]),
                in_=PS(bk, 0, [[32, 6], [3, NSG], [1, 3]]))
            wx1 = wnext(); wx2 = wnext()
            for c in range(6):
                raw = bigs[6] if c % 2 == 0 else raw2
                wt_, mc = (wx1, c) if c < 4 else (wx2, c - 4)
                S.I('dve', 'tensor_copy', out=raw.v(0, [[1, 3]]), in_=ssc.v((l * 6 + c) * 3, [[1, 3]]))

                def cons_xb(si, a, n, bank, c=c, raw=raw):
                    if si < 2:
                        S.I('act', 'activation', out=raw.v(3 + a, [[1, n]]), in_=PS(bank, 0, [[1, n]]), func=AF.Copy)
                    else:
                        S.I('act', 'activation', out=xbs.v(c * NSG * 4 + 3, [[4, NSG]]), in_=PS(bank, 0, [[1, n]]),
                            func=AF.Copy)
                dense(wt_, 8, mc * 128, 128, uT, cons_xb)
                S.I('dve', 'tensor_copy', out=ssc.v((l * 6 + c) * 3, [[1, 3]]), in_=raw.v(TPG, [[1, 3]]))
                xo_ = bigs[c]
                wc0 = cb + C_CBW + c * 4
                S.I('dve', 'tensor_scalar', out=xo_.v(0, [[1, TPG]]), in0=raw.v(0, [[1, TPG]]),
                    scalar1=col(wc0), scalar2=col(cb + C_CBB + c), op0=ALU.mult, op1=ALU.add)
                for k in range(1, 4):
                    S.I('dve', 'scalar_tensor_tensor', out=xo_.v(0, [[1, TPG]]), in0=raw.v(k, [[1, TPG]]),
                        scalar=col(wc0 + k), in1=xo_.v(0, [[1, TPG]]), op0=ALU.mult, op1=ALU.add)
                S.I('dve', 'tensor_scalar', out=xo_.v(TPG, [[1, NSG]]), in0=xbs.v(c * NSG * 4, [[4, NSG]]),
                    scalar1=col(wc0), scalar2=col(cb + C_CBB + c), op0=ALU.mult, op1=ALU.add)
                for k in range(1, 4):
                    S.I('dve', 'scalar_tensor_tensor', out=xo_.v(TPG, [[1, NSG]]),
                        in0=xbs.v(c * NSG * 4 + k, [[4, NSG]]), scalar=col(wc0 + k), in1=xo_.v(TPG, [[1, NSG]]),
                        op0=ALU.mult, op1=ALU.add)
                S.I('act', 'activation', out=xo_.v(), in_=xo_.v(), func=AF.Silu)
                if c >= 4:
                    S.I('act', 'activation', out=CH(xbf, c - 4), in_=xo_.v(), func=AF.Copy)
            S.I('dve', 'tensor_copy', out=cmp_.v(0, [[24, 6], [3, NSG], [1, 3]]),
                in_=xbs.v(1, [[NSG * 4, 6], [4, NSG], [1, 3]]))
            bka, bkb = pbank2()
            for c in range(6):
                bk_ = bka if c < 4 else bkb
                S.I('pe', 'transpose', out=PS(bk_, (c % 4) * 128, [[1, 128]], n=24), in_=cmp_.v(c * 24, [[1, 24]]),
                    identity=ident)
            xo = big(6, 7)
            S.I('act', 'activation', out=xo.v(0, [[1, 512]], n=24), in_=PS(bka, 0, [[1, 512]], n=24), func=AF.Copy)
            S.I('act', 'activation', out=xo.v(512, [[1, 256]], n=24), in_=PS(bkb, 0, [[1, 256]], n=24), func=AF.Copy)
            S.dma('sp', o_sc_s[l, s_base:s_base + NSG].rearrange("b k c -> (b k) c"), xo.v(0, [[1, 768]], n=24),
                  is_output=True)
            if g == NG - 1:
                bka, bkb = pbank2()
                for c in range(6):
                    bk_ = bka if c < 4 else bkb
                    S.I('pe', 'transpose', out=PS(bk_, (c % 4) * 128, [[1, 128]], n=3),
                        in_=ssc.v((l * 6 + c) * 3, [[1, 3]]), identity=ident)
                xo = raw2
                S.I('act', 'activation', out=xo.v(0, [[1, 512]], n=3), in_=PS(bka, 0, [[1, 512]], n=3), func=AF.Copy)
                S.I('act', 'activation', out=xo.v(512, [[1, 256]], n=3), in_=PS(bkb, 0, [[1, 256]], n=3),
                    func=AF.Copy)
                S.dma('sp', o_sc_p[l], xo.v(0, [[1, 768]], n=3), is_output=True)
            wdt = wnext()

            def cons_dt(si, a, n, bank):
                S.I('act', 'activation', out=dtT.v(a, [[1, n]], n=8), in_=PS(bank, 0, [[1, n]], n=8), func=AF.Exp,
                    bias=col(cb + C_DTB, n=8))
            dense(wdt, 8, 0, 8, uT, cons_dt)
            S.I('act', 'activation', out=dtT.v(0, [[1, TC]], n=8), in_=dtT.v(0, [[1, TC]], n=8), func=AF.Ln, bias=1.0)

            o0 = 2 * TC
            Ss = tmt(o0, 4096); t1 = tmt(o0 + 4096, 1024); dg = tmt(o0 + 5120, 1024)
            sm2 = tmt(o0 + 6144, 256)
            dte = sm2.v(0, [[1, 32]]); dec = sm2.v(32, [[1, 32]]); xdts = sm2.v(64, [[1, 32]])
            ys = sm2.v(96, [[1, 32]]); ygs = sm2.v(128, [[1, 32]]); rs8 = sm2.v(160, [[1, 8]])
            sq8 = sm2.v(192, [[1, 32]])
            for c in range(4):
                S.dma('sp', Ss.v(c * 1024, [[128, NSG], [1, 128]]),
                      st_ss[l, s_base:s_base + NSG, c * 128:(c + 1) * 128, :].rearrange("b p n -> p b n"))
            bk = pbank()
            for c in range(4):
                S.I('pe', 'matmul', out=PS(bk, c * 8, [[1, NSG]]), lhsT=cst.v(K_E8 + c * 128, [[1, 128]], n=8),
                    rhs=dtT.v(TPG, [[1, NSG]], n=8), start=True, stop=True)
            S.I('dve', 'tensor_copy', out=dte, in_=PS(bk, 0, [[1, 32]]))
            for c in range(4):
                S.I('act', 'activation', out=sm2.v(32 + c * 8, [[1, 8]]), in_=sm2.v(c * 8, [[1, 8]]), func=AF.Exp,
                    scale=anegx.v(l * 4 + c, [[1, 1]]))
                S.I('dve', 'tensor_tensor', out=sm2.v(64 + c * 8, [[1, 8]]), in0=sm2.v(c * 8, [[1, 8]]),
                    in1=bigs[c].v(TPG, [[1, NSG]]), op=ALU.mult)
            bcs = []
            for which in (4, 5):
                S.I('dve', 'tensor_tensor', out=dg.v(0, [[128, NSG], [1, 128]]), in0=cst.v(K_ID, [[0, NSG], [1, 128]]),
                    in1=bigs[which].v(TPG, [[1, NSG], [0, 128]]), op=ALU.mult)
                b2 = pbank2()
                for hf in range(2):
                    S.I('pe', 'matmul', out=PS(b2[hf]), lhsT=ones_f, rhs=dg.v(hf * 512, [[1, 512]]), start=True,
                        stop=True)
                bcs.append(b2)
            for c in range(4):
                for hf in range(2):
                    S.I('dve', 'tensor_tensor', out=t1.v(hf * 512, [[128, 4], [1, 128]]),
                        in0=PS(bcs[0][hf], 0, [[128, 4], [1, 128]]),
                        in1=sm2.v(64 + c * 8 + hf * 4, [[1, 4], [0, 128]]), op=ALU.mult)
                Sc = Ss.v(c * 1024, [[128, NSG], [1, 128]])
                S.I('dve', 'tensor_tensor', out=Sc, in0=Sc, in1=sm2.v(32 + c * 8, [[1, NSG], [0, 128]]), op=ALU.mult)
                S.I('dve', 'tensor_tensor', out=Sc, in0=Sc, in1=t1.v(0, [[128, NSG], [1, 128]]), op=ALU.add)
                S.dma('sp', o_ss_s[l, s_base:s_base + NSG, c * 128:(c + 1) * 128, :].rearrange("b p n -> p b n"),
                      Ss.v(c * 1024, [[128, NSG], [1, 128]]), is_output=True)
                for hf in range(2):
                    S.I('dve', 'tensor_tensor', out=t1.v(hf * 512, [[128, 4], [1, 128]]),
                        in0=PS(bcs[1][hf], 0, [[128, 4], [1, 128]]),
                        in1=Ss.v(c * 1024 + hf * 512, [[128, 4], [1, 128]]), op=ALU.mult)
                S.I('dve', 'tensor_reduce', out=sm2.v(96 + c * 8, [[1, NSG]]), in_=t1.v(0, [[128, NSG], [1, 128]]),
                    axis=AX.X, op=ALU.add)
                S.I('dve', 'scalar_tensor_tensor', out=sm2.v(96 + c * 8, [[1, NSG]]), in0=bigs[c].v(TPG, [[1, NSG]]),
                    scalar=col(cb + C_DEXP + c), in1=sm2.v(96 + c * 8, [[1, NSG]]), op0=ALU.mult, op1=ALU.add)
                S.I('dve', 'tensor_tensor', out=sm2.v(128 + c * 8, [[1, NSG]]), in0=sm2.v(96 + c * 8, [[1, NSG]]),
                    in1=CH(zs, c, TPG, TC), op=ALU.mult)
            S.I('dve', 'tensor_tensor', out=sq8, in0=ygs, in1=ygs, op=ALU.mult)
            bk = pbank()
            for c in range(4):
                S.I('pe', 'matmul', out=PS(bk, 0, [[1, NSG]]), lhsT=ones_f, rhs=sm2.v(192 + c * 8, [[1, NSG]]),
                    start=(c == 0), stop=(c == 3))
            S.I('act', 'activation', out=rs8, in_=PS(bk, 0, [[1, NSG]]), func=AF.Ln, scale=1.0 / 512, bias=EPS)
            S.I('act', 'activation', out=rs8, in_=rs8, func=AF.Exp, scale=-0.5)
            for c in range(4):
                S.I('dve', 'scalar_tensor_tensor', out=CH(ymT, 4 + c, TPG, TC), in0=sm2.v(128 + c * 8, [[1, NSG]]),
                    scalar=col(cb + C_SNORMC + c), in1=rs8, op0=ALU.mult, op1=ALU.mult)

            o1 = TC

            def sset(p):
                o = o1 + p * 3328
                return dict(R1h=tmt(o, 512, bf=True), R1l=tmt(o + 512, 512, bf=True), LM=tmt(o + 1024, 512, bf=True), smt=tmt(o + 1536, 128),
                            cbs=tmt(o + 1664, 64, bf=True), Btm=tmt(o + 1728, 64, bf=True),
                            xdt=tmt(o + 1792, 256, bf=True), xw=tmt(o + 2048, 256, bf=True),
                            xDb=tmt(o + 2304, 256, bf=True), yy=tmt(o + 2560, 512), ynb=tmt(o + 3072, 256, bf=True),
                            dth=tmt(o + 1536 + 96, 4, bf=True), dtl=tmt(o + 1536 + 104, 4, bf=True))
            ssets = [sset(0), sset(1)]
            junk = tmt(o1 + 6656, 256, bf=True)
            Sbf = [tmt(o1 + 6912, 256, bf=True), tmt(o1 + 7168, 256, bf=True)]
            Scur = Sst.v(l * 512, [[1, 512]])
            S.I('act', 'activation', out=Sbf[0].v(), in_=Scur, func=AF.Copy)
            ugt_f = cst.v(K_NEGS, [[1, 128]])

            def ssd_tile(j):
                t0 = j * 128
                Q = ssets[j % 2]
                R1h, R1l, LM, smt, cbs, Btm, xdt, xw, xDb, yy, ynb = (Q['R1h'], Q['R1l'], Q['LM'], Q['smt'], Q['cbs'], Q['Btm'], Q['xdt'],
                                                               Q['xw'], Q['xDb'], Q['yy'], Q['ynb'])
                dt_tm = smt.v(0, [[1, 8]]); dta = smt.v(8, [[1, 8]]); cum_sb = smt.v(16, [[1, 16]])
                ecum = smt.v(32, [[1, 8]]); etot = smt.v(40, [[1, 8]]); wdec = smt.v(48, [[1, 8]])
                w2 = smt.v(56, [[1, 8]]); ddv = smt.v(64, [[1, 8]]); ssq = smt.v(72, [[1, 1]])
                dth = Q['dth']; dtl = Q['dtl']
                bkX = pbank()
                phold(bkX)
                for c in range(4):
                    S.I('pe', 'transpose', out=PS(bkX, c * 128, [[1, 128]]), in_=bigs[c].v(t0, [[1, 128]]),
                        identity=ident)
                bkB = pbank()
                S.I('pe', 'transpose', out=PS(bkB, 0, [[1, 128]]), in_=bigs[4].v(t0, [[1, 128]]), identity=ident)
                S.I('pe', 'transpose', out=PS(bkB, 128, [[1, 8]]), in_=dtT.v(t0, [[1, 128]], n=8),
                    identity=cst.v(K_ID, [[1, 8]], n=8))
                S.I('dve', 'tensor_copy', out=dt_tm, in_=PS(bkB, 128, [[1, 8]]))
                S.I('dve', 'tensor_tensor', out=dta, in0=dt_tm, in1=anegb.v(l * 8, [[1, 8]]), op=ALU.mult)
                S.I('dve', 'tensor_copy', out=Btm.v(), in_=PS(bkB, 0, [[1, 128]]))
                S.I('dve', 'tensor_copy', out=dth.v(), in_=dta)
                S.I('dve', 'tensor_tensor', out=dtl.v(), in0=dta, in1=dth.v(), op=ALU.subtract)
                S.I('pool', 'tensor_tensor', out=R1h.v(0, [[128, 8], [1, 128]]), in0=cbf.v(128, [[0, 8], [1, 128]]),
                    in1=dth.v(0, [[1, 8], [0, 128]]), op=ALU.mult)
                S.I('dve', 'tensor_tensor', out=R1l.v(0, [[128, 8], [1, 128]]), in0=cbf.v(128, [[0, 8], [1, 128]]),
                    in1=dtl.v(0, [[1, 8], [0, 128]]), op=ALU.mult)
                yield
                b2 = pbank2()
                for hf in range(2):
                    S.I('pe', 'matmul', out=PS(b2[hf]), lhsT=cbf.v(512, [[1, 128]]), rhs=R1h.v(hf * 512, [[1, 512]]),
                        start=True, stop=False)
                    S.I('pe', 'matmul', out=PS(b2[hf]), lhsT=cbf.v(512, [[1, 128]]), rhs=R1l.v(hf * 512, [[1, 512]]),
                        start=False, stop=True)
                bkC = pbank()
                S.I('pe', 'matmul', out=PS(bkC, 0, [[1, 8]]), lhsT=ule_f, rhs=dta, start=True, stop=True)
                S.I('pe', 'matmul', out=PS(bkC, 8, [[1, 8]]), lhsT=ones_f, rhs=dta, start=True, stop=True)
                S.I('pe', 'matmul', out=PS(bkC, 128, [[1, 128]]), lhsT=CH(xbf, 0, t0, t0 + 128),
                    rhs=CH(xbf, 1, t0, t0 + 128), start=True, stop=True)
                for hf in range(2):
                    S.I('act', 'activation', out=LM.v(hf * 512, [[1, 512]]), in_=PS(b2[hf]), func=AF.Exp)
                S.I('dve', 'tensor_copy', out=cum_sb, in_=PS(bkC, 0, [[1, 16]]))
                S.I('dve', 'tensor_tensor', out=cbs.v(), in0=PS(bkC, 128, [[1, 128]]), in1=ule_f, op=ALU.mult)
                S.I('dve', 'tensor_tensor', out=ddv, in0=smt.v(24, [[1, 8]]), in1=smt.v(16, [[1, 8]]), op=ALU.subtract)
                S.I('act', 'activation', out=ecum, in_=smt.v(16, [[1, 8]]), func=AF.Exp)
                S.I('act', 'activation', out=etot, in_=smt.v(24, [[1, 8]]), func=AF.Exp)
                S.I('act', 'activation', out=wdec, in_=ddv, func=AF.Exp)
                S.I('dve', 'tensor_tensor', out=w2, in0=dt_tm, in1=wdec, op=ALU.mult)
                S.I('dve', 'tensor_tensor', out=LM.v(0, [[128, 8], [1, 128]]), in0=LM.v(0, [[128, 8], [1, 128]]),
                    in1=cbs.v(0, [[0, 8], [1, 128]]), op=ALU.mult)
                X3 = PS(bkX, 0, [[64, 8], [1, 64]])
                S.I('dve', 'tensor_tensor', out=xdt.v(0, [[64, 8], [1, 64]]), in0=X3, in1=smt.v(0, [[1, 8], [0, 64]]),
                    op=ALU.mult)
                S.I('dve', 'tensor_tensor', out=xw.v(0, [[64, 8], [1, 64]]), in0=X3, in1=smt.v(56, [[1, 8], [0, 64]]),
                    op=ALU.mult)
                S.I('dve', 'tensor_tensor', out=xDb.v(0, [[64, 8], [1, 64]]), in0=X3,
                    in1=rows.v(l * LR + R_SD, [[1, 8], [0, 64]]), op=ALU.mult)
                prel(bkX)
                yield
                bkY = pbank()
                S.I('pe', 'matmul', out=PS(bkY), lhsT=identb, rhs=xDb.v(), start=True, stop=False,
                    skip_group_check=True)
                for h in range(8):
                    S.I('pe', 'matmul', out=PS(bkY, h * 64, [[1, 64]]), lhsT=LM.v(h * 128, [[1, 128]]),
                        rhs=xdt.v(h * 64, [[1, 64]]), start=False, stop=(h == 7), skip_group_check=True)
                bkD = pbank()
                S.I('pe', 'matmul', out=PS(bkD), lhsT=Btm.v(), rhs=xw.v(), start=True, stop=True)
                bkYi = pbank()
                S.I('pe', 'matmul', out=PS(bkYi), lhsT=CH(xbf, 1, t0, t0 + 128), rhs=Sbf[j % 2].v(), start=True,
                    stop=True)
                S.I('dve', 'tensor_tensor', out=Sst.v(l * 512, [[64, 8], [1, 64]]), in0=Sst.v(l * 512, [[64, 8], [1, 64]]),
                    in1=smt.v(40, [[1, 8], [0, 64]]), op=ALU.mult)
                S.I('dve', 'tensor_tensor', out=Scur, in0=Scur, in1=PS(bkD), op=ALU.add)
                S.I('act', 'activation', out=Sbf[(j + 1) % 2].v(), in_=Scur, func=AF.Copy)
                S.I('dve', 'tensor_tensor', out=yy.v(0, [[64, 8], [1, 64]]), in0=PS(bkYi, 0, [[64, 8], [1, 64]]),
                    in1=smt.v(32, [[1, 8], [0, 64]]), op=ALU.mult)
                S.I('dve', 'tensor_tensor', out=yy.v(), in0=yy.v(), in1=PS(bkY), op=ALU.add)
                yield
                bkZ = pbank()
                for c in range(4):
                    S.I('pe', 'transpose', out=PSB(bkZ, c * 128, [[1, 128]]), in_=CH(zs, c, t0, t0 + 128),
                        identity=identb)
                S.I('dve', 'tensor_tensor', out=yy.v(), in0=yy.v(), in1=PSB(bkZ, 0, [[1, 512]]), op=ALU.mult)
                S.I('act', 'activation', out=junk.v(), in_=yy.v(), func=AF.Square, accum_out=ssq)
                S.I('act', 'activation', out=ssq, in_=ssq, func=AF.Ln, scale=1.0 / 512, bias=EPS)
                S.I('act', 'activation', out=ssq, in_=ssq, func=AF.Exp, scale=-0.5)
                S.I('dve', 'scalar_tensor_tensor', out=ynb.v(), in0=yy.v(), scalar=ssq,
                    in1=rows.v(l * LR + R_SNORM, [[1, 512]]), op0=ALU.mult, op1=ALU.mult)
                bkT = pbank()
                for c in range(4):
                    S.I('pe', 'transpose', out=PSB(bkT, c * 128, [[1, 128]]), in_=ynb.v(c * 128, [[1, 128]]),
                        identity=identb)
                S.I('act', 'activation', out=ymT.v(4 * TC + t0, [[TC, 4], [1, 128]], sl=(4, 8)),
                    in_=PSB(bkT, 0, [[128, 4], [1, 128]]), func=AF.Copy)

            interleave((ssd_tile(j) for j in range(8)), width=2)
            xD = tmt(o1 + 2560, 512)
            if g == NG - 1:
                bk = pbank()
                for c in range(4):
                    S.I('pe', 'transpose', out=PS(bk, c * 128, [[1, 128]]), in_=Sst.v(l * 512 + c * 128, [[1, 128]]),
                        identity=ident)
                S.I('act', 'activation', out=xD.v(), in_=PS(bk), func=AF.Copy)
                S.dma('sp', o_ss_p[l].rearrange("(c p) n -> p c n", p=128), xD.v(0, [[128, 4], [1, 128]]),
                      is_output=True)


            AqT = bigbf_tile(0, 4); BkT = bigbf_tile(2, 4); KdT = bigbf_tile(4, 4)
            vTb = bigbf_tile(6, 4); gsT = bigbf_tile(8, 4)
            tA = tmt(0, TC); tB = tmt(TC, TC); tC_ = tmt(2 * TC, TC)
            tE = [tmt((3 + c) * TC, TC) for c in range(4)]
            blk_f = cst.v(K_BLK, [[1, 128]])
            def HV2(t, a, n):
                w_ = (t.dsl[1] - t.dsl[0]) / 2.0
                lo = t.dsl[0] + (0.0 if a < 512 else w_)
                hi = t.dsl[0] + (w_ if a + n <= 512 else 2 * w_)
                return t.v(a, [[1, n]], sl=(lo, hi))

            wf_ = wnext()

            def hgf_unit(c, hf):
                a0, nn = (0, 512) if hf == 0 else (512, 512 + NSG)
                sgs_h = [SGS[0]] if hf == 0 else [SGS[1], SGS[2]]

                def cons_f(si, a, n, bank):
                    S.I('act', 'activation', out=HV2(tA, a, n), in_=PS(bank, 0, [[1, n]]), func=AF.Tanh, scale=0.5)
                dense(wf_, 8, c * 128, 128, uT, cons_f, sgs=sgs_h)
                yield
                S.I('dve', 'tensor_scalar', out=HV2(tA, a0, nn), in0=HV2(tA, a0, nn), scalar1=dcol(64 + l * 8 + c),
                    scalar2=dcol(64 + l * 8 + 4 + c), op0=ALU.mult, op1=ALU.add)
                if hf == 1:
                    S.I('dve', 'tensor_copy', out=hsm.v((4 + c) * NSG, [[1, NSG]]), in_=HV2(tA, TPG, NSG))
                S.I('act', 'activation', out=HV2(tB, a0, nn), in_=HV2(tA, a0, nn), func=AF.Ln)
                S.I('dve', 'tensor_scalar', out=HV2(tA, a0, nn), in0=HV2(tA, a0, nn), scalar1=-1.0, scalar2=1.0,
                    op0=ALU.mult, op1=ALU.add)
                if hf == 1:
                    S.I('dve', 'tensor_copy', out=hsm.v((8 + c) * NSG, [[1, NSG]]), in_=HV2(tA, TPG, NSG))
                yield
                S.I('dve', 'tensor_tensor_scan', out=HV2(tC_, a0, 512), data0=cst.v(K_RMASK + a0, [[1, 512]]),
                    data1=HV2(tB, a0, 512), initial=0.0, op0=ALU.mult, op1=ALU.add)
                S.I('act', 'activation', out=HV2(tE[c], a0, 512), in_=HV2(tC_, a0, 512), func=AF.Exp)
                S.I('act', 'activation', out=HV2(tB, a0, 512), in_=HV2(tC_, a0, 512), func=AF.Exp, scale=-1.0)
                yield
                S.I('dve', 'tensor_tensor', out=HV2(tC_, a0, 512), in0=HV2(tA, a0, 512), in1=HV2(tB, a0, 512),
                    op=ALU.mult)
                S.I('act', 'activation', out=CH(BkT, c, a0, a0 + 512), in_=HV2(tC_, a0, 512), func=AF.Copy)
                slh = (tC_.dsl[0] + hf * (TC / 2.0), tC_.dsl[0] + (hf + 1) * (TC / 2.0))
                sle = (tE[c].dsl[0] + hf * (TC / 2.0), tE[c].dsl[0] + (hf + 1) * (TC / 2.0))
                S.I('dve', 'tensor_tensor', out=KdT.v(c * TC + a0, [[64, 8], [1, 64]], sl=KdT.csl(c, c + 1)),
                    in0=tC_.v(a0, [[64, 8], [1, 64]], sl=slh), in1=tE[c].v(a0 + 63, [[64, 8], [0, 64]], sl=sle),
                    op=ALU.mult)
                S.I('dve', 'tensor_copy', out=Dd.v(c * 16 + hf * 8, [[1, 8]]), in_=tE[c].v(a0 + 63, [[64, 8]], sl=sle))

            interleave((hgf_unit(c, hf) for c in range(4) for hf in range(2)), width=2)
            wq_ = wnext()

            def hgq_unit(c, hf):
                a0, nn = (0, 512) if hf == 0 else (512, 512 + NSG)
                sgs_h = [SGS[0]] if hf == 0 else [SGS[1], SGS[2]]

                def cons_q(si, a, n, bank):
                    S.I('act', 'activation', out=HV2(tA, a, n), in_=PS(bank, 0, [[1, n]]), func=AF.Silu)
                dense(wq_, 8, c * 128, 128, uT, cons_q, sgs=sgs_h)
                yield
                if hf == 1:
                    S.I('dve', 'tensor_copy', out=hsm.v(c * NSG, [[1, NSG]]), in_=HV2(tA, TPG, NSG))
                S.I('dve', 'tensor_tensor', out=CH(AqT, c, a0, a0 + 512), in0=HV2(tA, a0, 512),
                    in1=HV2(tE[c], a0, 512), op=ALU.mult)

            interleave((hgq_unit(c, hf) for c in range(4) for hf in range(2)), width=2)
            wi_ = wnext()
            for c in range(4):
                def cons_v(si, a, n, bank, c=c):
                    if si < 2:
                        S.I('act', 'activation', out=CH(vTb, c, a, a + n), in_=PS(bank, 0, [[1, n]]), func=AF.Copy)
                    else:
                        S.I('act', 'activation', out=hsm.v((12 + c) * NSG, [[1, NSG]]), in_=PS(bank, 0, [[1, n]]),
                            func=AF.Copy)
                dense(wi_, 8, c * 128, 128, uT, cons_v)
            wg2 = wnext()
            for c in range(4):
                def cons_g2(si, a, n, bank, c=c):
                    S.I('act', 'activation', out=CH(gsT, c, a, a + n), in_=PS(bank, 0, [[1, n]]), func=AF.Silu)
                dense(wg2, 8, c * 128, 128, uT, cons_g2)

            Ssh = tmt(0, 2048); dgv = tmt(2048, 512); h1 = tmt(2560, 512); h2 = tmt(3072, 512)
            hs2 = tmt(3584, 128)
            for c in range(4):
                S.dma('sp', Ssh.v(c * 512, [[64, NSG], [1, 64]]),
                      st_hs[l, s_base:s_base + NSG, c * 128:(c + 1) * 128, :].rearrange("b p e -> p b e"))
            for c in range(4):
                S.I('dve', 'tensor_tensor', out=dgv.v(0, [[64, NSG], [1, 64]]), in0=cst.v(K_I2, [[0, NSG], [1, 64]]),
                    in1=hsm.v((12 + c) * NSG, [[1, NSG], [0, 64]]), op=ALU.mult)
                bkv = pbank()
                S.I('pe', 'matmul', out=PS(bkv), lhsT=blk_f, rhs=dgv.v(), start=True, stop=True)
                S.I('dve', 'tensor_tensor', out=h1.v(0, [[64, NSG], [1, 64]]), in0=PS(bkv, 0, [[64, NSG], [1, 64]]),
                    in1=hsm.v((8 + c) * NSG, [[1, NSG], [0, 64]]), op=ALU.mult)
                Sc = Ssh.v(c * 512, [[64, NSG], [1, 64]])
                S.I('dve', 'tensor_tensor', out=Sc, in0=Sc, in1=hsm.v((4 + c) * NSG, [[1, NSG], [0, 64]]), op=ALU.mult)
                S.I('dve', 'tensor_tensor', out=Sc, in0=Sc, in1=h1.v(0, [[64, NSG], [1, 64]]), op=ALU.add)
                S.dma('sp', o_hs_s[l, s_base:s_base + NSG, c * 128:(c + 1) * 128, :].rearrange("b p e -> p b e"),
                      Ssh.v(c * 512, [[64, NSG], [1, 64]]), is_output=True)
                S.I('dve', 'tensor_tensor', out=h2.v(0, [[64, NSG], [1, 64]]), in0=Sc,
                    in1=hsm.v(c * NSG, [[1, NSG], [0, 64]]), op=ALU.mult)
                bko = pbank()
                S.I('pe', 'matmul', out=PS(bko), lhsT=blk_f, rhs=h2.v(), start=True, stop=True)
                S.I('dve', 'tensor_tensor', out=h1.v(0, [[64, NSG], [1, 64]]), in0=PS(bko, 0, [[64, NSG], [1, 64]]),
                    in1=cst.v(K_I2, [[0, NSG], [1, 64]]), op=ALU.mult)
                osv = hs2.v(c * NSG, [[1, NSG]])
                S.I('dve', 'tensor_reduce', out=osv, in_=h1.v(0, [[64, NSG], [1, 64]]), axis=AX.X, op=ALU.add)
                sqv = hs2.v(32 + c * NSG, [[1, NSG]])
                S.I('dve', 'tensor_tensor', out=sqv, in0=osv, in1=osv, op=ALU.mult)
                bkn = pbank()
                S.I('pe', 'matmul', out=PS(bkn, 0, [[1, NSG]]), lhsT=blk_f, rhs=sqv, start=True, stop=True)
                rsv = hs2.v(64 + c * NSG, [[1, NSG]])
                S.I('act', 'activation', out=rsv, in_=PS(bkn, 0, [[1, NSG]]), func=AF.Ln, scale=1.0 / 64, bias=EPS)
                S.I('act', 'activation', out=rsv, in_=rsv, func=AF.Exp, scale=-0.5)
                S.I('dve', 'scalar_tensor_tensor', out=osv, in0=osv, scalar=col(cb + C_HNORMC), in1=rsv, op0=ALU.mult,
                    op1=ALU.mult)
                S.I('dve', 'tensor_tensor', out=CH(ymT, 8 + c, TPG, TC), in0=osv, in1=CH(gsT, c, TPG, TC), op=ALU.mult)

            HB = 8500 - 2 * 2624

            def hset(p):
                o = HB + p * 2624
                return dict(Kdtm=tmt(o, 256, bf=True), vtm=tmt(o + 256, 256, bf=True), attm=tmt(o + 512, 512, bf=True),
                            sqo=tmt(o + 1024, 512), onf=tmt(o + 1536, 512), on2=tmt(o + 2048, 256, bf=True),
                            Sbh=[tmt(o + 2304, 128, bf=True), tmt(o + 2432, 128, bf=True)], hq=tmt(o + 2560, 64))
            hsets = [hset(0), hset(1)]
            Sl = Shg.v(l * 256, [[1, 256]])

            def hg_tile(j):
                t0 = j * 128
                H = hsets[j % 2]
                Kdtm, vtm_, attm_, sqo, onf, on2, Sbh_, hq = (H['Kdtm'], H['vtm'], H['attm'], H['sqo'], H['onf'],
                                                             H['on2'], H['Sbh'], H['hq'])
                ss8 = hq.v(0, [[1, 8]])
                bkK = pbank()
                for c in range(4):
                    S.I('pe', 'transpose', out=PSB(bkK, c * 128, [[1, 128]]), in_=CH(KdT, c, t0, t0 + 128),
                        identity=identb)
                bkV = pbank()
                for c in range(4):
                    S.I('pe', 'transpose', out=PSB(bkV, c * 128, [[1, 128]]), in_=CH(vTb, c, t0, t0 + 128),
                        identity=identb)
                S.I('act', 'activation', out=Kdtm.v(), in_=PSB(bkK, 0, [[1, 512]]), func=AF.Copy)
                S.I('dve', 'tensor_copy', out=vtm_.v(), in_=PSB(bkV, 0, [[1, 512]]))
                yield
                b2 = pbank2()
                for hh in range(2):
                    for c in range(4):
                        S.I('pe', 'matmul', out=PS(b2[hh], c * 128, [[1, 128]]),
                            lhsT=BkT.v(c * TC + t0, [[1, 128]], p0=hh * 64, n=64, sl=BkT.csl(c, c + 1)),
                            rhs=AqT.v(c * TC + t0, [[1, 128]], p0=hh * 64, n=64, sl=AqT.csl(c, c + 1)),
                            start=(c == 0), stop=(c == 3), skip_group_check=True)
                for hh in range(2):
                    S.I('dve', 'tensor_tensor', out=attm_.v(hh * 128, [[256, 4], [1, 128]]),
                        in0=PS(b2[hh], 0, [[128, 4], [1, 128]]), in1=cst.v(K_MASK2, [[0, 4], [1, 128]]), op=ALU.mult)
                bS = pbank2()
                for jj in range(2):
                    for h in range(8):
                        c, hh = h // 2, h % 2
                        S.I('pe', 'matmul', out=PS(bS[jj], c * 64, [[1, 64]], p0=hh * 64, n=64),
                            lhsT=Kdtm.v(h * 64, [[1, 64]], p0=jj * 64, n=64),
                            rhs=vtm_.v(h * 64, [[1, 64]], p0=jj * 64, n=64), start=(h < 2),
                            stop=(h >= 6), skip_group_check=True)
                for jj in range(2):
                    S.I('act', 'activation', out=Sbh_[jj].v(), in_=Sl, func=AF.Copy)
                    for c in range(4):
                        S.I('dve', 'scalar_tensor_tensor', out=Shg.v(l * 256 + c * 64, [[1, 64]]),
                            in0=Shg.v(l * 256 + c * 64, [[1, 64]]), scalar=Dd.v(c * 16 + 2 * j + jj, [[1, 1]]),
                            in1=PS(bS[jj], c * 64, [[1, 64]]), op0=ALU.mult, op1=ALU.add)
                yield
                bO = pbank2()
                for hh in range(2):
                    for c in range(4):
                        h = 2 * c + hh
                        S.I('pe', 'matmul', out=PS(bO[hh], c * 64, [[1, 64]]), lhsT=attm_.v(h * 128, [[1, 128]]),
                            rhs=vtm_.v(h * 64, [[1, 64]]), start=(c == 0), stop=False, skip_group_check=True)
                for hh in range(2):
                    for jj in range(2):
                        for c in range(4):
                            S.I('pe', 'matmul', out=PS(bO[hh], c * 64, [[1, 64]], p0=jj * 64, n=64),
                                lhsT=AqT.v(c * TC + t0 + jj * 64, [[1, 64]], p0=hh * 64, n=64, sl=AqT.csl(c, c + 1)),
                                rhs=Sbh_[jj].v(c * 64, [[1, 64]], p0=hh * 64, n=64), start=False,
                                stop=(jj == 1 and c == 3), skip_group_check=True)
                for hh in range(2):
                    S.I('act', 'activation', out=sqo.v(hh * 256, [[1, 256]]), in_=PS(bO[hh], 0, [[1, 256]]),
                        func=AF.Square)
                S.I('dve', 'tensor_reduce', out=ss8, in_=sqo.v(0, [[64, 8], [1, 64]]), axis=AX.X, op=ALU.add)
                S.I('act', 'activation', out=ss8, in_=ss8, func=AF.Ln, scale=1.0 / 64, bias=EPS)
                S.I('act', 'activation', out=ss8, in_=ss8, func=AF.Exp, scale=-0.5)
                yield
                for hh in range(2):
                    S.I('dve', 'tensor_tensor', out=onf.v(hh * 256, [[64, 4], [1, 64]]),
                        in0=PS(bO[hh], 0, [[64, 4], [1, 64]]), in1=hq.v(hh * 4, [[1, 4], [0, 64]]), op=ALU.mult)
                    S.I('dve', 'tensor_tensor', out=on2.v(hh * 64, [[128, 4], [1, 64]]),
                        in0=onf.v(hh * 256, [[64, 4], [1, 64]]), in1=rows.v(l * LR + R_HNORM, [[0, 4], [1, 64]]),
                        op=ALU.mult)
                bkT = pbank()
                for c in range(4):
                    S.I('pe', 'transpose', out=PSB(bkT, c * 128, [[1, 128]]), in_=on2.v(c * 128, [[1, 128]]),
                        identity=identb)
                S.I('dve', 'tensor_tensor', out=ymT.v(8 * TC + t0, [[TC, 4], [1, 128]], sl=(8, 12)),
                    in0=PSB(bkT, 0, [[128, 4], [1, 128]]), in1=gsT.v(t0, [[TC, 4], [1, 128]]), op=ALU.mult)

            interleave((hg_tile(j) for j in range(8)), width=2)
            if g == NG - 1:
                S.dma('sp', o_hs_p[l].rearrange("(c p) e -> p c e", p=128), Shg.v(l * 256, [[64, 4], [1, 64]]),
                      is_output=True)

            for nh in range(2):
                wa = wnext(); wb_ = wnext()
                for m in range(4):
                    mo = nh * 4 + m

                    def cons_res(si, a, n, bank, mo=mo):
                        S.I('dve', 'tensor_tensor', out=CH(hT, mo, a, a + n), in0=CH(hT, mo, a, a + n),
                            in1=PS(bank, 0, [[1, n]]), op=ALU.add)
                    bks = dense(wa, 8, m * 128, 128, ymT, None, kc0=0, first=True, last=False)
                    dense(wb_, 4, m * 128, 128, ymT, cons_res, kc0=8, first=False, last=True, banks=bks)

            norm(cb + C_NFFN, uT, tmpn=bigs[8])
            for G in range(2):
                for q in range(4):
                    wu = wnext()
                    for m in range(4):
                        zc = q * 4 + m

                        def cons_z(si, a, n, bank, zc=zc):
                            t = big(8, 10)
                            S.I('act', 'activation', out=t.v(a, [[1, n]]), in_=PS(bank, 0, [[1, n]]), func=AF.Relu)
                            S.I('pool', 'tensor_tensor', out=CH(zT, zc, a, a + n), in0=t.v(a, [[1, n]]),
                                in1=t.v(a, [[1, n]]), op=ALU.mult)
                        dense(wu, 8, m * 128, 128, uT, cons_z)
                for nh in range(2):
                    wa = wnext(); wb_ = wnext()
                    for m in range(4):
                        mo = nh * 4 + m

                        def cons_res(si, a, n, bank, mo=mo):
                            S.I('dve', 'tensor_tensor', out=CH(hT, mo, a, a + n), in0=CH(hT, mo, a, a + n),
                                in1=PS(bank, 0, [[1, n]]), op=ALU.add)
                        bks = dense(wa, 8, m * 128, 128, zT, None, kc0=0, first=True, last=False)
                        dense(wb_, 8, m * 128, 128, zT, cons_res, kc0=8, first=False, last=True, banks=bks)

            norm(cb + C_NPLE, uT, tmpn=bigs[8])
            gts = []
            for nh in range(2):
                wg_ = wnext()
                for m in range(4):
                    gt = bigs[nh * 4 + m]
                    gts.append(gt)

                    def cons_g(si, a, n, bank, gt=gt):
                        S.I('act', 'activation', out=gt.v(a, [[1, n]]), in_=PS(bank, 0, [[1, n]]), func=AF.Sigmoid)
                    dense(wg_, 8, m * 128, 128, uT, cons_g)
            for nh in range(2):
                wp_ = wnext()
                for m in range(4):
                    mo = nh * 4 + m
                    gt = gts[mo]

                    def cons_p(si, a, n, bank, mo=mo, gt=gt):
                        S.I('dve', 'tensor_tensor', out=gt.v(a, [[1, n]]), in0=gt.v(a, [[1, n]]),
                            in1=PS(bank, 0, [[1, n]]), op=ALU.mult)
                        S.I('dve', 'tensor_tensor', out=CH(hT, mo, a, a + n), in0=CH(hT, mo, a, a + n),
                            in1=gt.v(a, [[1, n]]), op=ALU.add)
                    dense(wp_, 2, m * 128, 128, pT, cons_p)

        norm(C_NFIN, hT, in_place=True)
        for i in range(8):
            yt = bigs[i] if i != 8 else xin[1]
            for hf in range(2):
                bk = pbank()
                for c in range(4):
                    S.I('pe', 'transpose', out=PS(bk, c * 128, [[1, 128]]),
                        in_=CH(hT, hf * 4 + c, i * 128, (i + 1) * 128), identity=ident)
                if hf == 0:
                    S.I('act', 'activation', out=yt.v(0, [[1, 512]]), in_=PS(bk), func=AF.Copy)
                else:
                    S.I('dve', 'tensor_copy', out=yt.v(512, [[1, 512]]), in_=PS(bk))
            S.dma('sp', y_p[t_base + i * 128: t_base + (i + 1) * 128, :], yt.v(0, [[1, 1024]]), is_output=True)
        yt = xin[0]
        for hf in range(2):
            bk = pbank()
            for c in range(4):
                S.I('pe', 'transpose', out=PS(bk, c * 128, [[1, 128]], n=NSG),
                    in_=CH(hT, hf * 4 + c, TPG, TC), identity=ident)
            S.I('act', 'activation', out=yt.v(hf * 512, [[1, 512]], n=NSG), in_=PS(bk, 0, [[1, 512]], n=NSG),
                func=AF.Copy)
        S.dma('sp', y_s[s_base:s_base + NSG, :], yt.v(0, [[1, 1024]], n=NSG), is_output=True)

    S.finish()
    S.emit(stack)
    stack.close()
    return nc


def make_consts():
    c = np.zeros((128, NCONST), np.float32)
    idx = np.arange(128)
    c[:, K_ID:K_ID + 128] = np.eye(128, dtype=np.float32)
    c[:, K_ULE:K_ULE + 128] = (idx[:, None] <= idx[None, :]).astype(np.float32)
    c[:, K_ONES:K_ONES + 128] = 1.0
    c[:, K_NEGS:K_NEGS + 128] = (idx[:, None] > idx[None, :]).astype(np.float32)
    c[:, K_MASK2:K_MASK2 + 128] = ((idx[:, None] // 64 == idx[None, :] // 64) & (idx[:, None] <= idx[None, :])).astype(np.float32)
    c[:, K_I2:K_I2 + 64] = (idx[:, None] % 64 == np.arange(64)[None, :]).astype(np.float32)
    c[:, K_BLK:K_BLK + 128] = (idx[:, None] // 64 == idx[None, :] // 64).astype(np.float32)
    c[:, K_RMASK:K_RMASK + 1024] = (np.arange(1024) % 64 != 0).astype(np.float32)[None, :]
    c[:8, K_E8:K_E8 + 512] = (np.arange(8)[:, None] == (np.arange(512) // 64)[None, :]).astype(np.float32)
    return c


_NC_CACHE = {}


def kernel(**inp):
    f = lambda a: np.ascontiguousarray(np.asarray(a, dtype=np.float32))
    x_prompt = f(inp['x_prompt']); x_sample = f(inp['x_sample'])
    cols = np.zeros((128, NCOLS), np.float32)
    rows = np.zeros((128, NROWS), np.float32)

    def colv(v):
        v = f(v)
        return v.reshape(-1, 128).T

    for l in range(DEPTH):
        b = l * LC
        cols[:, b + C_NMIX:b + C_NMIX + 8] = colv(inp['norm_mix'][l])
        cols[:, b + C_NFFN:b + C_NFFN + 8] = colv(inp['norm_ffn'][l])
        cols[:, b + C_NPLE:b + C_NPLE + 8] = colv(inp['norm_ple'][l])
        caw = f(inp['conv_a_w'][l])
        for c in range(4):
            for k in range(4):
                cols[:, b + C_CAW + c * 4 + k] = caw[k, c * 128:(c + 1) * 128]
        cols[:, b + C_CAB:b + C_CAB + 4] = colv(inp['conv_a_b'][l])
        cols[:, b + C_RBA:b + C_RBA + 4] = colv(inp['rg_ba'][l])
        cols[:, b + C_RBX:b + C_RBX + 4] = colv(inp['rg_bx'][l])
        cols[:, b + C_LAM:b + C_LAM + 4] = colv(inp['rg_lambda'][l])
        cbw = f(inp['conv_b_w'][l])
        for c in range(6):
            for k in range(4):
                cols[:, b + C_CBW + c * 4 + k] = cbw[k, c * 128:(c + 1) * 128]
        cols[:, b + C_CBB:b + C_CBB + 6] = colv(inp['conv_b_b'][l])
        cols[:8, b + C_DTB] = f(inp['ssd_dt_bias'][l])
        cols[:, b + C_DEXP:b + C_DEXP + 4] = colv(np.repeat(f(inp['ssd_d'][l]), 64))
        cols[:, b + C_ALEXP:b + C_ALEXP + 4] = colv(np.repeat(f(inp['ssd_a_log'][l]), 64))
        cols[:, b + C_SNORMC:b + C_SNORMC + 4] = colv(inp['ssd_norm'][l])
        cols[:, b + C_HNORMC] = np.tile(f(inp['hg_norm'][l]), 2)
        r = l * LR
        rows[:, r + R_SNORM:r + R_SNORM + 512] = f(inp['ssd_norm'][l])[None, :]
        rows[:, r + R_HNORM:r + R_HNORM + 64] = f(inp['hg_norm'][l])[None, :]
        rows[:, r + R_SD:r + R_SD + 8] = f(inp['ssd_d'][l])[None, :]
        rows[:, r + R_ALOG:r + R_ALOG + 8] = f(inp['ssd_a_log'][l])[None, :]
    cols[:, C_NFIN:C_NFIN + 8] = colv(inp['norm_final'])
    cols[:, C_HLB0:C_HLB0 + 4] = colv(inp['hg_lower_bounds'][0])
    cols[:, C_HLB1:C_HLB1 + 4] = colv(inp['hg_lower_bounds'][1])
    rgw = np.zeros((DEPTH, 2, 4, 128, 128), np.float32)
    for l in range(DEPTH):
        for wi, nm in enumerate(('rg_wa', 'rg_wx')):
            w = f(inp[nm][l])
            for h in range(8):
                c, hh = h // 2, h % 2
                rgw[l, wi, c, hh * 64:(hh + 1) * 64, hh * 64:(hh + 1) * 64] = w[h]
    rgw = rgw.reshape(16, 128, 128)
    cst = make_consts()

    if 'nc' not in _NC_CACHE:
        _NC_CACHE['nc'] = build_program()
    nc = _NC_CACHE['nc']

    shared = dict(w_in=f(inp['w_in']), w_out=f(inp['w_out']), w_up=f(inp['w_up']), w_down=f(inp['w_down']),
                  w_pg=f(inp['w_ple_gate']), w_pp=f(inp['w_ple_proj']), rgw=rgw, cols=cols, rows=rows, cst=cst)
    in_maps = []
    for c in range(NCORES):
        sl = slice(c * NSAMP, (c + 1) * NSAMP)
        m = dict(shared)
        m['xp'] = f(x_prompt[c]); m['xs'] = f(x_sample[sl, 0])
        m['pp'] = f(inp['p_prompt'][:, c]); m['psm'] = f(inp['p_sample'][:, sl, 0])
        m['st_rc'] = f(inp['state_rg_conv'][:, sl]); m['st_rh'] = f(inp['state_rg_h'][:, sl])
        m['st_sc'] = f(inp['state_ssd_conv'][:, sl])
        m['st_ss'] = f(inp['state_ssd'][:, sl]).reshape(DEPTH, NSAMP, 512, 128)
        m['st_hs'] = f(inp['state_hgrn'][:, sl]).reshape(DEPTH, NSAMP, 512, 64)
        in_maps.append(m)
    res = run_bass_kernel_spmd(nc, in_maps, core_ids=list(range(NCORES)))
    R = res.results

    def cat(name, axis):
        return np.concatenate([np.asarray(r[name]) for r in R], axis=axis)

    def stk(name):
        return np.stack([np.asarray(r[name]) for r in R], axis=1)

    y_prompt = np.stack([np.asarray(r['y_p']) for r in R], 0)
    y_sample = cat('y_s', 0).reshape(NCORES * NSAMP, 1, D)
    rc_p = stk('o_rc_p'); rh_p = stk('o_rh_p'); sc_p = stk('o_sc_p')
    ss_p = stk('o_ss_p').reshape(DEPTH, NCORES, 8, 64, 128)
    hs_p = stk('o_hs_p').reshape(DEPTH, NCORES, 8, 64, 64)
    rc_s = cat('o_rc_s', 1); rh_s = cat('o_rh_s', 1); sc_s = cat('o_sc_s', 1)
    ss_s = cat('o_ss_s', 1).reshape(DEPTH, NCORES * NSAMP, 8, 64, 128)
    hs_s = cat('o_hs_s', 1).reshape(DEPTH, NCORES * NSAMP, 8, 64, 64)
    outs = (y_prompt, y_sample, rc_p, rh_p, sc_p, ss_p, hs_p, rc_s, rh_s, sc_s, ss_s, hs_s)
    return tuple(np.ascontiguousarray(o, dtype=np.float32) for o in outs)
```

```python
import contextlib
import numpy as np
import concourse.bass as bass
import concourse.mybir as mybir
from concourse.bass_utils import run_bass_kernel_spmd

F32 = mybir.dt.float32
BF16 = mybir.dt.bfloat16
ALU = mybir.AluOpType
AF = mybir.ActivationFunctionType
AX = mybir.AxisListType

NCORES = 8
D = 1024
SEQ = 2048
DEPTH = 2
NSAMP = 16
TPG = 1024
NSG = 8
NG = 2
TC = TPG + NSG
SGS = [(0, 512), (512, 512), (1024, NSG)]
EPS = 1e-6
DIN = 4360
DFF = 4096
BIGNEG = -30000.0

LC = 104
C_NMIX, C_NFFN, C_NPLE, C_CAW, C_CAB, C_RBA, C_RBX, C_LAM, C_CBW, C_CBB, C_DTB, C_DEXP, C_ALEXP = \
    0, 8, 16, 24, 40, 44, 48, 52, 56, 80, 86, 88, 92
C_SNORMC, C_HNORMC = 96, 100
C_NFIN, C_HLB0, C_HLB1 = 208, 216, 220
NCOLS = 224
LR = 592
R_SNORM, R_HNORM, R_SD, R_ALOG = 0, 512, 576, 584
NROWS = LR * 2
K_ID, K_ULE, K_ONES, K_NEGS, K_MASK2, K_I2, K_BLK, K_RMASK, K_E8 = 0, 128, 256, 384, 512, 640, 704, 832, 1856
NCONST = 2368

ENG = ['pe', 'act', 'dve', 'pool', 'sp']
BIG = 1 << 30


class Tile:
    def __init__(self, h, name, shape, base=0, ps=None, dsl=None, sloff=0.0, slscale=1.0, width=None):
        self.h = h
        self.name = name
        self.shape = list(shape)
        self.ps = ps if ps is not None else int(np.prod(shape[1:]))
        self.base = base
        self.dsl = dsl if dsl is not None else (0, BIG)
        self.sloff = sloff
        self.slscale = slscale
        self.width = width if width is not None else int(np.prod(shape[1:]))

    def v(self, off=0, dims=None, p0=0, n=None, sl=None):
        if n is None:
            n = self.shape[0] - p0
        if dims is None:
            dims = [[1, self.width - off]]
        ap = bass.AP(self.h, p0 * self.ps + self.base + off, [[self.ps, n]] + [list(d) for d in dims])
        return View(ap, self, sl if sl is not None else self.dsl)

    def csl(self, c0, c1):
        return (self.sloff + c0 * self.slscale, self.sloff + c1 * self.slscale)


class View:
    def __init__(self, ap, tile, sl):
        self.ap = ap
        self.tile = tile
        self.sl = sl


class Sched:
    def __init__(self, nc):
        self.nc = nc
        self.streams = {e: [] for e in ENG}
        self.count = {e: 0 for e in ENG}
        self.waited = {e: {} for e in ENG}
        self.recs = {}
        self.ndma = 0
        self.KD = 24
        self.dma_tokens = []
        self.out_tokens = []
        self.alias = {}
        self.nsw = 0
        self.NSWS = 8

    def _deps(self, eng, reads, writes):
        need = {}

        def add(tok, raw):
            sem, val, src = tok
            if src == eng and sem == eng:
                if eng == 'pe' or eng == 'sp':
                    return
                if not raw:
                    return
            if need.get(sem, 0) < val:
                need[sem] = val

        for v in reads:
            r = self.recs.setdefault(v.tile.name, {'w': [], 'r': []})
            for (lo, hi, tok) in r['w']:
                if lo < v.sl[1] and v.sl[0] < hi:
                    add(tok, True)
            if v.tile.name.startswith('ps'):
                for (lo, hi, tok) in r['r']:
                    if lo < v.sl[1] and v.sl[0] < hi and tok[2] != eng:
                        add(tok, False)
        for v in writes:
            r = self.recs.setdefault(v.tile.name, {'w': [], 'r': []})
            for (lo, hi, tok) in r['w']:
                if lo < v.sl[1] and v.sl[0] < hi:
                    add(tok, False)
            for (lo, hi, tok) in r['r']:
                if lo < v.sl[1] and v.sl[0] < hi:
                    add(tok, False)
        return need

    def _record(self, tok, reads, writes):
        for v in reads:
            r = self.recs[v.tile.name]
            r['r'] = [x for x in r['r'] if not (x[2][0] == tok[0] and v.sl[0] <= x[0] and x[1] <= v.sl[1])]
            r['r'].append((v.sl[0], v.sl[1], tok))
        for v in writes:
            r = self.recs[v.tile.name]
            r['w'] = [x for x in r['w'] if not (v.sl[0] <= x[0] and x[1] <= v.sl[1])]
            r['r'] = [x for x in r['r'] if not (v.sl[0] <= x[0] and x[1] <= v.sl[1])]
            r['w'].append((v.sl[0], v.sl[1], tok))

    def _emit(self, eng, need, fn, inc):
        waits = []
        for sem, val in need.items():
            if self.waited[eng].get(sem, 0) < val:
                self.waited[eng][sem] = val
                waits.append((sem, val))
        self.streams[eng].append((waits, fn, inc, None))

    def I(self, eng, method, *args, **kw):
        reads, writes = [], []
        for k, a in kw.items():
            if isinstance(a, View):
                (writes if k in ('out', 'accum_out', 'ap') else reads).append(a)
        if kw.pop('_rmw', False):
            pass
        need = self._deps(eng, reads, writes)
        self.count[eng] += 1
        tok = (eng, self.count[eng], eng)
        kw2 = {k: (a.ap if isinstance(a, View) else a) for k, a in kw.items()}

        def fn(e, method=method, args=args, kw2=kw2):
            return getattr(e, method)(*args, **kw2)

        self._emit(eng, need, fn, (eng, 1))
        self._record(tok, reads, writes)
        return tok

    def dma(self, eng, out, in_, is_output=False, **kw):
        reads = [in_] if isinstance(in_, View) else []
        writes = [out] if isinstance(out, View) else []
        need = self._deps(eng, reads, writes)
        n = self.ndma
        self.ndma += 1
        sem = 'dma%d' % (n % self.KD)
        val = 16 * (n // self.KD + 1)
        if n >= self.KD:
            ptok = self.dma_tokens[n - self.KD]
            if need.get(ptok[0], 0) < ptok[1]:
                need[ptok[0]] = ptok[1]
        tok = (sem, val, eng)
        self.dma_tokens.append(tok)
        o = out.ap if isinstance(out, View) else out
        i = in_.ap if isinstance(in_, View) else in_

        def fn(e, o=o, i=i, kw=kw):
            return e.dma_start(out=o, in_=i, **kw)

        self._emit(eng, need, fn, (sem, 16))
        self._record(tok, reads, writes)
        if is_output:
            self.out_tokens.append(tok)
        return tok

    def dma_sw(self, eng, out, in_, slot):
        reads = [in_] if isinstance(in_, View) else []
        writes = [out] if isinstance(out, View) else []
        need = self._deps(eng, reads, writes)
        key = 'sw%d' % self.nsw
        self.nsw += 1
        real = 'wsem%d' % slot
        self.alias[key] = real
        tok = (key, 16, eng)
        o = out.ap if isinstance(out, View) else out
        i = in_.ap if isinstance(in_, View) else in_

        def fn(e, o=o, i=i):
            return e.dma_start(out=o, in_=i)

        waits = []
        for sem, val in need.items():
            if self.waited[eng].get(sem, 0) < val:
                self.waited[eng][sem] = val
                waits.append((sem, val))
        self.streams[eng].append((waits, fn, (real, 16), real))
        self._record(tok, reads, writes)
        return tok

    def finish(self):
        need = {}
        for (sem, val, _) in self.out_tokens:
            need[sem] = max(need.get(sem, 0), val)
        waits = [(s, v) for s, v in need.items() if self.waited['sp'].get(s, 0) < v]
        self.streams['sp'].append((waits, None, None, None))

    def emit(self, stack):
        nc = self.nc
        sems = {}
        for e in ENG:
            sems[e] = stack.enter_context(nc.semaphore("s_" + e))
        for i in range(self.KD):
            sems['dma%d' % i] = stack.enter_context(nc.semaphore("s_dma%d" % i))
        for i in range(self.NSWS):
            sems['wsem%d' % i] = stack.enter_context(nc.semaphore("s_wsem%d" % i))
        alias = self.alias
        block = stack.enter_context(nc.Block())
        amap = {'pe': 'tensor', 'act': 'scalar', 'dve': 'vector', 'pool': 'gpsimd', 'sp': 'sync'}
        for e in ENG:
            stream = self.streams[e]

            def body(engine, stream=stream):
                for (waits, fn, inc, pre) in stream:
                    for (s, v) in waits:
                        engine.wait_ge(sems[alias.get(s, s)], v)
                    if pre is not None:
                        engine.sem_clear(sems[pre])
                    if fn is not None:
                        ins = fn(engine)
                        ins.then_inc(sems[inc[0]], inc[1])

            getattr(block, amap[e])(body)


def interleave(gens, width=2):
    it = iter(gens)
    active = []
    first = next(it, None)
    if first is not None:
        active.append(first)
    while active:
        for gnr in list(active):
            try:
                next(gnr)
            except StopIteration:
                active.remove(gnr)
        if len(active) < width:
            nxt = next(it, None)
            if nxt is not None:
                active.append(nxt)


def build_program():
    nc = bass.Bass("TRN2", target_bir_lowering=False)
    stack = contextlib.ExitStack()
    S = Sched(nc)

    def din(name, shape):
        return nc.dram_tensor(name, list(shape), F32, kind="ExternalInput")

    def dout(name, shape):
        return nc.dram_tensor(name, list(shape), F32, kind="ExternalOutput")

    xp = din("xp", [SEQ, D]); xs = din("xs", [NSAMP, D])
    pp = din("pp", [DEPTH, SEQ, 256]); psm = din("psm", [DEPTH, NSAMP, 256])
    st_rc = din("st_rc", [DEPTH, NSAMP, 3, 512]); st_rh = din("st_rh", [DEPTH, NSAMP, 512])
    st_sc = din("st_sc", [DEPTH, NSAMP, 3, 768]); st_ss = din("st_ss", [DEPTH, NSAMP, 512, 128])
    st_hs = din("st_hs", [DEPTH, NSAMP, 512, 64])
    w_in = din("w_in", [DEPTH, D, DIN]); w_out = din("w_out", [DEPTH, 1536, D])
    w_up = din("w_up", [DEPTH, D, DFF]); w_down = din("w_down", [DEPTH, DFF, D])
    w_pg = din("w_pg", [DEPTH, D, D]); w_pp = din("w_pp", [DEPTH, 256, D])
    rgw = din("rgw", [16, 128, 128])
    cols_d = din("cols", [128, NCOLS]); rows_d = din("rows", [128, NROWS]); cst_d = din("cst", [128, NCONST])

    y_p = dout("y_p", [SEQ, D]); y_s = dout("y_s", [NSAMP, D])
    o_rc_p = dout("o_rc_p", [DEPTH, 3, 512]); o_rh_p = dout("o_rh_p", [DEPTH, 512])
    o_sc_p = dout("o_sc_p", [DEPTH, 3, 768]); o_ss_p = dout("o_ss_p", [DEPTH, 512, 128])
    o_hs_p = dout("o_hs_p", [DEPTH, 512, 64])
    o_rc_s = dout("o_rc_s", [DEPTH, NSAMP, 3, 512]); o_rh_s = dout("o_rh_s", [DEPTH, NSAMP, 512])
    o_sc_s = dout("o_sc_s", [DEPTH, NSAMP, 3, 768]); o_ss_s = dout("o_ss_s", [DEPTH, NSAMP, 512, 128])
    o_hs_s = dout("o_hs_s", [DEPTH, NSAMP, 512, 64])

    def sb(name, shape, dt=F32):
        h = stack.enter_context(nc.sbuf_tensor(name, list(shape), dt))
        return Tile(h, name, shape)

    NWB = 3
    cst = sb("cstT", [128, NCONST]); cols = sb("colsT", [128, NCOLS]); rows = sb("rowsT", [128, NROWS])
    cbf = sb("cbf", [128, 640], BF16)
    der = sb("der", [128, 96])
    rgwb = sb("rgwb", [128, 16, 128], BF16)
    hT = sb("hT", [128, 8, TC]); uT = sb("uT", [128, 8, TC], BF16); ymT = sb("ymT", [128, 12, TC], BF16)
    wbs = [sb("wb%d" % i, [128, 8, 512], BF16) for i in range(NWB)]
    NBG = 10
    bigmem = sb("bigmem", [128, NBG * TC])
    bigs = [Tile(bigmem.h, "bigmem", [128, TC], base=i * TC, ps=NBG * TC, dsl=(i, i + 1), width=TC)
            for i in range(NBG)]
    xin = [bigs[9], sb("xin1", [128, 1024])]
    axs = sb("axs", [128, 4, NSG, 4]); cmp_ = sb("cmp", [128, 6, 24])
    rgc = sb("rgc", [128, DEPTH, 4, 3]); rgh = sb("rgh", [128, DEPTH, 4]); h0s = sb("h0s", [128, 4, NSG])
    pT = sb("pT", [128, 2, TC], BF16)
    zT = Tile(bigmem.h.bitcast(BF16), "bigmem", [128, 16, TC], ps=2 * NBG * TC, sloff=0.0, slscale=0.5,
              width=16 * TC)
    pss = []
    for i in range(4):
        h = stack.enter_context(nc.psum_tensor("ps%d" % i, [128, 1024], F32))
        pss.append(Tile(h, "ps%d" % i, [128, 1024]))
    psb = [Tile(t.h.bitcast(BF16), t.name, [128, 2048]) for t in pss]
    pctr = [0]
    ptouch = [0] * 8
    pclock = [0]

    please = [0] * 8

    def _touch(bank):
        t, hf = bank[0], bank[1]
        if len(bank) > 2 and please[t * 2 + hf] != bank[2]:
            raise RuntimeError("stale PSUM bank use: bank %d lease %d != %d" % (t * 2 + hf, bank[2], please[t * 2 + hf]))
        pclock[0] += 1
        ptouch[t * 2 + hf] = pclock[0]

    pheld = set()

    def phold(bank):
        pheld.add(bank[0] * 2 + bank[1])

    def prel(bank):
        pheld.discard(bank[0] * 2 + bank[1])

    def pbank():
        i = min((b for b in range(8) if b not in pheld), key=lambda b: ptouch[b])
        please[i] += 1
        bank = (i // 2, i % 2, please[i])
        _touch(bank)
        return bank

    def pbank2():
        t = min((k for k in range(4) if 2 * k not in pheld and 2 * k + 1 not in pheld),
                key=lambda k: max(ptouch[2 * k], ptouch[2 * k + 1]))
        please[2 * t] += 1; please[2 * t + 1] += 1
        b0 = (t, 0, please[2 * t]); b1 = (t, 1, please[2 * t + 1])
        _touch(b0); _touch(b1)
        return b0, b1

    def PS(bank, off=0, dims=None, p0=0, n=None):
        t, hf = bank[0], bank[1]
        _touch(bank)
        return pss[t].v(hf * 512 + off, dims if dims is not None else [[1, 512 - off]], p0=p0, n=n, sl=(hf, hf + 1))

    def PSB(bank, off=0, dims=None, p0=0, n=None):
        t, hf = bank[0], bank[1]
        _touch(bank)
        return psb[t].v(hf * 1024 + off, dims if dims is not None else [[1, 1024 - off]], p0=p0, n=n, sl=(hf, hf + 1))

    bctr = [0]

    def big(lo=0, hi=NBG):
        t = bigs[lo + bctr[0] % (hi - lo)]
        bctr[0] += 1
        return t

    def col(c, n=128):
        return cols.v(c, [[1, 1]], n=n)

    def dcol(c, n=128):
        return der.v(c, [[1, 1]], n=n)

    def CH(t, c, a=0, b=TC):
        W = t.shape[2]
        return t.v(c * W + a, [[1, b - a]], sl=t.csl(c, c + 1))

    ident = cst.v(K_ID, [[1, 128]])
    identb = cbf.v(0, [[1, 128]])
    onesb = cbf.v(256, [[1, 128]])
    ones_f = cst.v(K_ONES, [[1, 128]])
    ule_f = cst.v(K_ULE, [[1, 128]])

    S.dma('sp', cst.v(), cst_d.ap())
    S.dma('sp', cols.v(), cols_d.ap())
    S.dma('sp', rows.v(), rows_d.ap())
    S.dma('pool', rgwb.v(), rgw.ap().rearrange("a k m -> k a m"))
    S.I('act', 'activation', out=cbf.v(0, [[1, 384]]), in_=cst.v(0, [[1, 384]]), func=AF.Copy)
    S.I('act', 'activation', out=cbf.v(384, [[1, 128]]), in_=cst.v(K_BLK, [[1, 128]]), func=AF.Copy)
    S.I('act', 'activation', out=cbf.v(512, [[1, 128]]), in_=cst.v(K_NEGS, [[1, 128]]), func=AF.Copy)
    anegb = sb("anegb", [128, DEPTH, 8]); anegx = sb("anegx", [128, DEPTH, 4])
    TMW = 8500
    tm = sb("tm", [128, TMW])
    tm_bf = tm.h.bitcast(BF16)

    def tmt(off, n, bf=False):
        assert off + n <= TMW
        if bf:
            return Tile(tm_bf, "tm", [128, 2 * n], base=2 * off, ps=2 * TMW, dsl=(off, off + n), width=2 * n)
        return Tile(tm.h, "tm", [128, n], base=off, ps=TMW, dsl=(off, off + n), width=n)

    xabf = tmt(0, 516, bf=True)
    sqb = [tmt(520, 516, bf=True), tmt(1040, 516, bf=True)]
    rstd = tmt(1560, TC)
    ssc = sb("ssc", [128, DEPTH, 6, 3]); xbs = sb("xbs", [128, 6, NSG, 4])
    Sst = sb("Sst", [128, DEPTH, 512])
    Shg = sb("Shg", [128, DEPTH, 256]); hsm = sb("hsm", [128, 16, NSG]); Dd = sb("Dd", [128, 4, 16])
    S.I('dve', 'memset', ap=Shg.v(), constant=0.0)
    S.I('dve', 'memset', ap=ssc.v(), constant=0.0)
    S.I('dve', 'memset', ap=Sst.v(), constant=0.0)
    bigbf = bigmem.h.bitcast(BF16)

    def bigbf_tile(i0, nchunks):
        return Tile(bigbf, "bigmem", [128, nchunks, TC], base=2 * i0 * TC, ps=2 * NBG * TC, sloff=float(i0),
                    slscale=0.5, width=nchunks * TC, dsl=(i0, i0 + nchunks / 2.0))

    for l in range(DEPTH):
        lam = cols.v(l * LC + C_LAM, [[1, 4]])
        t1 = der.v(48, [[1, 4]])
        S.I('act', 'activation', out=t1, in_=lam, func=AF.Exp, scale=-1.0)
        S.I('act', 'activation', out=t1, in_=t1, func=AF.Ln, bias=1.0)
        S.I('dve', 'tensor_scalar', out=der.v(l * 16, [[1, 4]]), in0=t1, scalar1=-4.0, scalar2=None, op0=ALU.mult)
        S.I('dve', 'tensor_scalar', out=der.v(l * 16 + 4, [[1, 4]]), in0=t1, scalar1=-8.0, scalar2=None, op0=ALU.mult)
        S.I('dve', 'tensor_scalar', out=der.v(l * 16 + 8, [[1, 4]]), in0=cols.v(l * LC + C_RBA, [[1, 4]]), scalar1=0.5,
            scalar2=None, op0=ALU.mult)
        S.I('dve', 'tensor_scalar', out=der.v(l * 16 + 12, [[1, 4]]), in0=cols.v(l * LC + C_RBX, [[1, 4]]), scalar1=0.5,
            scalar2=None, op0=ALU.mult)
        S.I('act', 'activation', out=anegb.v(l * 8, [[1, 8]]), in_=rows.v(l * LR + R_ALOG, [[1, 8]]), func=AF.Exp)
        S.I('dve', 'tensor_scalar', out=anegb.v(l * 8, [[1, 8]]), in0=anegb.v(l * 8, [[1, 8]]), scalar1=-1.0,
            scalar2=None, op0=ALU.mult)
        S.I('act', 'activation', out=anegx.v(l * 4, [[1, 4]]), in_=cols.v(l * LC + C_ALEXP, [[1, 4]]), func=AF.Exp)
        S.I('dve', 'tensor_scalar', out=anegx.v(l * 4, [[1, 4]]), in0=anegx.v(l * 4, [[1, 4]]), scalar1=-1.0,
            scalar2=None, op0=ALU.mult)
    S.I('dve', 'memset', ap=der.v(32, [[1, 4]]), constant=0.0)
    S.I('dve', 'memset', ap=der.v(36, [[1, 4]]), constant=1.0)
    S.I('dve', 'tensor_tensor', out=der.v(52, [[1, 4]]), in0=cols.v(C_HLB1, [[1, 4]]), in1=cols.v(C_HLB0, [[1, 4]]),
        op=ALU.subtract)
    S.I('act', 'activation', out=der.v(40, [[1, 4]]), in_=der.v(52, [[1, 4]]), func=AF.Sigmoid)
    S.I('dve', 'tensor_scalar', out=der.v(44, [[1, 4]]), in0=der.v(40, [[1, 4]]), scalar1=-1.0, scalar2=1.0,
        op0=ALU.mult, op1=ALU.add)
    for l in range(DEPTH):
        S.I('dve', 'tensor_scalar', out=der.v(64 + l * 8, [[1, 4]]), in0=der.v(36 + l * 8, [[1, 4]]), scalar1=0.5,
            scalar2=None, op0=ALU.mult)
        S.I('dve', 'tensor_tensor', out=der.v(64 + l * 8 + 4, [[1, 4]]), in0=der.v(64 + l * 8, [[1, 4]]),
            in1=der.v(32 + l * 8, [[1, 4]]), op=ALU.add)
    S.I('dve', 'memset', ap=rgc.v(), constant=0.0)
    S.I('dve', 'memset', ap=rgh.v(), constant=0.0)

    wq = []

    def wpiece(dt_, l, r0, kc, c0, ncols):
        src = dt_[l, r0:r0 + kc * 128, c0:c0 + ncols].rearrange("(k p) n -> p k n", p=128)
        wq.append((src, kc, ncols))
        return len(wq) - 1

    wissued = [0]

    def wget(i):
        while wissued[0] < len(wq) and wissued[0] <= i + NWB - 2:
            j = wissued[0]
            src, kc, ncols = wq[j]
            t = wbs[j % NWB]
            S.dma('pool', t.v(0, [[512, kc], [1, ncols]]), src)
            wissued[0] += 1
        return wbs[i % NWB]

    plan = []
    for g in range(NG):
        for l in range(DEPTH):
            for c0, n in [(0, 512), (512, 512), (1024, 512), (1536, 512), (2048, 256), (2304, 8),
                          (2824, 512), (2312, 512), (3336, 512), (3848, 512)]:
                plan.append(wpiece(w_in, l, 0, 8, c0, n))
            for nh in range(2):
                plan.append(wpiece(w_out, l, 0, 8, nh * 512, 512))
                plan.append(wpiece(w_out, l, 1024, 4, nh * 512, 512))
            for G in range(2):
                for q in range(4):
                    plan.append(wpiece(w_up, l, 0, 8, G * 2048 + q * 512, 512))
                for nh in range(2):
                    for rb in range(2):
                        plan.append(wpiece(w_down, l, G * 2048 + rb * 1024, 8, nh * 512, 512))
            for nh in range(2):
                plan.append(wpiece(w_pg, l, 0, 8, nh * 512, 512))
            for nh in range(2):
                plan.append(wpiece(w_pp, l, 0, 2, nh * 512, 512))
    wptr = [0]

    def wnext():
        i = wptr[0]
        wptr[0] += 1
        return wget(i)

    def WV(t, kc, m0, M):
        return t.v(kc * 512 + m0, [[1, M]])

    def transpose_in(src_rows_ap, nrows, dst_tile, dst_cols0, nchunks, xt, xoff=0, dma=True, flip=0):
        if dma:
            S.dma('sp', xt.v(xoff, [[1, nchunks * 128]], n=nrows), src_rows_ap)
        for c0 in range(0, nchunks, 4):
            ncc = min(4, nchunks - c0)
            bk = pbank()
            for c in range(ncc):
                S.I('pe', 'transpose', out=PS(bk, c * 128, [[1, nrows]]),
                    in_=xt.v(xoff + (c0 + c) * 128, [[1, 128]], n=nrows), identity=cst.v(K_ID, [[1, nrows]], n=nrows))
            W = dst_tile.shape[2]
            eng = 'act' if (c0 // 4 + flip) % 2 == 0 else 'dve'
            outv = dst_tile.v(c0 * W + dst_cols0, [[W, ncc], [1, nrows]], sl=(c0, c0 + ncc))
            inv = PS(bk, 0, [[128, ncc], [1, nrows]])
            if eng == 'act':
                S.I('act', 'activation', out=outv, in_=inv, func=AF.Copy)
            else:
                S.I('dve', 'tensor_copy', out=outv, in_=inv)

    def norm(gcol0, out_tile, nchunks=8, src=None, in_place=False, dim=1024.0):
        src = src or hT
        bks = [pbank() for _ in SGS]
        for c in range(nchunks):
            sq = sqb[c % 2]
            S.I('act', 'activation', out=sq.v(), in_=CH(src, c), func=AF.Square)
            for si, (a, n) in enumerate(SGS):
                S.I('pe', 'matmul', out=PS(bks[si], 0, [[1, n]]), lhsT=onesb, rhs=sq.v(a, [[1, n]]),
                    start=(c == 0), stop=(c == nchunks - 1))
        for si, (a, n) in enumerate(SGS):
            S.I('act', 'activation', out=rstd.v(a, [[1, n]]), in_=PS(bks[si], 0, [[1, n]]), func=AF.Ln,
                scale=1.0 / dim, bias=EPS)
        S.I('act', 'activation', out=rstd.v(), in_=rstd.v(), func=AF.Exp, scale=-0.5)
        for c in range(nchunks):
            S.I('dve', 'scalar_tensor_tensor', out=CH(out_tile, c), in0=CH(src, c), scalar=col(gcol0 + c),
                in1=rstd.v(), op0=ALU.mult, op1=ALU.mult)

    def dense(wt, nk, m0, M, rhs_tile, consume, kc0=0, first=True, last=True, banks=None, sgs=None):
        sgs = sgs or SGS
        bks = banks or [pbank() for _ in sgs]
        for kc in range(nk):
            for si, (a, n) in enumerate(sgs):
                S.I('pe', 'matmul', out=PS(bks[si], 0, [[1, n]], n=M), lhsT=WV(wt, kc, m0, M),
                    rhs=CH(rhs_tile, kc0 + kc, a, a + n), start=(first and kc == 0), stop=(last and kc == nk - 1))
        if last and consume is not None:
            for si, (a, n) in enumerate(sgs):
                consume(si, a, n, bks[si])
        return bks

    for g in range(NG):
        t_base = g * TPG
        s_base = g * NSG
        for i in range(8):
            S.dma('sp', bigs[i].v(0, [[1, 1024]]), xp[t_base + i * 128: t_base + (i + 1) * 128, :])
        for i in range(8):
            transpose_in(None, 128, hT, i * 128, 8, bigs[i], dma=False)
        transpose_in(xs[s_base:s_base + NSG, :], NSG, hT, TPG, 8, bigs[8])

        for l in range(DEPTH):
            cb = l * LC
            pbufs = [xin[1], bigs[9]]
            for hb in range(2):
                S.dma('sp', pbufs[hb].v(0, [[256, 4], [1, 256]]),
                      pp[l, t_base + hb * 512: t_base + (hb + 1) * 512, :].rearrange("(i p) d -> p i d", p=128))
            for i in range(8):
                transpose_in(None, 128, pT, i * 128, 2, pbufs[i // 4], xoff=(i % 4) * 256, dma=False, flip=i % 2)
            transpose_in(psm[l, s_base:s_base + NSG, :], NSG, pT, TPG, 2, bigs[8])

            norm(cb + C_NMIX, uT)

            wax = wnext()
            S.I('dve', 'tensor_copy', out=bigmem.v(0, [[TC, 4], [1, 3]], sl=(0, 4)), in_=rgc.v(l * 12, [[3, 4], [1, 3]]))
            xt = xin[1]
            S.dma('sp', xt.v(0, [[1, 512]], n=3 * NSG),
                  st_rc[l, s_base:s_base + NSG].rearrange("b k c -> (b k) c"))
            bk = pbank()
            for c in range(4):
                S.I('pe', 'transpose', out=PS(bk, c * 32, [[1, 24]]), in_=xt.v(c * 128, [[1, 128]], n=24),
                    identity=cst.v(K_ID, [[1, 24]], n=24))
            S.I('dve', 'tensor_copy', out=axs.v(0, [[NSG * 4, 4], [4, NSG], [1, 3]]),
                in_=PS(bk, 0, [[32, 4], [3, NSG], [1, 3]]))
            S.dma('sp', xt.v(0, [[1, 512]], n=NSG), st_rh[l, s_base:s_base + NSG, :])
            bk = pbank()
            for c in range(4):
                S.I('pe', 'transpose', out=PS(bk, c * 8, [[1, NSG]]), in_=xt.v(c * 128, [[1, 128]], n=NSG),
                    identity=cst.v(K_ID, [[1, NSG]], n=NSG))
            S.I('dve', 'tensor_copy', out=h0s.v(), in_=PS(bk, 0, [[1, 4 * NSG]]))

            for c in range(4):
                def cons_ax(si, a, n, bank, c=c):
                    if si < 2:
                        S.I('act', 'activation', out=bigs[c].v(3 + a, [[1, n]]),
                            in_=PS(bank, 0, [[1, n]]), func=AF.Copy)
                    else:
                        S.I('act', 'activation', out=axs.v(c * NSG * 4 + 3, [[4, NSG]]), in_=PS(bank, 0, [[1, n]]),
                            func=AF.Copy)
                dense(wax, 8, c * 128, 128, uT, cons_ax)
            S.I('dve', 'tensor_copy', out=rgc.v(l * 12, [[3, 4], [1, 3]]), in_=bigmem.v(TPG, [[TC, 4], [1, 3]], sl=(0, 4)))
            wag = wnext()
            def HV(t, a, n):
                lo = t.dsl[0] + (0.0 if a < 512 else 0.5)
                hi = t.dsl[0] + (0.5 if a + n <= 512 else 1.0)
                return t.v(a, [[1, n]], sl=(lo, hi))

            def rg_unit(c, hf):
                xa, r_, i_, a_, m_, hh = bigs[4:10]
                a0, nn = (0, 512) if hf == 0 else (512, 512 + NSG)
                npr = 512
                sgs_h = [SGS[0]] if hf == 0 else [SGS[1], SGS[2]]
                S.I('dve', 'tensor_scalar', out=HV(xa, a0, npr), in0=bigs[c].v(a0, [[1, npr]]),
                    scalar1=col(cb + C_CAW + c * 4), scalar2=col(cb + C_CAB + c), op0=ALU.mult, op1=ALU.add)
                for k in range(1, 4):
                    S.I('dve', 'scalar_tensor_tensor', out=HV(xa, a0, npr),
                        in0=bigs[c].v(a0 + k, [[1, npr]]), scalar=col(cb + C_CAW + c * 4 + k),
                        in1=HV(xa, a0, npr), op0=ALU.mult, op1=ALU.add)
                if hf == 1:
                    S.I('dve', 'tensor_scalar', out=HV(xa, TPG, NSG), in0=axs.v(c * NSG * 4, [[4, NSG]]),
                        scalar1=col(cb + C_CAW + c * 4), scalar2=col(cb + C_CAB + c), op0=ALU.mult, op1=ALU.add)
                    for k in range(1, 4):
                        S.I('dve', 'scalar_tensor_tensor', out=HV(xa, TPG, NSG),
                            in0=axs.v(c * NSG * 4 + k, [[4, NSG]]), scalar=col(cb + C_CAW + c * 4 + k),
                            in1=HV(xa, TPG, NSG), op0=ALU.mult, op1=ALU.add)
                xab = xabf.v(a0, [[1, nn]], sl=(xabf.dsl[0] + hf * 258, xabf.dsl[0] + (hf + 1) * 258))
                S.I('act', 'activation', out=xab, in_=HV(xa, a0, nn), func=AF.Copy)
                yield
                for which, dst, bcol in ((0, r_, C_RBA), (1, i_, C_RBX)):
                    for (a, n) in sgs_h:
                        bk = pbank()
                        S.I('pe', 'matmul', out=PS(bk, 0, [[1, n]]),
                            lhsT=rgwb.v(((l * 2 + which) * 4 + c) * 128, [[1, 128]]),
                            rhs=xabf.v(a, [[1, n]], sl=(xabf.dsl[0] + hf * 258, xabf.dsl[0] + (hf + 1) * 258)),
                            start=True, stop=True)
                        S.I('act', 'activation', out=HV(dst, a, n), in_=PS(bk, 0, [[1, n]]), func=AF.Tanh,
                            scale=0.5, bias=dcol(l * 16 + (8 if which == 0 else 12) + c))
                S.I('act', 'activation', out=HV(a_, a0, nn), in_=HV(r_, a0, nn), func=AF.Exp,
                    scale=dcol(l * 16 + c), bias=dcol(l * 16 + c))
                S.I('act', 'activation', out=HV(m_, a0, nn), in_=HV(r_, a0, nn), func=AF.Exp,
                    scale=dcol(l * 16 + 4 + c), bias=dcol(l * 16 + 4 + c))
                S.I('act', 'activation', out=HV(m_, a0, nn), in_=HV(m_, a0, nn), func=AF.Ln, scale=-1.0, bias=1.0)
                S.I('act', 'activation', out=HV(m_, a0, nn), in_=HV(m_, a0, nn), func=AF.Exp, scale=0.5,
                    bias=-0.6931471805599453)
                yield
                if g == 0 and hf == 0:
                    S.I('dve', 'memset', ap=HV(m_, 0, 1), constant=0.5)
                S.I('dve', 'scalar_tensor_tensor', out=HV(i_, a0, nn), in0=HV(i_, a0, nn), scalar=1.0,
                    in1=HV(xa, a0, nn), op0=ALU.add, op1=ALU.mult)
                S.I('dve', 'tensor_tensor', out=HV(i_, a0, nn), in0=HV(i_, a0, nn), in1=HV(m_, a0, nn), op=ALU.mult)
                init = rgh.v(l * 4 + c, [[1, 1]]) if hf == 0 else HV(hh, 511, 1)
                S.I('dve', 'tensor_tensor_scan', out=HV(hh, a0, npr), data0=HV(a_, a0, npr),
                    data1=HV(i_, a0, npr), initial=init, op0=ALU.mult, op1=ALU.add)
                if hf == 1:
                    S.I('dve', 'tensor_copy', out=rgh.v(l * 4 + c, [[1, 1]]), in_=HV(hh, TPG - 1, 1))
                    S.I('dve', 'tensor_tensor', out=HV(hh, TPG, NSG), in0=HV(a_, TPG, NSG),
                        in1=h0s.v(c * NSG, [[1, NSG]]), op=ALU.mult)
                    S.I('dve', 'tensor_tensor', out=HV(hh, TPG, NSG), in0=HV(hh, TPG, NSG),
                        in1=HV(i_, TPG, NSG), op=ALU.add)
                    S.I('dve', 'tensor_copy', out=h0s.v(c * NSG, [[1, NSG]]), in_=HV(hh, TPG, NSG))

                yield
                def cons_ag(si, a, n, bank, c=c, hh=hh, xa=xa, r_=r_):
                    u1 = HV(xa, a, n)
                    u3 = HV(r_, a, n)
                    S.I('act', 'activation', out=u1, in_=PS(bank, 0, [[1, n]]), func=AF.Square,
                        scale=0.21145921592590907)
                    S.I('dve', 'scalar_tensor_tensor', out=u3, in0=u1, scalar=1.0, in1=PS(bank, 0, [[1, n]]),
                        op0=ALU.add, op1=ALU.mult)
                    S.I('act', 'activation', out=u3, in_=u3, func=AF.Tanh, scale=0.7978845608028654)
                    S.I('dve', 'scalar_tensor_tensor', out=u3, in0=u3, scalar=1.0, in1=PS(bank, 0, [[1, n]]),
                        op0=ALU.add, op1=ALU.mult)
                    S.I('dve', 'scalar_tensor_tensor', out=CH(ymT, c, a, a + n), in0=u3, scalar=0.5,
                        in1=HV(hh, a, n), op0=ALU.mult, op1=ALU.mult)
                dense(wag, 8, c * 128, 128, uT, cons_ag, sgs=sgs_h)

            interleave((rg_unit(c, hf) for c in range(4) for hf in range(2)), width=2)
            bk = pbank()
            S.I('dve', 'tensor_copy', out=cmp_.v(0, [[24, 4], [3, NSG], [1, 3]]),
                in_=axs.v(1, [[NSG * 4, 4], [4, NSG], [1, 3]]))
            for c in range(4):
                S.I('pe', 'transpose', out=PS(bk, c * 128, [[1, 128]], n=24),
                    in_=cmp_.v(c * 24, [[1, 24]]), identity=ident)
            xo = big(8, 10)
            S.I('act', 'activation', out=xo.v(0, [[1, 512]], n=24), in_=PS(bk, 0, [[1, 512]], n=24), func=AF.Copy)
            S.dma('sp', o_rc_s[l, s_base:s_base + NSG].rearrange("b k c -> (b k) c"), xo.v(0, [[1, 512]], n=24),
                  is_output=True)
            bk = pbank()
            for c in range(4):
                S.I('pe', 'transpose', out=PS(bk, c * 128, [[1, 128]], n=NSG), in_=h0s.v(c * NSG, [[1, NSG]]),
                    identity=ident)
            xo = big(8, 10)
            S.I('act', 'activation', out=xo.v(0, [[1, 512]], n=NSG), in_=PS(bk, 0, [[1, 512]], n=NSG), func=AF.Copy)
            S.dma('sp', o_rh_s[l, s_base:s_base + NSG, :], xo.v(0, [[1, 512]], n=NSG), is_output=True)
            if g == NG - 1:
                bk = pbank()
                for c in range(4):
                    S.I('pe', 'transpose', out=PS(bk, c * 128, [[1, 128]], n=3), in_=rgc.v(l * 12 + c * 3, [[1, 3]]),
                        identity=ident)
                xo = big(8, 10)
                S.I('act', 'activation', out=xo.v(0, [[1, 512]], n=3), in_=PS(bk, 0, [[1, 512]], n=3), func=AF.Copy)
                S.dma('sp', o_rc_p[l], xo.v(0, [[1, 512]], n=3), is_output=True)
                bk = pbank()
                for c in range(4):
                    S.I('pe', 'transpose', out=PS(bk, c * 128, [[1, 128]], n=1), in_=rgh.v(l * 4 + c, [[1, 1]]),
                        identity=ident)
                xo = big(8, 10)
                S.I('act', 'activation', out=xo.v(0, [[1, 512]], n=1), in_=PS(bk, 0, [[1, 512]], n=1), func=AF.Copy)
                S.dma('sp', o_rh_p[l:l + 1, :], xo.v(0, [[1, 512]], n=1), is_output=True)


            zs = bigbf_tile(7, 4)
            xbf = bigbf_tile(9, 2)
            dtT = tmt(0, TC)
            raw2 = tmt(TC, TC)
            wz = wnext()
            for c in range(4):
                def cons_zs(si, a, n, bank, c=c):
                    S.I('act', 'activation', out=CH(zs, c, a, a + n), in_=PS(bank, 0, [[1, n]]), func=AF.Silu)
                dense(wz, 8, c * 128, 128, uT, cons_zs)
            xt = xin[1]
            S.dma('sp', xt.v(0, [[1, 768]], n=3 * NSG),
                  st_sc[l, s_base:s_base + NSG].rearrange("b k c -> (b k) c"))
            bk = pbank()
            for c in range(6):
                S.I('pe', 'transpose', out=PS(bk, c * 32, [[1, 24]]), in_=xt.v(c * 128, [[1, 128]], n=24),
                    identity=cst.v(K_ID, [[1, 24]], n=24))
            S.I('dve', 'tensor_copy', out=xbs.v(0, [[NSG * 4, 6], [4, NSG], [1, 3]]),
                in_=PS(bk, 0, [[32, 6], [3, NSG], [1, 3]]))
            wx1 = wnext(); wx2 = wnext()
            for c in range(6):
                raw = bigs[6] if c % 2 == 0 else raw2
                wt_, mc = (wx1, c) if c < 4 else (wx2, c - 4)
                S.I('dve', 'tensor_copy', out=raw.v(0, [[1, 3]]), in_=ssc.v((l * 6 + c) * 3, [[1, 3]]))

                def cons_xb(si, a, n, bank, c=c, raw=raw):
                    if si < 2:
                        S.I('act', 'activation', out=raw.v(3 + a, [[1, n]]), in_=PS(bank, 0, [[1, n]]), func=AF.Copy)
                    else:
                        S.I('act', 'activation', out=xbs.v(c * NSG * 4 + 3, [[4, NSG]]), in_=PS(bank, 0, [[1, n]]),
                            func=AF.Copy)
                dense(wt_, 8, mc * 128, 128, uT, cons_xb)
                S.I('dve', 'tensor_copy', out=ssc.v((l * 6 + c) * 3, [[1, 3]]), in_=raw.v(TPG, [[1, 3]]))
                xo_ = bigs[c]
                wc0 = cb + C_CBW + c * 4
                S.I('dve', 'tensor_scalar', out=xo_.v(0, [[1, TPG]]), in0=raw.v(0, [[1, TPG]]),
                    scalar1=col(wc0), scalar2=col(cb + C_CBB + c), op0=ALU.mult, op1=ALU.add)
                for k in range(1, 4):
                    S.I('dve', 'scalar_tensor_tensor', out=xo_.v(0, [[1, TPG]]), in0=raw.v(k, [[1, TPG]]),
                        scalar=col(wc0 + k), in1=xo_.v(0, [[1, TPG]]), op0=ALU.mult, op1=ALU.add)
                S.I('dve', 'tensor_scalar', out=xo_.v(TPG, [[1, NSG]]), in0=xbs.v(c * NSG * 4, [[4, NSG]]),
                    scalar1=col(wc0), scalar2=col(cb + C_CBB + c), op0=ALU.mult, op1=ALU.add)
                for k in range(1, 4):
                    S.I('dve', 'scalar_tensor_tensor', out=xo_.v(TPG, [[1, NSG]]),
                        in0=xbs.v(c * NSG * 4 + k, [[4, NSG]]), scalar=col(wc0 + k), in1=xo_.v(TPG, [[1, NSG]]),
                        op0=ALU.mult, op1=ALU.add)
                S.I('act', 'activation', out=xo_.v(), in_=xo_.v(), func=AF.Silu)
                if c >= 4:
                    S.I('act', 'activation', out=CH(xbf, c - 4), in_=xo_.v(), func=AF.Copy)
            S.I('dve', 'tensor_copy', out=cmp_.v(0, [[24, 6], [3, NSG], [1, 3]]),
                in_=xbs.v(1, [[NSG * 4, 6], [4, NSG], [1, 3]]))
            bka, bkb = pbank2()
            for c in range(6):
                bk_ = bka if c < 4 else bkb
                S.I('pe', 'transpose', out=PS(bk_, (c % 4) * 128, [[1, 128]], n=24), in_=cmp_.v(c * 24, [[1, 24]]),
                    identity=ident)
            xo = big(6, 7)
            S.I('act', 'activation', out=xo.v(0, [[1, 512]], n=24), in_=PS(bka, 0, [[1, 512]], n=24), func=AF.Copy)
            S.I('act', 'activation', out=xo.v(512, [[1, 256]], n=24), in_=PS(bkb, 0, [[1, 256]], n=24), func=AF.Copy)
            S.dma('sp', o_sc_s[l, s_base:s_base + NSG].rearrange("b k c -> (b k) c"), xo.v(0, [[1, 768]], n=24),
                  is_output=True)
            if g == NG - 1:
                bka, bkb = pbank2()
                for c in range(6):
                    bk_ = bka if c < 4 else bkb
                    S.I('pe', 'transpose', out=PS(bk_, (c % 4) * 128, [[1, 128]], n=3),
                        in_=ssc.v((l * 6 + c) * 3, [[1, 3]]), identity=ident)
                xo = raw2
                S.I('act', 'activation', out=xo.v(0, [[1, 512]], n=3), in_=PS(bka, 0, [[1, 512]], n=3), func=AF.Copy)
                S.I('act', 'activation', out=xo.v(512, [[1, 256]], n=3), in_=PS(bkb, 0, [[1, 256]], n=3),
                    func=AF.Copy)
                S.dma('sp', o_sc_p[l], xo.v(0, [[1, 768]], n=3), is_output=True)
            wdt = wnext()

            def cons_dt(si, a, n, bank):
                S.I('act', 'activation', out=dtT.v(a, [[1, n]], n=8), in_=PS(bank, 0, [[1, n]], n=8), func=AF.Exp,
                    bias=col(cb + C_DTB, n=8))
            dense(wdt, 8, 0, 8, uT, cons_dt)
            S.I('act', 'activation', out=dtT.v(0, [[1, TC]], n=8), in_=dtT.v(0, [[1, TC]], n=8), func=AF.Ln, bias=1.0)

            o0 = 2 * TC
            Ss = tmt(o0, 4096); t1 = tmt(o0 + 4096, 1024); dg = tmt(o0 + 5120, 1024)
            sm2 = tmt(o0 + 6144, 256)
            dte = sm2.v(0, [[1, 32]]); dec = sm2.v(32, [[1, 32]]); xdts = sm2.v(64, [[1, 32]])
            ys = sm2.v(96, [[1, 32]]); ygs = sm2.v(128, [[1, 32]]); rs8 = sm2.v(160, [[1, 8]])
            sq8 = sm2.v(192, [[1, 32]])
            for c in range(4):
                S.dma('sp', Ss.v(c * 1024, [[128, NSG], [1, 128]]),
                      st_ss[l, s_base:s_base + NSG, c * 128:(c + 1) * 128, :].rearrange("b p n -> p b n"))
            bk = pbank()
            for c in range(4):
                S.I('pe', 'matmul', out=PS(bk, c * 8, [[1, NSG]]), lhsT=cst.v(K_E8 + c * 128, [[1, 128]], n=8),
                    rhs=dtT.v(TPG, [[1, NSG]], n=8), start=True, stop=True)
            S.I('dve', 'tensor_copy', out=dte, in_=PS(bk, 0, [[1, 32]]))
            for c in range(4):
                S.I('act', 'activation', out=sm2.v(32 + c * 8, [[1, 8]]), in_=sm2.v(c * 8, [[1, 8]]), func=AF.Exp,
                    scale=anegx.v(l * 4 + c, [[1, 1]]))
                S.I('dve', 'tensor_tensor', out=sm2.v(64 + c * 8, [[1, 8]]), in0=sm2.v(c * 8, [[1, 8]]),
                    in1=bigs[c].v(TPG, [[1, NSG]]), op=ALU.mult)
            bcs = []
            for which in (4, 5):
                S.I('dve', 'tensor_tensor', out=dg.v(0, [[128, NSG], [1, 128]]), in0=cst.v(K_ID, [[0, NSG], [1, 128]]),
                    in1=bigs[which].v(TPG, [[1, NSG], [0, 128]]), op=ALU.mult)
                b2 = pbank2()
                for hf in range(2):
                    S.I('pe', 'matmul', out=PS(b2[hf]), lhsT=ones_f, rhs=dg.v(hf * 512, [[1, 512]]), start=True,
                        stop=True)
                bcs.append(b2)
            for c in range(4):
                for hf in range(2):
                    S.I('dve', 'tensor_tensor', out=t1.v(hf * 512, [[128, 4], [1, 128]]),
                        in0=PS(bcs[0][hf], 0, [[128, 4], [1, 128]]),
                        in1=sm2.v(64 + c * 8 + hf * 4, [[1, 4], [0, 128]]), op=ALU.mult)
                Sc = Ss.v(c * 1024, [[128, NSG], [1, 128]])
                S.I('dve', 'tensor_tensor', out=Sc, in0=Sc, in1=sm2.v(32 + c * 8, [[1, NSG], [0, 128]]), op=ALU.mult)
                S.I('dve', 'tensor_tensor', out=Sc, in0=Sc, in1=t1.v(0, [[128, NSG], [1, 128]]), op=ALU.add)
                S.dma('sp', o_ss_s[l, s_base:s_base + NSG, c * 128:(c + 1) * 128, :].rearrange("b p n -> p b n"),
                      Ss.v(c * 1024, [[128, NSG], [1, 128]]), is_output=True)
                for hf in range(2):
                    S.I('dve', 'tensor_tensor', out=t1.v(hf * 512, [[128, 4], [1, 128]]),
                        in0=PS(bcs[1][hf], 0, [[128, 4], [1, 128]]),
                        in1=Ss.v(c * 1024 + hf * 512, [[128, 4], [1, 128]]), op=ALU.mult)
                S.I('dve', 'tensor_reduce', out=sm2.v(96 + c * 8, [[1, NSG]]), in_=t1.v(0, [[128, NSG], [1, 128]]),
                    axis=AX.X, op=ALU.add)
                S.I('dve', 'scalar_tensor_tensor', out=sm2.v(96 + c * 8, [[1, NSG]]), in0=bigs[c].v(TPG, [[1, NSG]]),
                    scalar=col(cb + C_DEXP + c), in1=sm2.v(96 + c * 8, [[1, NSG]]), op0=ALU.mult, op1=ALU.add)
                S.I('dve', 'tensor_tensor', out=sm2.v(128 + c * 8, [[1, NSG]]), in0=sm2.v(96 + c * 8, [[1, NSG]]),
                    in1=CH(zs, c, TPG, TC), op=ALU.mult)
            S.I('dve', 'tensor_tensor', out=sq8, in0=ygs, in1=ygs, op=ALU.mult)
            bk = pbank()
            for c in range(4):
                S.I('pe', 'matmul', out=PS(bk, 0, [[1, NSG]]), lhsT=ones_f, rhs=sm2.v(192 + c * 8, [[1, NSG]]),
                    start=(c == 0), stop=(c == 3))
            S.I('act', 'activation', out=rs8, in_=PS(bk, 0, [[1, NSG]]), func=AF.Ln, scale=1.0 / 512, bias=EPS)
            S.I('act', 'activation', out=rs8, in_=rs8, func=AF.Exp, scale=-0.5)
            for c in range(4):
                S.I('dve', 'scalar_tensor_tensor', out=CH(ymT, 4 + c, TPG, TC), in0=sm2.v(128 + c * 8, [[1, NSG]]),
                    scalar=col(cb + C_SNORMC + c), in1=rs8, op0=ALU.mult, op1=ALU.mult)

            o1 = TC

            def sset(p):
                o = o1 + p * 3328
                return dict(R1h=tmt(o, 512, bf=True), R1l=tmt(o + 512, 512, bf=True), LM=tmt(o + 1024, 512, bf=True), smt=tmt(o + 1536, 128),
                            cbs=tmt(o + 1664, 64, bf=True), Btm=tmt(o + 1728, 64, bf=True),
                            xdt=tmt(o + 1792, 256, bf=True), xw=tmt(o + 2048, 256, bf=True),
                            xDb=tmt(o + 2304, 256, bf=True), yy=tmt(o + 2560, 512), ynb=tmt(o + 3072, 256, bf=True),
                            dth=tmt(o + 1536 + 96, 4, bf=True), dtl=tmt(o + 1536 + 104, 4, bf=True))
            ssets = [sset(0), sset(1)]
            junk = tmt(o1 + 6656, 256, bf=True)
            Sbf = [tmt(o1 + 6912, 256, bf=True), tmt(o1 + 7168, 256, bf=True)]
            Scur = Sst.v(l * 512, [[1, 512]])
            S.I('act', 'activation', out=Sbf[0].v(), in_=Scur, func=AF.Copy)
            ugt_f = cst.v(K_NEGS, [[1, 128]])

            def ssd_tile(j):
                t0 = j * 128
                Q = ssets[j % 2]
                R1h, R1l, LM, smt, cbs, Btm, xdt, xw, xDb, yy, ynb = (Q['R1h'], Q['R1l'], Q['LM'], Q['smt'], Q['cbs'], Q['Btm'], Q['xdt'],
                                                               Q['xw'], Q['xDb'], Q['yy'], Q['ynb'])
                dt_tm = smt.v(0, [[1, 8]]); dta = smt.v(8, [[1, 8]]); cum_sb = smt.v(16, [[1, 16]])
                ecum = smt.v(32, [[1, 8]]); etot = smt.v(40, [[1, 8]]); wdec = smt.v(48, [[1, 8]])
                w2 = smt.v(56, [[1, 8]]); ddv = smt.v(64, [[1, 8]]); ssq = smt.v(72, [[1, 1]])
                dth = Q['dth']; dtl = Q['dtl']
                bkX = pbank()
                phold(bkX)
                for c in range(4):
                    S.I('pe', 'transpose', out=PS(bkX, c * 128, [[1, 128]]), in_=bigs[c].v(t0, [[1, 128]]),
                        identity=ident)
                bkB = pbank()
                S.I('pe', 'transpose', out=PS(bkB, 0, [[1, 128]]), in_=bigs[4].v(t0, [[1, 128]]), identity=ident)
                S.I('pe', 'transpose', out=PS(bkB, 128, [[1, 8]]), in_=dtT.v(t0, [[1, 128]], n=8),
                    identity=cst.v(K_ID, [[1, 8]], n=8))
                S.I('dve', 'tensor_copy', out=dt_tm, in_=PS(bkB, 128, [[1, 8]]))
                S.I('dve', 'tensor_tensor', out=dta, in0=dt_tm, in1=anegb.v(l * 8, [[1, 8]]), op=ALU.mult)
                S.I('dve', 'tensor_copy', out=Btm.v(), in_=PS(bkB, 0, [[1, 128]]))
                S.I('dve', 'tensor_copy', out=dth.v(), in_=dta)
                S.I('dve', 'tensor_tensor', out=dtl.v(), in0=dta, in1=dth.v(), op=ALU.subtract)
                S.I('pool', 'tensor_tensor', out=R1h.v(0, [[128, 8], [1, 128]]), in0=cbf.v(128, [[0, 8], [1, 128]]),
                    in1=dth.v(0, [[1, 8], [0, 128]]), op=ALU.mult)
                S.I('dve', 'tensor_tensor', out=R1l.v(0, [[128, 8], [1, 128]]), in0=cbf.v(128, [[0, 8], [1, 128]]),
                    in1=dtl.v(0, [[1, 8], [0, 128]]), op=ALU.mult)
                yield
                b2 = pbank2()
                for hf in range(2):
                    S.I('pe', 'matmul', out=PS(b2[hf]), lhsT=cbf.v(512, [[1, 128]]), rhs=R1h.v(hf * 512, [[1, 512]]),
                        start=True, stop=False)
                    S.I('pe', 'matmul', out=PS(b2[hf]), lhsT=cbf.v(512, [[1, 128]]), rhs=R1l.v(hf * 512, [[1, 512]]),
                        start=False, stop=True)
                bkC = pbank()
                S.I('pe', 'matmul', out=PS(bkC, 0, [[1, 8]]), lhsT=ule_f, rhs=dta, start=True, stop=True)
                S.I('pe', 'matmul', out=PS(bkC, 8, [[1, 8]]), lhsT=ones_f, rhs=dta, start=True, stop=True)
                S.I('pe', 'matmul', out=PS(bkC, 128, [[1, 128]]), lhsT=CH(xbf, 0, t0, t0 + 128),
                    rhs=CH(xbf, 1, t0, t0 + 128), start=True, stop=True)
                for hf in range(2):
                    S.I('act', 'activation', out=LM.v(hf * 512, [[1, 512]]), in_=PS(b2[hf]), func=AF.Exp)
                S.I('dve', 'tensor_copy', out=cum_sb, in_=PS(bkC, 0, [[1, 16]]))
                S.I('dve', 'tensor_tensor', out=cbs.v(), in0=PS(bkC, 128, [[1, 128]]), in1=ule_f, op=ALU.mult)
                S.I('dve', 'tensor_tensor', out=ddv, in0=smt.v(24, [[1, 8]]), in1=smt.v(16, [[1, 8]]), op=ALU.subtract)
                S.I('act', 'activation', out=ecum, in_=smt.v(16, [[1, 8]]), func=AF.Exp)
                S.I('act', 'activation', out=etot, in_=smt.v(24, [[1, 8]]), func=AF.Exp)
                S.I('act', 'activation', out=wdec, in_=ddv, func=AF.Exp)
                S.I('dve', 'tensor_tensor', out=w2, in0=dt_tm, in1=wdec, op=ALU.mult)
                S.I('dve', 'tensor_tensor', out=LM.v(0, [[128, 8], [1, 128]]), in0=LM.v(0, [[128, 8], [1, 128]]),
                    in1=cbs.v(0, [[0, 8], [1, 128]]), op=ALU.mult)
                X3 = PS(bkX, 0, [[64, 8], [1, 64]])
                S.I('dve', 'tensor_tensor', out=xdt.v(0, [[64, 8], [1, 64]]), in0=X3, in1=smt.v(0, [[1, 8], [0, 64]]),
                    op=ALU.mult)
                S.I('dve', 'tensor_tensor', out=xw.v(0, [[64, 8], [1, 64]]), in0=X3, in1=smt.v(56, [[1, 8], [0, 64]]),
                    op=ALU.mult)
                S.I('dve', 'tensor_tensor', out=xDb.v(0, [[64, 8], [1, 64]]), in0=X3,
                    in1=rows.v(l * LR + R_SD, [[1, 8], [0, 64]]), op=ALU.mult)
                prel(bkX)
                yield
                bkY = pbank()
                S.I('pe', 'matmul', out=PS(bkY), lhsT=identb, rhs=xDb.v(), start=True, stop=False,
                    skip_group_check=True)
                for h in range(8):
                    S.I('pe', 'matmul', out=PS(bkY, h * 64, [[1, 64]]), lhsT=LM.v(h * 128, [[1, 128]]),
                        rhs=xdt.v(h * 64, [[1, 64]]), start=False, stop=(h == 7), skip_group_check=True)
                bkD = pbank()
                S.I('pe', 'matmul', out=PS(bkD), lhsT=Btm.v(), rhs=xw.v(), start=True, stop=True)
                bkYi = pbank()
                S.I('pe', 'matmul', out=PS(bkYi), lhsT=CH(xbf, 1, t0, t0 + 128), rhs=Sbf[j % 2].v(), start=True,
                    stop=True)
                S.I('dve', 'tensor_tensor', out=Sst.v(l * 512, [[64, 8], [1, 64]]), in0=Sst.v(l * 512, [[64, 8], [1, 64]]),
                    in1=smt.v(40, [[1, 8], [0, 64]]), op=ALU.mult)
                S.I('dve', 'tensor_tensor', out=Scur, in0=Scur, in1=PS(bkD), op=ALU.add)
                S.I('act', 'activation', out=Sbf[(j + 1) % 2].v(), in_=Scur, func=AF.Copy)
                S.I('dve', 'tensor_tensor', out=yy.v(0, [[64, 8], [1, 64]]), in0=PS(bkYi, 0, [[64, 8], [1, 64]]),
                    in1=smt.v(32, [[1, 8], [0, 64]]), op=ALU.mult)
                S.I('dve', 'tensor_tensor', out=yy.v(), in0=yy.v(), in1=PS(bkY), op=ALU.add)
                yield
                bkZ = pbank()
                for c in range(4):
                    S.I('pe', 'transpose', out=PSB(bkZ, c * 128, [[1, 128]]), in_=CH(zs, c, t0, t0 + 128),
                        identity=identb)
                S.I('dve', 'tensor_tensor', out=yy.v(), in0=yy.v(), in1=PSB(bkZ, 0, [[1, 512]]), op=ALU.mult)
                S.I('act', 'activation', out=junk.v(), in_=yy.v(), func=AF.Square, accum_out=ssq)
                S.I('act', 'activation', out=ssq, in_=ssq, func=AF.Ln, scale=1.0 / 512, bias=EPS)
                S.I('act', 'activation', out=ssq, in_=ssq, func=AF.Exp, scale=-0.5)
                S.I('dve', 'scalar_tensor_tensor', out=ynb.v(), in0=yy.v(), scalar=ssq,
                    in1=rows.v(l * LR + R_SNORM, [[1, 512]]), op0=ALU.mult, op1=ALU.mult)
                bkT = pbank()
                for c in range(4):
                    S.I('pe', 'transpose', out=PSB(bkT, c * 128, [[1, 128]]), in_=ynb.v(c * 128, [[1, 128]]),
                        identity=identb)
                S.I('act', 'activation', out=ymT.v(4 * TC + t0, [[TC, 4], [1, 128]], sl=(4, 8)),
                    in_=PSB(bkT, 0, [[128, 4], [1, 128]]), func=AF.Copy)

            interleave((ssd_tile(j) for j in range(8)), width=2)
            xD = tmt(o1 + 2560, 512)
            if g == NG - 1:
                bk = pbank()
                for c in range(4):
                    S.I('pe', 'transpose', out=PS(bk, c * 128, [[1, 128]]), in_=Sst.v(l * 512 + c * 128, [[1, 128]]),
                        identity=ident)
                S.I('act', 'activation', out=xD.v(), in_=PS(bk), func=AF.Copy)
                S.dma('sp', o_ss_p[l].rearrange("(c p) n -> p c n", p=128), xD.v(0, [[128, 4], [1, 128]]),
                      is_output=True)


            AqT = bigbf_tile(0, 4); BkT = bigbf_tile(2, 4); KdT = bigbf_tile(4, 4)
            vTb = bigbf_tile(6, 4); gsT = bigbf_tile(8, 4)
            tA = tmt(0, TC); tB = tmt(TC, TC); tC_ = tmt(2 * TC, TC)
            tE = [tmt((3 + c) * TC, TC) for c in range(4)]
            blk_f = cst.v(K_BLK, [[1, 128]])
            def HV2(t, a, n):
                w_ = (t.dsl[1] - t.dsl[0]) / 2.0
                lo = t.dsl[0] + (0.0 if a < 512 else w_)
                hi = t.dsl[0] + (w_ if a + n <= 512 else 2 * w_)
                return t.v(a, [[1, n]], sl=(lo, hi))

            wf_ = wnext()

            def hgf_unit(c, hf):
                a0, nn = (0, 512) if hf == 0 else (512, 512 + NSG)
                sgs_h = [SGS[0]] if hf == 0 else [SGS[1], SGS[2]]

                def cons_f(si, a, n, bank):
                    S.I('act', 'activation', out=HV2(tA, a, n), in_=PS(bank, 0, [[1, n]]), func=AF.Tanh, scale=0.5)
                dense(wf_, 8, c * 128, 128, uT, cons_f, sgs=sgs_h)
                yield
                S.I('dve', 'tensor_scalar', out=HV2(tA, a0, nn), in0=HV2(tA, a0, nn), scalar1=dcol(64 + l * 8 + c),
                    scalar2=dcol(64 + l * 8 + 4 + c), op0=ALU.mult, op1=ALU.add)
                if hf == 1:
                    S.I('dve', 'tensor_copy', out=hsm.v((4 + c) * NSG, [[1, NSG]]), in_=HV2(tA, TPG, NSG))
                S.I('act', 'activation', out=HV2(tB, a0, nn), in_=HV2(tA, a0, nn), func=AF.Ln)
                S.I('dve', 'tensor_scalar', out=HV2(tA, a0, nn), in0=HV2(tA, a0, nn), scalar1=-1.0, scalar2=1.0,
                    op0=ALU.mult, op1=ALU.add)
                if hf == 1:
                    S.I('dve', 'tensor_copy', out=hsm.v((8 + c) * NSG, [[1, NSG]]), in_=HV2(tA, TPG, NSG))
                yield
                S.I('dve', 'tensor_tensor_scan', out=HV2(tC_, a0, 512), data0=cst.v(K_RMASK + a0, [[1, 512]]),
                    data1=HV2(tB, a0, 512), initial=0.0, op0=ALU.mult, op1=ALU.add)
                S.I('act', 'activation', out=HV2(tE[c], a0, 512), in_=HV2(tC_, a0, 512), func=AF.Exp)
                S.I('act', 'activation', out=HV2(tB, a0, 512), in_=HV2(tC_, a0, 512), func=AF.Exp, scale=-1.0)
                yield
                S.I('dve', 'tensor_tensor', out=HV2(tC_, a0, 512), in0=HV2(tA, a0, 512), in1=HV2(tB, a0, 512),
                    op=ALU.mult)
                S.I('act', 'activation', out=CH(BkT, c, a0, a0 + 512), in_=HV2(tC_, a0, 512), func=AF.Copy)
                slh = (tC_.dsl[0] + hf * (TC / 2.0), tC_.dsl[0] + (hf + 1) * (TC / 2.0))
                sle = (tE[c].dsl[0] + hf * (TC / 2.0), tE[c].dsl[0] + (hf + 1) * (TC / 2.0))
                S.I('dve', 'tensor_tensor', out=KdT.v(c * TC + a0, [[64, 8], [1, 64]], sl=KdT.csl(c, c + 1)),
                    in0=tC_.v(a0, [[64, 8], [1, 64]], sl=slh), in1=tE[c].v(a0 + 63, [[64, 8], [0, 64]], sl=sle),
                    op=ALU.mult)
                S.I('dve', 'tensor_copy', out=Dd.v(c * 16 + hf * 8, [[1, 8]]), in_=tE[c].v(a0 + 63, [[64, 8]], sl=sle))

            interleave((hgf_unit(c, hf) for c in range(4) for hf in range(2)), width=2)
            wq_ = wnext()

            def hgq_unit(c, hf):
                a0, nn = (0, 512) if hf == 0 else (512, 512 + NSG)
                sgs_h = [SGS[0]] if hf == 0 else [SGS[1], SGS[2]]

                def cons_q(si, a, n, bank):
                    S.I('act', 'activation', out=HV2(tA, a, n), in_=PS(bank, 0, [[1, n]]), func=AF.Silu)
                dense(wq_, 8, c * 128, 128, uT, cons_q, sgs=sgs_h)
                yield
                if hf == 1:
                    S.I('dve', 'tensor_copy', out=hsm.v(c * NSG, [[1, NSG]]), in_=HV2(tA, TPG, NSG))
                S.I('dve', 'tensor_tensor', out=CH(AqT, c, a0, a0 + 512), in0=HV2(tA, a0, 512),
                    in1=HV2(tE[c], a0, 512), op=ALU.mult)

            interleave((hgq_unit(c, hf) for c in range(4) for hf in range(2)), width=2)
            wi_ = wnext()
            for c in range(4):
                def cons_v(si, a, n, bank, c=c):
                    if si < 2:
                        S.I('act', 'activation', out=CH(vTb, c, a, a + n), in_=PS(bank, 0, [[1, n]]), func=AF.Copy)
                    else:
                        S.I('act', 'activation', out=hsm.v((12 + c) * NSG, [[1, NSG]]), in_=PS(bank, 0, [[1, n]]),
                            func=AF.Copy)
                dense(wi_, 8, c * 128, 128, uT, cons_v)
            wg2 = wnext()
            for c in range(4):
                def cons_g2(si, a, n, bank, c=c):
                    S.I('act', 'activation', out=CH(gsT, c, a, a + n), in_=PS(bank, 0, [[1, n]]), func=AF.Silu)
                dense(wg2, 8, c * 128, 128, uT, cons_g2)

            Ssh = tmt(0, 2048); dgv = tmt(2048, 512); h1 = tmt(2560, 512); h2 = tmt(3072, 512)
            hs2 = tmt(3584, 128)
            for c in range(4):
                S.dma('sp', Ssh.v(c * 512, [[64, NSG], [1, 64]]),
                      st_hs[l, s_base:s_base + NSG, c * 128:(c + 1) * 128, :].rearrange("b p e -> p b e"))
            for c in range(4):
                S.I('dve', 'tensor_tensor', out=dgv.v(0, [[64, NSG], [1, 64]]), in0=cst.v(K_I2, [[0, NSG], [1, 64]]),
                    in1=hsm.v((12 + c) * NSG, [[1, NSG], [0, 64]]), op=ALU.mult)
                bkv = pbank()
                S.I('pe', 'matmul', out=PS(bkv), lhsT=blk_f, rhs=dgv.v(), start=True, stop=True)
                S.I('dve', 'tensor_tensor', out=h1.v(0, [[64, NSG], [1, 64]]), in0=PS(bkv, 0, [[64, NSG], [1, 64]]),
                    in1=hsm.v((8 + c) * NSG, [[1, NSG], [0, 64]]), op=ALU.mult)
                Sc = Ssh.v(c * 512, [[64, NSG], [1, 64]])
                S.I('dve', 'tensor_tensor', out=Sc, in0=Sc, in1=hsm.v((4 + c) * NSG, [[1, NSG], [0, 64]]), op=ALU.mult)
                S.I('dve', 'tensor_tensor', out=Sc, in0=Sc, in1=h1.v(0, [[64, NSG], [1, 64]]), op=ALU.add)
                S.dma('sp', o_hs_s[l, s_base:s_base + NSG, c * 128:(c + 1) * 128, :].rearrange("b p e -> p b e"),
                      Ssh.v(c * 512, [[64, NSG], [1, 64]]), is_output=True)
                S.I('dve', 'tensor_tensor', out=h2.v(0, [[64, NSG], [1, 64]]), in0=Sc,
                    in1=hsm.v(c * NSG, [[1, NSG], [0, 64]]), op=ALU.mult)
                bko = pbank()
                S.I('pe', 'matmul', out=PS(bko), lhsT=blk_f, rhs=h2.v(), start=True, stop=True)
                S.I('dve', 'tensor_tensor', out=h1.v(0, [[64, NSG], [1, 64]]), in0=PS(bko, 0, [[64, NSG], [1, 64]]),
                    in1=cst.v(K_I2, [[0, NSG], [1, 64]]), op=ALU.mult)
                osv = hs2.v(c * NSG, [[1, NSG]])
                S.I('dve', 'tensor_reduce', out=osv, in_=h1.v(0, [[64, NSG], [1, 64]]), axis=AX.X, op=ALU.add)
                sqv = hs2.v(32 + c * NSG, [[1, NSG]])
                S.I('dve', 'tensor_tensor', out=sqv, in0=osv, in1=osv, op=ALU.mult)
                bkn = pbank()
                S.I('pe', 'matmul', out=PS(bkn, 0, [[1, NSG]]), lhsT=blk_f, rhs=sqv, start=True, stop=True)
                rsv = hs2.v(64 + c * NSG, [[1, NSG]])
                S.I('act', 'activation', out=rsv, in_=PS(bkn, 0, [[1, NSG]]), func=AF.Ln, scale=1.0 / 64, bias=EPS)
                S.I('act', 'activation', out=rsv, in_=rsv, func=AF.Exp, scale=-0.5)
                S.I('dve', 'scalar_tensor_tensor', out=osv, in0=osv, scalar=col(cb + C_HNORMC), in1=rsv, op0=ALU.mult,
                    op1=ALU.mult)
                S.I('dve', 'tensor_tensor', out=CH(ymT, 8 + c, TPG, TC), in0=osv, in1=CH(gsT, c, TPG, TC), op=ALU.mult)

            HB = 8500 - 2 * 2624

            def hset(p):
                o = HB + p * 2624
                return dict(Kdtm=tmt(o, 256, bf=True), vtm=tmt(o + 256, 256, bf=True), attm=tmt(o + 512, 512, bf=True),
                            sqo=tmt(o + 1024, 512), onf=tmt(o + 1536, 512), on2=tmt(o + 2048, 256, bf=True),
                            Sbh=[tmt(o + 2304, 128, bf=True), tmt(o + 2432, 128, bf=True)], hq=tmt(o + 2560, 64))
            hsets = [hset(0), hset(1)]
            Sl = Shg.v(l * 256, [[1, 256]])

            def hg_tile(j):
                t0 = j * 128
                H = hsets[j % 2]
                Kdtm, vtm_, attm_, sqo, onf, on2, Sbh_, hq = (H['Kdtm'], H['vtm'], H['attm'], H['sqo'], H['onf'],
                                                             H['on2'], H['Sbh'], H['hq'])
                ss8 = hq.v(0, [[1, 8]])
                bkK = pbank()
                for c in range(4):
                    S.I('pe', 'transpose', out=PSB(bkK, c * 128, [[1, 128]]), in_=CH(KdT, c, t0, t0 + 128),
                        identity=identb)
                bkV = pbank()
                for c in range(4):
                    S.I('pe', 'transpose', out=PSB(bkV, c * 128, [[1, 128]]), in_=CH(vTb, c, t0, t0 + 128),
                        identity=identb)
                S.I('act', 'activation', out=Kdtm.v(), in_=PSB(bkK, 0, [[1, 512]]), func=AF.Copy)
                S.I('dve', 'tensor_copy', out=vtm_.v(), in_=PSB(bkV, 0, [[1, 512]]))
                yield
                b2 = pbank2()
                for hh in range(2):
                    for c in range(4):
                        S.I('pe', 'matmul', out=PS(b2[hh], c * 128, [[1, 128]]),
                            lhsT=BkT.v(c * TC + t0, [[1, 128]], p0=hh * 64, n=64, sl=BkT.csl(c, c + 1)),
                            rhs=AqT.v(c * TC + t0, [[1, 128]], p0=hh * 64, n=64, sl=AqT.csl(c, c + 1)),
                            start=(c == 0), stop=(c == 3), skip_group_check=True)
                for hh in range(2):
                    S.I('dve', 'tensor_tensor', out=attm_.v(hh * 128, [[256, 4], [1, 128]]),
                        in0=PS(b2[hh], 0, [[128, 4], [1, 128]]), in1=cst.v(K_MASK2, [[0, 4], [1, 128]]), op=ALU.mult)
                bS = pbank2()
                for jj in range(2):
                    for h in range(8):
                        c, hh = h // 2, h % 2
                        S.I('pe', 'matmul', out=PS(bS[jj], c * 64, [[1, 64]], p0=hh * 64, n=64),
                            lhsT=Kdtm.v(h * 64, [[1, 64]], p0=jj * 64, n=64),
                            rhs=vtm_.v(h * 64, [[1, 64]], p0=jj * 64, n=64), start=(h < 2),
                            stop=(h >= 6), skip_group_check=True)
                for jj in range(2):
                    S.I('act', 'activation', out=Sbh_[jj].v(), in_=Sl, func=AF.Copy)
                    for c in range(4):
                        S.I('dve', 'scalar_tensor_tensor', out=Shg.v(l * 256 + c * 64, [[1, 64]]),
                            in0=Shg.v(l * 256 + c * 64, [[1, 64]]), scalar=Dd.v(c * 16 + 2 * j + jj, [[1, 1]]),
                            in1=PS(bS[jj], c * 64, [[1, 64]]), op0=ALU.mult, op1=ALU.add)
                yield
                bO = pbank2()
                for hh in range(2):
                    for c in range(4):
                        h = 2 * c + hh
                        S.I('pe', 'matmul', out=PS(bO[hh], c * 64, [[1, 64]]), lhsT=attm_.v(h * 128, [[1, 128]]),
                            rhs=vtm_.v(h * 64, [[1, 64]]), start=(c == 0), stop=False, skip_group_check=True)
                for hh in range(2):
                    for jj in range(2):
                        for c in range(4):
                            S.I('pe', 'matmul', out=PS(bO[hh], c * 64, [[1, 64]], p0=jj * 64, n=64),
                                lhsT=AqT.v(c * TC + t0 + jj * 64, [[1, 64]], p0=hh * 64, n=64, sl=AqT.csl(c, c + 1)),
                                rhs=Sbh_[jj].v(c * 64, [[1, 64]], p0=hh * 64, n=64), start=False,
                                stop=(jj == 1 and c == 3), skip_group_check=True)
                for hh in range(2):
                    S.I('act', 'activation', out=sqo.v(hh * 256, [[1, 256]]), in_=PS(bO[hh], 0, [[1, 256]]),
                        func=AF.Square)
                S.I('dve', 'tensor_reduce', out=ss8, in_=sqo.v(0, [[64, 8], [1, 64]]), axis=AX.X, op=ALU.add)
                S.I('act', 'activation', out=ss8, in_=ss8, func=AF.Ln, scale=1.0 / 64, bias=EPS)
                S.I('act', 'activation', out=ss8, in_=ss8, func=AF.Exp, scale=-0.5)
                yield
                for hh in range(2):
                    S.I('dve', 'tensor_tensor', out=onf.v(hh * 256, [[64, 4], [1, 64]]),
                        in0=PS(bO[hh], 0, [[64, 4], [1, 64]]), in1=hq.v(hh * 4, [[1, 4], [0, 64]]), op=ALU.mult)
                    S.I('dve', 'tensor_tensor', out=on2.v(hh * 64, [[128, 4], [1, 64]]),
                        in0=onf.v(hh * 256, [[64, 4], [1, 64]]), in1=rows.v(l * LR + R_HNORM, [[0, 4], [1, 64]]),
                        op=ALU.mult)
                bkT = pbank()
                for c in range(4):
                    S.I('pe', 'transpose', out=PSB(bkT, c * 128, [[1, 128]]), in_=on2.v(c * 128, [[1, 128]]),
                        identity=identb)
                S.I('dve', 'tensor_tensor', out=ymT.v(8 * TC + t0, [[TC, 4], [1, 128]], sl=(8, 12)),
                    in0=PSB(bkT, 0, [[128, 4], [1, 128]]), in1=gsT.v(t0, [[TC, 4], [1, 128]]), op=ALU.mult)

            interleave((hg_tile(j) for j in range(8)), width=2)
            if g == NG - 1:
                S.dma('sp', o_hs_p[l].rearrange("(c p) e -> p c e", p=128), Shg.v(l * 256, [[64, 4], [1, 64]]),
                      is_output=True)

            for nh in range(2):
                wa = wnext(); wb_ = wnext()
                for m in range(4):
                    mo = nh * 4 + m

                    def cons_res(si, a, n, bank, mo=mo):
                        S.I('dve', 'tensor_tensor', out=CH(hT, mo, a, a + n), in0=CH(hT, mo, a, a + n),
                            in1=PS(bank, 0, [[1, n]]), op=ALU.add)
                    bks = dense(wa, 8, m * 128, 128, ymT, None, kc0=0, first=True, last=False)
                    dense(wb_, 4, m * 128, 128, ymT, cons_res, kc0=8, first=False, last=True, banks=bks)

            norm(cb + C_NFFN, uT)
            for G in range(2):
                for q in range(4):
                    wu = wnext()
                    for m in range(4):
                        zc = q * 4 + m

                        def cons_z(si, a, n, bank, zc=zc):
                            t = big(8, 10)
                            S.I('act', 'activation', out=t.v(a, [[1, n]]), in_=PS(bank, 0, [[1, n]]), func=AF.Relu)
                            S.I('pool', 'tensor_tensor', out=CH(zT, zc, a, a + n), in0=t.v(a, [[1, n]]),
                                in1=t.v(a, [[1, n]]), op=ALU.mult)
                        dense(wu, 8, m * 128, 128, uT, cons_z)
                for nh in range(2):
                    wa = wnext(); wb_ = wnext()
                    for m in range(4):
                        mo = nh * 4 + m

                        def cons_res(si, a, n, bank, mo=mo):
                            S.I('dve', 'tensor_tensor', out=CH(hT, mo, a, a + n), in0=CH(hT, mo, a, a + n),
                                in1=PS(bank, 0, [[1, n]]), op=ALU.add)
                        bks = dense(wa, 8, m * 128, 128, zT, None, kc0=0, first=True, last=False)
                        dense(wb_, 8, m * 128, 128, zT, cons_res, kc0=8, first=False, last=True, banks=bks)

            norm(cb + C_NPLE, uT)
            gts = []
            for nh in range(2):
                wg_ = wnext()
                for m in range(4):
                    gt = bigs[nh * 4 + m]
                    gts.append(gt)

                    def cons_g(si, a, n, bank, gt=gt):
                        S.I('act', 'activation', out=gt.v(a, [[1, n]]), in_=PS(bank, 0, [[1, n]]), func=AF.Sigmoid)
                    dense(wg_, 8, m * 128, 128, uT, cons_g)
            for nh in range(2):
                wp_ = wnext()
                for m in range(4):
                    mo = nh * 4 + m
                    gt = gts[mo]

                    def cons_p(si, a, n, bank, mo=mo, gt=gt):
                        S.I('dve', 'tensor_tensor', out=gt.v(a, [[1, n]]), in0=gt.v(a, [[1, n]]),
                            in1=PS(bank, 0, [[1, n]]), op=ALU.mult)
                        S.I('dve', 'tensor_tensor', out=CH(hT, mo, a, a + n), in0=CH(hT, mo, a, a + n),
                            in1=gt.v(a, [[1, n]]), op=ALU.add)
                    dense(wp_, 2, m * 128, 128, pT, cons_p)

        norm(C_NFIN, hT, in_place=True)
        for i in range(8):
            yt = bigs[i] if i != 8 else xin[1]
            for hf in range(2):
                bk = pbank()
                for c in range(4):
                    S.I('pe', 'transpose', out=PS(bk, c * 128, [[1, 128]]),
                        in_=CH(hT, hf * 4 + c, i * 128, (i + 1) * 128), identity=ident)
                if hf == 0:
                    S.I('act', 'activation', out=yt.v(0, [[1, 512]]), in_=PS(bk), func=AF.Copy)
                else:
                    S.I('dve', 'tensor_copy', out=yt.v(512, [[1, 512]]), in_=PS(bk))
            S.dma('sp', y_p[t_base + i * 128: t_base + (i + 1) * 128, :], yt.v(0, [[1, 1024]]), is_output=True)
        yt = xin[0]
        for hf in range(2):
            bk = pbank()
            for c in range(4):
                S.I('pe', 'transpose', out=PS(bk, c * 128, [[1, 128]], n=NSG),
                    in_=CH(hT, hf * 4 + c, TPG, TC), identity=ident)
            S.I('act', 'activation', out=yt.v(hf * 512, [[1, 512]], n=NSG), in_=PS(bk, 0, [[1, 512]], n=NSG),
                func=AF.Copy)
        S.dma('sp', y_s[s_base:s_base + NSG, :], yt.v(0, [[1, 1024]], n=NSG), is_output=True)

    S.finish()
    S.emit(stack)
    stack.close()
    return nc


def make_consts():
    c = np.zeros((128, NCONST), np.float32)
    idx = np.arange(128)
    c[:, K_ID:K_ID + 128] = np.eye(128, dtype=np.float32)
    c[:, K_ULE:K_ULE + 128] = (idx[:, None] <= idx[None, :]).astype(np.float32)
    c[:, K_ONES:K_ONES + 128] = 1.0
    c[:, K_NEGS:K_NEGS + 128] = (idx[:, None] > idx[None, :]).astype(np.float32)
    c[:, K_MASK2:K_MASK2 + 128] = ((idx[:, None] // 64 == idx[None, :] // 64) & (idx[:, None] <= idx[None, :])).astype(np.float32)
    c[:, K_I2:K_I2 + 64] = (idx[:, None] % 64 == np.arange(64)[None, :]).astype(np.float32)
    c[:, K_BLK:K_BLK + 128] = (idx[:, None] // 64 == idx[None, :] // 64).astype(np.float32)
    c[:, K_RMASK:K_RMASK + 1024] = (np.arange(1024) % 64 != 0).astype(np.float32)[None, :]
    c[:8, K_E8:K_E8 + 512] = (np.arange(8)[:, None] == (np.arange(512) // 64)[None, :]).astype(np.float32)
    return c


_NC_CACHE = {}


def kernel(**inp):
    f = lambda a: np.ascontiguousarray(np.asarray(a, dtype=np.float32))
    x_prompt = f(inp['x_prompt']); x_sample = f(inp['x_sample'])
    cols = np.zeros((128, NCOLS), np.float32)
    rows = np.zeros((128, NROWS), np.float32)

    def colv(v):
        v = f(v)
        return v.reshape(-1, 128).T

    for l in range(DEPTH):
        b = l * LC
        cols[:, b + C_NMIX:b + C_NMIX + 8] = colv(inp['norm_mix'][l])
        cols[:, b + C_NFFN:b + C_NFFN + 8] = colv(inp['norm_ffn'][l])
        cols[:, b + C_NPLE:b + C_NPLE + 8] = colv(inp['norm_ple'][l])
        caw = f(inp['conv_a_w'][l])
        for c in range(4):
            for k in range(4):
                cols[:, b + C_CAW + c * 4 + k] = caw[k, c * 128:(c + 1) * 128]
        cols[:, b + C_CAB:b + C_CAB + 4] = colv(inp['conv_a_b'][l])
        cols[:, b + C_RBA:b + C_RBA + 4] = colv(inp['rg_ba'][l])
        cols[:, b + C_RBX:b + C_RBX + 4] = colv(inp['rg_bx'][l])
        cols[:, b + C_LAM:b + C_LAM + 4] = colv(inp['rg_lambda'][l])
        cbw = f(inp['conv_b_w'][l])
        for c in range(6):
            for k in range(4):
                cols[:, b + C_CBW + c * 4 + k] = cbw[k, c * 128:(c + 1) * 128]
        cols[:, b + C_CBB:b + C_CBB + 6] = colv(inp['conv_b_b'][l])
        cols[:8, b + C_DTB] = f(inp['ssd_dt_bias'][l])
        cols[:, b + C_DEXP:b + C_DEXP + 4] = colv(np.repeat(f(inp['ssd_d'][l]), 64))
        cols[:, b + C_ALEXP:b + C_ALEXP + 4] = colv(np.repeat(f(inp['ssd_a_log'][l]), 64))
        cols[:, b + C_SNORMC:b + C_SNORMC + 4] = colv(inp['ssd_norm'][l])
        cols[:, b + C_HNORMC] = np.tile(f(inp['hg_norm'][l]), 2)
        r = l * LR
        rows[:, r + R_SNORM:r + R_SNORM + 512] = f(inp['ssd_norm'][l])[None, :]
        rows[:, r + R_HNORM:r + R_HNORM + 64] = f(inp['hg_norm'][l])[None, :]
        rows[:, r + R_SD:r + R_SD + 8] = f(inp['ssd_d'][l])[None, :]
        rows[:, r + R_ALOG:r + R_ALOG + 8] = f(inp['ssd_a_log'][l])[None, :]
    cols[:, C_NFIN:C_NFIN + 8] = colv(inp['norm_final'])
    cols[:, C_HLB0:C_HLB0 + 4] = colv(inp['hg_lower_bounds'][0])
    cols[:, C_HLB1:C_HLB1 + 4] = colv(inp['hg_lower_bounds'][1])
    rgw = np.zeros((DEPTH, 2, 4, 128, 128), np.float32)
    for l in range(DEPTH):
        for wi, nm in enumerate(('rg_wa', 'rg_wx')):
            w = f(inp[nm][l])
            for h in range(8):
                c, hh = h // 2, h % 2
                rgw[l, wi, c, hh * 64:(hh + 1) * 64, hh * 64:(hh + 1) * 64] = w[h]
    rgw = rgw.reshape(16, 128, 128)
    cst = make_consts()

    if 'nc' not in _NC_CACHE:
        _NC_CACHE['nc'] = build_program()
    nc = _NC_CACHE['nc']

    shared = dict(w_in=f(inp['w_in']), w_out=f(inp['w_out']), w_up=f(inp['w_up']), w_down=f(inp['w_down']),
                  w_pg=f(inp['w_ple_gate']), w_pp=f(inp['w_ple_proj']), rgw=rgw, cols=cols, rows=rows, cst=cst)
    in_maps = []
    for c in range(NCORES):
        sl = slice(c * NSAMP, (c + 1) * NSAMP)
        m = dict(shared)
        m['xp'] = f(x_prompt[c]); m['xs'] = f(x_sample[sl, 0])
        m['pp'] = f(inp['p_prompt'][:, c]); m['psm'] = f(inp['p_sample'][:, sl, 0])
        m['st_rc'] = f(inp['state_rg_conv'][:, sl]); m['st_rh'] = f(inp['state_rg_h'][:, sl])
        m['st_sc'] = f(inp['state_ssd_conv'][:, sl])
        m['st_ss'] = f(inp['state_ssd'][:, sl]).reshape(DEPTH, NSAMP, 512, 128)
        m['st_hs'] = f(inp['state_hgrn'][:, sl]).reshape(DEPTH, NSAMP, 512, 64)
        in_maps.append(m)
    res = run_bass_kernel_spmd(nc, in_maps, core_ids=list(range(NCORES)))
    R = res.results

    def cat(name, axis):
        return np.concatenate([np.asarray(r[name]) for r in R], axis=axis)

    def stk(name):
        return np.stack([np.asarray(r[name]) for r in R], axis=1)

    y_prompt = np.stack([np.asarray(r['y_p']) for r in R], 0)
    y_sample = cat('y_s', 0).reshape(NCORES * NSAMP, 1, D)
    rc_p = stk('o_rc_p'); rh_p = stk('o_rh_p'); sc_p = stk('o_sc_p')
    ss_p = stk('o_ss_p').reshape(DEPTH, NCORES, 8, 64, 128)
    hs_p = stk('o_hs_p').reshape(DEPTH, NCORES, 8, 64, 64)
    rc_s = cat('o_rc_s', 1); rh_s = cat('o_rh_s', 1); sc_s = cat('o_sc_s', 1)
    ss_s = cat('o_ss_s', 1).reshape(DEPTH, NCORES * NSAMP, 8, 64, 128)
    hs_s = cat('o_hs_s', 1).reshape(DEPTH, NCORES * NSAMP, 8, 64, 64)
    outs = (y_prompt, y_sample, rc_p, rh_p, sc_p, ss_p, hs_p, rc_s, rh_s, sc_s, ss_s, hs_s)
    return tuple(np.ascontiguousarray(o, dtype=np.float32) for o in outs)
```
